# Optimizing a Trainium2 kernel written in Bass

```python
import math
import jax
import jax.numpy as jnp
from jax import lax
import numpy as np

D_MODEL = 2048
BATCH = 4
SEQ = 4096
DEPTH = 2

D_MIX = D_MODEL
GROUP_W = D_MIX // 4
LRU_W = GROUP_W
LRU_BLOCKS = 8
LRU_BLOCK_W = LRU_W // LRU_BLOCKS
LRU_CONV = 4
LRU_C = 8.0
ATT_HEADS = 8
ATT_KV_HEADS = 2
ATT_GROUP = ATT_HEADS // ATT_KV_HEADS
ATT_HEAD_DIM = GROUP_W // ATT_HEADS
ROPE_AXIS = ATT_HEAD_DIM // 2
ROPE_THETA = 10000.0
Q_BLOCK = 128
GRID_W = 64
HY_W = GROUP_W
HY_CONV = 3
HY_ORDER = 2
HY_BANDS = 8
HY_EMB = 2 * HY_BANDS + 1
HY_FFN = 64
HY_TARGET = 1e-2
HY_FAST_DECAY = 0.3
HY_SLOW_DECAY = 1.5
ML_HEADS = 4
ML_HEAD_DIM = GROUP_W // ML_HEADS
ML_CHUNK = 128
D_FF = -(-8 * D_MODEL // 768) * 256
EPS = 1e-6
IN_SIZES = (LRU_W, LRU_W, ATT_HEADS * ATT_HEAD_DIM, ATT_KV_HEADS * ATT_HEAD_DIM, ATT_KV_HEADS * ATT_HEAD_DIM, 3 * HY_W, GROUP_W, GROUP_W, GROUP_W, GROUP_W, 4 * ML_HEADS)
D_IN = sum(IN_SIZES)

kernel_name = 'hybrid_parallel_group_encoder'


def rmsnorm(x, g):
    xf = x.astype(jnp.float32)
    y = xf * lax.rsqrt(jnp.mean(xf * xf, axis=-1, keepdims=True) + EPS)
    return (y * g.astype(jnp.float32)).astype(x.dtype)


def dwconv_centred(x, w, b):
    K = w.shape[0]
    S = x.shape[1]
    left = K // 2
    xp = jnp.pad(x, ((0, 0), (left, K - 1 - left), (0, 0)))
    out = b
    for j in range(K):
        out = out + xp[:, j:j + S, :] * w[j]
    return out


def _linrec_combine(left, right):
    a1, u1 = left
    a2, u2 = right
    return a1 * a2, a2 * u1 + u2


def rglru_direction(xc, wa, ba, wx, bx, lam, reverse):
    Bn, S, W = xc.shape
    xb = xc.reshape(Bn, S, LRU_BLOCKS, LRU_BLOCK_W)
    r = jax.nn.sigmoid((jnp.einsum('bsnk,nkj->bsnj', xb, wa).reshape(Bn, S, W) + ba).astype(jnp.float32))
    i = jax.nn.sigmoid((jnp.einsum('bsnk,nkj->bsnj', xb, wx).reshape(Bn, S, W) + bx).astype(jnp.float32))
    log_a = -LRU_C * jax.nn.softplus(-lam.astype(jnp.float32)) * r
    a = jnp.exp(log_a)
    u = jnp.sqrt(-jnp.expm1(2.0 * log_a)) * (i * xc.astype(jnp.float32))
    _, h = lax.associative_scan(_linrec_combine, (a, u), reverse=reverse, axis=1)
    return h


def mixer_rglru(xa, ga, conv_w, conv_b, wa, ba, wx, bx, lam):
    xc = dwconv_centred(xa, conv_w, conv_b)
    h = (rglru_direction(xc, wa[0], ba[0], wx[0], bx[0], lam[0], False)
         + rglru_direction(xc, wa[1], ba[1], wx[1], bx[1], lam[1], True))
    return jax.nn.gelu(ga.astype(jnp.float32)) * h


def axial_angles(S):
    rows = S // GRID_W
    row = jnp.repeat(jnp.arange(rows, dtype=jnp.float32), GRID_W)
    col = jnp.tile(jnp.arange(GRID_W, dtype=jnp.float32), rows)
    inv = ROPE_THETA ** (-jnp.arange(0, ROPE_AXIS, 2, dtype=jnp.float32) / ROPE_AXIS)
    return row[:, None] * inv, col[:, None] * inv


def rotate_pairs(x, ang):
    m = x.shape[-1] // 2
    xf = x.astype(jnp.float32)
    cos = jnp.cos(ang)[None, :, None, :]
    sin = jnp.sin(ang)[None, :, None, :]
    x1, x2 = xf[..., :m], xf[..., m:]
    return jnp.concatenate([x1 * cos - x2 * sin, x2 * cos + x1 * sin], axis=-1).astype(x.dtype)


def axial_rope(x, ang_row, ang_col):
    return jnp.concatenate([rotate_pairs(x[..., :ROPE_AXIS], ang_row), rotate_pairs(x[..., ROPE_AXIS:], ang_col)], axis=-1)


def mixer_attention(q, k, v, q_g, k_g):
    Bn, S, _ = q.shape
    q = rmsnorm(q.reshape(Bn, S, ATT_HEADS, ATT_HEAD_DIM), q_g)
    k = rmsnorm(k.reshape(Bn, S, ATT_KV_HEADS, ATT_HEAD_DIM), k_g)
    v = v.reshape(Bn, S, ATT_KV_HEADS, ATT_HEAD_DIM)
    ang_row, ang_col = axial_angles(S)
    q = axial_rope(q, ang_row, ang_col) * (ATT_HEAD_DIM ** -0.5)
    k = axial_rope(k, ang_row, ang_col)
    qh = q.reshape(Bn, S, ATT_KV_HEADS, ATT_GROUP, ATT_HEAD_DIM).transpose(0, 2, 3, 1, 4)
    kh = k.transpose(0, 2, 1, 3)
    vh = v.transpose(0, 2, 1, 3)
    nb = S // Q_BLOCK
    qb = jnp.moveaxis(qh.reshape(Bn, ATT_KV_HEADS, ATT_GROUP, nb, Q_BLOCK, ATT_HEAD_DIM), 3, 0)

    def attend(qi):
        s = jnp.einsum('bhgqd,bhkd->bhgqk', qi, kh).astype(jnp.float32)
        p = jax.nn.softmax(s, axis=-1).astype(vh.dtype)
        return jnp.einsum('bhgqk,bhkd->bhgqd', p, vh)

    o = lax.map(attend, qb)
    return o.transpose(1, 0, 4, 2, 3, 5).reshape(Bn, S, ATT_HEADS * ATT_HEAD_DIM)


def hyena_filter_spectra(L, w1, b1, w2, b2, w3, sin_freq, decay):
    f32 = jnp.float32
    pos = jnp.arange(L, dtype=f32)
    t = pos / max(L - 1, 1)
    bands = jnp.linspace(1e-4, HY_BANDS - 1, HY_BANDS, dtype=f32)
    ang = (2.0 * math.pi * pos / L)[:, None] * bands
    feat = jnp.concatenate([t[:, None], jnp.cos(ang), -jnp.sin(ang)], axis=-1)
    sf = sin_freq.astype(f32)
    h = jnp.sin(sf * (feat @ w1.astype(f32) + b1.astype(f32)))
    h = jnp.sin(sf * (h @ w2.astype(f32) + b2.astype(f32)))
    h = h @ w3.astype(f32)
    h = h * jnp.exp(-t[:, None] * jnp.abs(decay.astype(f32)))
    h = h.reshape(L, HY_ORDER, 2, HY_W)
    h = h * lax.rsqrt(jnp.sum(h * h, axis=(0, 2), keepdims=True) + EPS)
    fwd, bwd = h[:, :, 0], h[:, :, 1]
    two_sided = jnp.concatenate([fwd, jnp.zeros((1, HY_ORDER, HY_W), f32), bwd[:0:-1]], axis=0)
    return jnp.moveaxis(jnp.fft.rfft(two_sided, axis=0), 1, 0)


def long_conv(u, h_spec, skip):
    S = u.shape[1]
    y = jnp.fft.irfft(jnp.fft.rfft(u, n=2 * S, axis=1) * h_spec[None], n=2 * S, axis=1)[:, :S]
    return y + u * skip


def mixer_hyena(u3, conv_w, conv_b, w1, b1, w2, b2, w3, sin_freq, decay, skip):
    z = dwconv_centred(u3, conv_w, conv_b).astype(jnp.float32)
    v, x1, x2 = jnp.split(z, 3, axis=-1)
    h_spec = hyena_filter_spectra(u3.shape[1], w1, b1, w2, b2, w3, sin_freq, decay)
    sk = skip.astype(jnp.float32)
    z = x1 * long_conv(v, h_spec[0], sk[0])
    z = x2 * long_conv(z, h_spec[1], sk[1])
    return z


def mlstm_chunkwise(q, k, v, log_i, log_f):
    Bn, H, S, Dh = q.shape
    nc = S // ML_CHUNK
    q = q.reshape(Bn, H, nc, ML_CHUNK, Dh)
    k = k.reshape(Bn, H, nc, ML_CHUNK, Dh)
    v = v.reshape(Bn, H, nc, ML_CHUNK, Dh)
    li = log_i.reshape(Bn, H, nc, ML_CHUNK)
    b = jnp.cumsum(log_f.reshape(Bn, H, nc, ML_CHUNK), axis=-1)
    b_tot = b[..., -1]
    causal = jnp.tril(jnp.ones((ML_CHUNK, ML_CHUNK), dtype=bool))
    d_intra = jnp.where(causal, b[..., :, None] - b[..., None, :] + li[..., None, :], -jnp.inf)
    w_end = b_tot[..., None] - b + li
    m_loc = jnp.max(w_end, axis=-1)
    e_end = jnp.exp(w_end - m_loc[..., None])
    dC = jnp.einsum('bhnl,bhnlk,bhnlv->bhnkv', e_end, k, v)
    dn = jnp.einsum('bhnl,bhnlk->bhnk', e_end, k)

    def step(carry, inp):
        C, n, m = carry
        dC_c, dn_c, mloc_c, bt_c = inp
        m_new = jnp.maximum(bt_c + m, mloc_c)
        decay = jnp.exp(bt_c + m - m_new)
        gain = jnp.exp(mloc_c - m_new)
        C_new = decay[..., None, None] * C + gain[..., None, None] * dC_c
        n_new = decay[..., None] * n + gain[..., None] * dn_c
        return (C_new, n_new, m_new), (C, n, m)

    init = (jnp.zeros((Bn, H, Dh, Dh), jnp.float32), jnp.zeros((Bn, H, Dh), jnp.float32), jnp.zeros((Bn, H), jnp.float32))
    xs = (jnp.moveaxis(dC, 2, 0), jnp.moveaxis(dn, 2, 0), jnp.moveaxis(m_loc, 2, 0), jnp.moveaxis(b_tot, 2, 0))
    _, (C_prev, n_prev, m_prev) = lax.scan(step, init, xs)
    C_prev = jnp.moveaxis(C_prev, 0, 2)
    n_prev = jnp.moveaxis(n_prev, 0, 2)
    m_prev = jnp.moveaxis(m_prev, 0, 2)
    m_inter = b + m_prev[..., None]
    m_t = jnp.maximum(m_inter, jnp.max(d_intra, axis=-1))
    e_inter = jnp.exp(m_inter - m_t)
    s = jnp.einsum('bhnlk,bhnsk->bhnls', q, k) * jnp.exp(d_intra - m_t[..., None])
    num = jnp.einsum('bhnls,bhnsv->bhnlv', s, v) + e_inter[..., None] * jnp.einsum('bhnlk,bhnkv->bhnlv', q, C_prev)
    den = jnp.sum(s, axis=-1) + e_inter * jnp.einsum('bhnlk,bhnk->bhnl', q, n_prev)
    h = num / jnp.maximum(jnp.abs(den), jnp.exp(-m_t))[..., None]
    return h.reshape(Bn, H, S, Dh)


def mixer_mlstm(q, k, v, o, gates, norm_g):
    Bn, S, _ = q.shape

    def heads(a):
        return a.astype(jnp.float32).reshape(Bn, S, ML_HEADS, ML_HEAD_DIM).transpose(0, 2, 1, 3)

    qh = heads(q) * (ML_HEAD_DIM ** -0.5)
    kh = heads(k)
    vh = heads(v)
    g = gates.astype(jnp.float32).transpose(0, 2, 1)
    i_f, f_f, i_b, f_b = jnp.split(g, 4, axis=1)
    h_f = mlstm_chunkwise(qh, kh, vh, i_f, jax.nn.log_sigmoid(f_f))
    fl = lambda a: jnp.flip(a, axis=2)
    h_b = fl(mlstm_chunkwise(fl(qh), fl(kh), fl(vh), fl(i_b), fl(jax.nn.log_sigmoid(f_b))))
    h = (h_f + h_b).transpose(0, 2, 1, 3)
    h = rmsnorm(h, norm_g.reshape(ML_HEADS, ML_HEAD_DIM))
    return jax.nn.sigmoid(o.astype(jnp.float32)) * h.reshape(Bn, S, GROUP_W)


def swiglu(h, w1, w3, w2):
    return (jax.nn.silu(h @ w1) * (h @ w3)) @ w2


def setup_inputs(seed: int = 0) -> dict:
    key = jax.random.key(seed)
    ks = jax.random.split(key, 40)
    f32 = jnp.float32

    def nrm(i, shape, s):
        return s * jax.random.normal(ks[i], shape, f32)

    x = nrm(0, (BATCH, SEQ, D_MODEL), 1.0)
    c = nrm(1, (BATCH, D_MODEL), 1.0)
    w_in = nrm(2, (DEPTH, D_MODEL, D_IN), D_MODEL ** -0.5)
    g0 = D_IN - 4 * ML_HEADS
    f_bias = jnp.linspace(3.0, 6.0, ML_HEADS, dtype=f32)
    b_in = nrm(3, (DEPTH, D_IN), 0.01)
    b_in = b_in.at[:, g0 + ML_HEADS:g0 + 2 * ML_HEADS].add(f_bias).at[:, g0 + 3 * ML_HEADS:].add(f_bias)
    w_out = nrm(4, (DEPTH, D_MIX, D_MODEL), D_MIX ** -0.5)
    norm_mix_g = 1.0 + nrm(5, (DEPTH, D_MODEL), 0.05)
    norm_ffn_g = 1.0 + nrm(6, (DEPTH, D_MODEL), 0.05)
    ada_w = nrm(7, (DEPTH, D_MODEL, 6 * D_MODEL), D_MODEL ** -0.5)
    ada_b = nrm(8, (DEPTH, 6 * D_MODEL), 0.01)
    lru_conv_w = nrm(9, (DEPTH, LRU_CONV, LRU_W), LRU_CONV ** -0.5)
    lru_conv_b = nrm(10, (DEPTH, LRU_W), 0.01)
    lru_wa = nrm(11, (DEPTH, 2, LRU_BLOCKS, LRU_BLOCK_W, LRU_BLOCK_W), LRU_BLOCK_W ** -0.5)
    lru_ba = nrm(12, (DEPTH, 2, LRU_W), 0.01)
    lru_wx = nrm(13, (DEPTH, 2, LRU_BLOCKS, LRU_BLOCK_W, LRU_BLOCK_W), LRU_BLOCK_W ** -0.5)
    lru_bx = nrm(14, (DEPTH, 2, LRU_W), 0.01)
    a_c = jax.random.uniform(ks[15], (DEPTH, 2, LRU_W), f32, minval=0.9, maxval=0.999)
    a0 = a_c ** (1.0 / LRU_C)
    lru_lambda = jnp.log(a0) - jnp.log1p(-a0)
    att_q_norm_g = 1.0 + nrm(16, (DEPTH, ATT_HEAD_DIM), 0.05)
    att_k_norm_g = 1.0 + nrm(17, (DEPTH, ATT_HEAD_DIM), 0.05)
    hy_conv_w = nrm(18, (DEPTH, HY_CONV, 3 * HY_W), HY_CONV ** -0.5)
    hy_conv_b = nrm(19, (DEPTH, 3 * HY_W), 0.01)
    hy_w1 = nrm(20, (DEPTH, HY_EMB, HY_FFN), HY_EMB ** -0.5)
    hy_b1 = nrm(21, (DEPTH, HY_FFN), 0.1)
    hy_w2 = nrm(22, (DEPTH, HY_FFN, HY_FFN), HY_FFN ** -0.5)
    hy_b2 = nrm(23, (DEPTH, HY_FFN), 0.1)
    hy_w3 = nrm(24, (DEPTH, HY_FFN, HY_ORDER * 2 * HY_W), HY_FFN ** -0.5)
    hy_sin_freq = 1.0 + nrm(25, (DEPTH, HY_FFN), 0.05)
    base_decay = jnp.abs(jnp.linspace(math.log(HY_TARGET) / HY_SLOW_DECAY, math.log(HY_TARGET) / HY_FAST_DECAY, HY_W, dtype=f32))
    hy_decay = jnp.tile(base_decay, HY_ORDER * 2)[None, :] * (1.0 + nrm(26, (DEPTH, HY_ORDER * 2 * HY_W), 0.05))
    hy_skip = nrm(27, (DEPTH, HY_ORDER, HY_W), 1.0)
    ml_norm_g = 1.0 + nrm(28, (DEPTH, GROUP_W), 0.05)
    ffn_w1 = nrm(29, (DEPTH, D_MODEL, D_FF), D_MODEL ** -0.5)
    ffn_w3 = nrm(30, (DEPTH, D_MODEL, D_FF), D_MODEL ** -0.5)
    ffn_w2 = nrm(31, (DEPTH, D_FF, D_MODEL), D_FF ** -0.5)
    final_g = 1.0 + nrm(32, (D_MODEL,), 0.05)
    return {'x': x, 'c': c, 'w_in': w_in, 'b_in': b_in, 'w_out': w_out,
            'norm_mix_g': norm_mix_g, 'norm_ffn_g': norm_ffn_g, 'ada_w': ada_w, 'ada_b': ada_b,
            'lru_conv_w': lru_conv_w, 'lru_conv_b': lru_conv_b, 'lru_wa': lru_wa, 'lru_ba': lru_ba,
            'lru_wx': lru_wx, 'lru_bx': lru_bx, 'lru_lambda': lru_lambda,
            'att_q_norm_g': att_q_norm_g, 'att_k_norm_g': att_k_norm_g,
            'hy_conv_w': hy_conv_w, 'hy_conv_b': hy_conv_b, 'hy_w1': hy_w1, 'hy_b1': hy_b1,
            'hy_w2': hy_w2, 'hy_b2': hy_b2, 'hy_w3': hy_w3, 'hy_sin_freq': hy_sin_freq,
            'hy_decay': hy_decay, 'hy_skip': hy_skip, 'ml_norm_g': ml_norm_g,
            'ffn_w1': ffn_w1, 'ffn_w3': ffn_w3, 'ffn_w2': ffn_w2, 'final_g': final_g}


def reference(x, c, w_in, b_in, w_out, norm_mix_g, norm_ffn_g, ada_w, ada_b,
              lru_conv_w, lru_conv_b, lru_wa, lru_ba, lru_wx, lru_bx, lru_lambda,
              att_q_norm_g, att_k_norm_g, hy_conv_w, hy_conv_b, hy_w1, hy_b1, hy_w2, hy_b2,
              hy_w3, hy_sin_freq, hy_decay, hy_skip, ml_norm_g, ffn_w1, ffn_w3, ffn_w2, final_g):
    split_points = np.cumsum(IN_SIZES)[:-1].tolist()
    c_act = jax.nn.silu(c)
    for l in range(DEPTH):
        mod = c_act @ ada_w[l] + ada_b[l]
        sh1, sc1, g1, sh2, sc2, g2 = [m[:, None, :] for m in jnp.split(mod, 6, axis=-1)]
        h = rmsnorm(x, norm_mix_g[l]) * (1.0 + sc1) + sh1
        proj = h @ w_in[l] + b_in[l]
        a_x, a_g, b_q, b_k, b_v, c_u, d_q, d_k, d_v, d_o, d_gates = jnp.split(proj, split_points, axis=-1)
        y_a = mixer_rglru(a_x, a_g, lru_conv_w[l], lru_conv_b[l], lru_wa[l], lru_ba[l], lru_wx[l], lru_bx[l], lru_lambda[l])
        y_b = mixer_attention(b_q, b_k, b_v, att_q_norm_g[l], att_k_norm_g[l])
        y_c = mixer_hyena(c_u, hy_conv_w[l], hy_conv_b[l], hy_w1[l], hy_b1[l], hy_w2[l], hy_b2[l], hy_w3[l], hy_sin_freq[l], hy_decay[l], hy_skip[l])
        y_d = mixer_mlstm(d_q, d_k, d_v, d_o, d_gates, ml_norm_g[l])
        y = jnp.concatenate([y_a.astype(x.dtype), y_b.astype(x.dtype), y_c.astype(x.dtype), y_d.astype(x.dtype)], axis=-1)
        x = x + g1 * (y @ w_out[l])
        h = rmsnorm(x, norm_ffn_g[l]) * (1.0 + sc2) + sh2
        x = x + g2 * swiglu(h, ffn_w1[l], ffn_w3[l], ffn_w2[l])
    return rmsnorm(x, final_g)
```

```python
import math
import ml_dtypes
from contextlib import ExitStack
from concourse.bass_utils import run_bass_kernel_spmd
import numpy as np
import concourse.bass as bass
import concourse.mybir as mybir

F32 = mybir.dt.float32
BF16 = mybir.dt.bfloat16
AF = mybir.ActivationFunctionType
ALU = mybir.AluOpType
AX = mybir.AxisListType


NPOOL = 90


class Buf:
    __slots__ = ("name", "lw", "rd", "sem", "cnt")

    def __init__(self, name=""):
        self.name = name
        self.lw = None
        self.rd = {}
        self.sem = None
        self.cnt = 0


class KB:
    def __init__(self, nc):
        self.nc = nc
        self.eng = {"pe": nc.tensor, "dve": nc.vector, "act": nc.scalar,
                    "pool": nc.gpsimd, "sp": nc.sync}
        self.sems = {}
        self.cnt = {}
        for k in self.eng:
            self.sems[k] = nc.alloc_semaphore("c_" + k)
            self.cnt[k] = 0
        self.waited = {k: {} for k in self.eng}
        self.ndma = 0
        self.nbuf = 0
        self.dmacnt = {}
        self.bufs = []
        self.pool = [nc.alloc_semaphore("d%d" % i) for i in range(NPOOL)]
        for h in list(self.sems.values()) + self.pool:
            nc.gpsimd.sem_clear(h)
        nc.all_engine_barrier()

    def buf(self, name=""):
        self.nbuf += 1
        b = Buf(name or ("b%d" % self.nbuf))
        self.bufs.append(b)
        return b

    def _wait(self, e, reads, writes):
        need = {}
        for b in reads:
            if b.lw is not None:
                k, v = b.lw
                if need.get(k, 0) < v:
                    need[k] = v
        for b in writes:
            if b.lw is not None:
                k, v = b.lw
                if need.get(k, 0) < v:
                    need[k] = v
            for k, v in b.rd.items():
                if need.get(k, 0) < v:
                    need[k] = v
        w = self.waited[e]
        for k, v in need.items():
            if k == e and (e == "pe" or v > self.cnt[e]):
                continue
            if w.get(k, 0) < v:
                self.eng[e].wait_ge(self.sems[k], v)
                w[k] = v

    def op(self, e, fn, reads=(), writes=(), inc=True):
        self._wait(e, reads, writes)
        ins = fn(self.eng[e])
        if inc:
            self.cnt[e] += 1
            ins.then_inc(self.sems[e], 1)
            tok = (e, self.cnt[e])
        else:
            tok = (e, self.cnt[e] + 1)
        for b in writes:
            b.lw = tok
            b.rd = {}
        for b in reads:
            if b.rd.get(tok[0], 0) < tok[1]:
                b.rd[tok[0]] = tok[1]
        return ins

    def dma(self, q, out, in_, reads, writes, **kw):
        self._wait(q, reads, writes)
        wb = writes[0]
        if wb.sem is None:
            key = "d%d" % self.ndma
            self.sems[key] = self.pool[self.ndma]
            self.ndma += 1
            wb.sem = key
        self.dmacnt[wb.sem] = self.dmacnt.get(wb.sem, 0) + 16
        self.eng[q].dma_start(out=out, in_=in_, **kw).then_inc(self.sems[wb.sem], 16)
        tok = (wb.sem, self.dmacnt[wb.sem])
        for b in writes:
            b.lw = tok
            b.rd = {}
        for b in reads:
            if b.rd.get(tok[0], 0) < tok[1]:
                b.rd[tok[0]] = tok[1]

    def barrier(self):
        cur = dict(self.cnt)
        cur.update(self.dmacnt)
        for e in self.eng:
            w = self.waited[e]
            for k, v in cur.items():
                if v <= 0 or (k == e and e == "pe"):
                    continue
                if w.get(k, 0) < v:
                    self.eng[e].wait_ge(self.sems[k], v)
                    w[k] = v
        for b in self.bufs:
            b.lw = None
            b.rd = {}
            b.sem = None
        self.ndma = 0

    def finish(self, bufs):
        self._wait("sp", bufs, [])


S = 4096
D = 2048
NT = 32
DIN = 5392
DFF = 5632
EPS = 1e-6
NF = 33
NDFT = 8192
PI = math.pi

FM = []
for i in range(4): FM.append([(0 + 128 * i, 128)])
for i in range(4): FM.append([(512 + 128 * i, 128)])
for i in range(4): FM.append([(1024 + 128 * i, 128)])
FM.append([(1536, 64), (1536, 64)])
FM.append([(1600, 64), (1600, 64)])
for i in range(12): FM.append([(1792 + 128 * i, 128)])
for i in range(4): FM.append([(3328 + 128 * i, 128)])
for i in range(4): FM.append([(3840 + 128 * i, 128)])
NFM = len(FM)
TM = [("vtm", 1664, 128), ("dk", 3840, 512), ("dv", 4352, 512), ("do", 4864, 512), ("g", 5376, 16)]
TMOFF = {}
_o = 0
for nm, lo, n in TM:
    TMOFF[nm] = _o
    _o += n
NTM = _o


CO = {}
_o = 0
for nm, n in (("lru_cw", 16), ("lru_cb", 4), ("lru_ba", 8), ("lru_bx", 8), ("lru_lam", 8), ("att_qg", 1), ("att_kg", 1),
              ("hy_cw", 36), ("hy_cb", 12), ("hy_b1", 1), ("hy_b2", 1), ("hy_sf", 1)):
    CO[nm] = _o
    _o += n
NCOL = _o


class Ctx:
    pass


_UC = [0]


def U(name):
    _UC[0] += 1
    return "%s_%d" % (name, _UC[0])


def ACT(kb, out, in_, func, rd, wr, **kw):
    return kb.op("act", lambda e: e.activation(out=out, in_=in_, func=func, **kw), rd, wr)


def MM(kb, out, lhsT, rhs, start, stop, rd, wr):
    return kb.op("pe", lambda e: e.matmul(out, lhsT=lhsT, rhs=rhs, start=start, stop=stop), rd, wr, inc=stop)


def TR(kb, out, in_, ident, rd, wr):
    return kb.op("pe", lambda e: e.transpose(out=out, in_=in_, identity=ident), rd, wr)


def TT(kb, eng, out, in0, in1, op, rd, wr):
    return kb.op(eng, lambda e: e.tensor_tensor(out=out, in0=in0, in1=in1, op=op), rd, wr)


def TS(kb, eng, out, in0, s1, s2, op0, op1, rd, wr):
    if op1 is None:
        return kb.op(eng, lambda e: e.tensor_scalar(out=out, in0=in0, scalar1=s1, scalar2=None, op0=op0), rd, wr)
    return kb.op(eng, lambda e: e.tensor_scalar(out=out, in0=in0, scalar1=s1, scalar2=s2, op0=op0, op1=op1), rd, wr)


def STT(kb, out, in0, scalar, in1, op0, op1, rd, wr):
    return kb.op("dve", lambda e: e.scalar_tensor_tensor(out=out, in0=in0, scalar=scalar, in1=in1, op0=op0, op1=op1), rd, wr)


def CP(kb, eng, out, in_, rd, wr):
    return kb.op(eng, lambda e: e.tensor_copy(out=out, in_=in_), rd, wr)


def RECIP(kb, out, in_, rd, wr):
    return kb.op("dve", lambda e: e.reciprocal(out=out, in_=in_), rd, wr)


def MEMSET(kb, eng, ap, val, wr):
    return kb.op(eng, lambda e: e.memset(ap, val), [], wr)


def phase_ada(cx, l):
    nc, kb = cx.nc, cx.kb
    adaw = cx.I["ada_w"][l].rearrange("(kc p) n -> p kc n", p=128)
    with ExitStack() as es:
        aw0 = es.enter_context(nc.sbuf_tensor(U("adaw0"), [128, 16, 512], F32))
        aw1 = es.enter_context(nc.sbuf_tensor(U("adaw1"), [128, 16, 512], F32))
        adab = es.enter_context(nc.sbuf_tensor(U("adab"), [128, 96], F32))
        gbias = es.enter_context(nc.sbuf_tensor(U("gbias"), [128, 512], F32))
        crep = es.enter_context(nc.sbuf_tensor(U("crep"), [128, 16, 128], F32))
        cx.crep = crep
        cx.bcrep = kb.buf()
        for kc in range(16):
            TS(kb, "dve", crep[:, kc, :], cx.ones[:], cx.cact[:, kc:kc + 1], None, ALU.mult, None, [cx.bconst, cx.bcact], [cx.bcrep])
        aws = [aw0, aw1]
        baw = [kb.buf(), kb.buf()]
        bab, bgb = kb.buf(), kb.buf()
        kb.dma("sp", adab[:], cx.I["ada_b_col"][l], [], [bab])
        psm, bpsm = cx.ps[7], cx.pb[7]
        for j in range(24):
            a, ba = aws[j % 2], baw[j % 2]
            kb.dma("sp", a[:], adaw[:, :, j * 512:(j + 1) * 512], [], [ba])
            which = {2: 0, 5: 1}.get(j // 4)
            if which is not None:
                p, bp = cx.ps[j % 2], cx.pb[j % 2]
                for kc in range(16):
                    MM(kb, p[:, :], cx.crep[:, kc, :], a[:, kc, :], kc == 0, kc == 15, [cx.bcrep, ba], [bp])
                kb.dma("sp", gbias[:], cx.I["ada_b_rep"][l][:, j * 512:(j + 1) * 512], [], [bgb])
                TT(kb, "dve", cx.gb[l][which][:, (j % 4) * 512:(j % 4 + 1) * 512], p[:, :], gbias[:], ALU.add, [bp, bgb], [cx.bgb[l][which]])
            else:
                for cc in range(4):
                    jj = 4 * j + cc
                    for kc in range(16):
                        MM(kb, psm[:, jj:jj + 1], a[:, kc, cc * 128:(cc + 1) * 128], cx.cact[:, kc:kc + 1], kc == 0, kc == 15, [ba, cx.bcact], [bpsm])
        for lo, hi in ((0, 32), (48, 80)):
            TT(kb, "dve", cx.modT[l][:, lo:hi], psm[:, lo:hi], adab[:, lo:hi], ALU.add, [bpsm, bab], [cx.bmod[l]])
        for s, (sc_lo, sh_lo, gcol) in enumerate(((16, 0, cx.gmix[l]), (64, 48, cx.gffn[l]))):
            TS(kb, "dve", cx.AB[l][:, 32 * s:32 * s + 16], cx.modT[l][:, sc_lo:sc_lo + 16], 1.0, None, ALU.add, None, [cx.bmod[l]], [cx.bAB[l]])
            TT(kb, "dve", cx.AB[l][:, 32 * s:32 * s + 16], cx.AB[l][:, 32 * s:32 * s + 16], gcol, ALU.mult, [cx.bAB[l], cx.bconst], [cx.bAB[l]])
            CP(kb, "dve", cx.AB[l][:, 32 * s + 16:32 * s + 32], cx.modT[l][:, sh_lo:sh_lo + 16], [cx.bmod[l]], [cx.bAB[l]])


def phase_norm(cx, xsrc, bxsrc, Acol, Bcol, bAB, hT_d, bhT):
    nc, kb = cx.nc, cx.kb
    hview = hT_d.rearrange("c p t -> p c t")
    with ExitStack() as es:
        x0 = es.enter_context(nc.sbuf_tensor(U("nx0"), [128, D], F32))
        x1 = es.enter_context(nc.sbuf_tensor(U("nx1"), [128, D], F32))
        xh0 = es.enter_context(nc.sbuf_tensor(U("nxh0"), [128, D], F32))
        xh1 = es.enter_context(nc.sbuf_tensor(U("nxh1"), [128, D], F32))
        st0 = es.enter_context(nc.sbuf_tensor(U("nst0"), [128, 16, 512], BF16))
        st1 = es.enter_context(nc.sbuf_tensor(U("nst1"), [128, 16, 512], BF16))
        ss = es.enter_context(nc.sbuf_tensor(U("nss"), [128, 8], F32))
        xs, bxs = [x0, x1], [kb.buf(), kb.buf()]
        xhs, bxh = [xh0, xh1], [kb.buf(), kb.buf()]
        sts, bst = [st0, st1], [kb.buf(), kb.buf()]
        bss = [kb.buf(), kb.buf()]
        for tb in range(8):
            st, bs = sts[tb % 2], bst[tb % 2]
            for tt in range(4):
                ti = tb * 4 + tt
                u = ti % 2
                kb.dma("sp", xs[u][:], xsrc[ti * 128:(ti + 1) * 128, :], [bxsrc], [bxs[u]])
                c0 = 4 * u
                ACT(kb, xhs[u][:], xs[u][:], AF.Square, [bxs[u]], [bxh[u]])
                kb.op("dve", lambda e: e.tensor_reduce(out=ss[:, c0:c0 + 1], in_=xhs[u][:], axis=AX.X, op=ALU.add), [bxh[u]], [bss[u]])
                TS(kb, "dve", ss[:, c0 + 1:c0 + 2], ss[:, c0:c0 + 1], 1.0 / D, EPS, ALU.mult, ALU.add, [bss[u]], [bss[u]])
                ACT(kb, ss[:, c0 + 2:c0 + 3], ss[:, c0 + 1:c0 + 2], AF.Sqrt, [bss[u]], [bss[u]])
                RECIP(kb, ss[:, c0 + 3:c0 + 4], ss[:, c0 + 2:c0 + 3], [bss[u]], [bss[u]])
                ACT(kb, xhs[u][:], xs[u][:], AF.Identity, [bxs[u], bss[u]], [bxh[u]], scale=ss[:, c0 + 3:c0 + 4])
                for q in range(4):
                    p, bp = cx.ps[(ti * 4 + q) % 4], cx.pb[(ti * 4 + q) % 4]
                    for r in range(4):
                        kc = 4 * q + r
                        TR(kb, p[:, r * 128:(r + 1) * 128], xhs[u][:, kc * 128:(kc + 1) * 128], cx.ident[:], [bxh[u], cx.bconst], [bp])
                    for r in range(4):
                        kc = 4 * q + r
                        ACT(kb, st[:, kc, tt * 128:(tt + 1) * 128], p[:, r * 128:(r + 1) * 128], AF.Identity, [bp, bAB], [bs],
                            scale=Acol[:, kc:kc + 1], bias=Bcol[:, kc:kc + 1])
            kb.dma("sp", hview[:, :, tb * 512:(tb + 1) * 512], st[:], [bs], [bhT])


def phase_inproj(cx, l):
    nc, kb = cx.nc, cx.kb
    win = cx.I["w_in"][l].rearrange("(kc p) n -> p kc n", p=128)
    hview = cx.hT_d.rearrange("c p t -> p c t")
    with ExitStack() as es:
        hTs = es.enter_context(nc.sbuf_tensor(U("hTs"), [128, 16, S], BF16))
        w0 = es.enter_context(nc.sbuf_tensor(U("ipw0"), [128, 16, 512], BF16))
        w1 = es.enter_context(nc.sbuf_tensor(U("ipw1"), [128, 16, 512], BF16))
        o0 = es.enter_context(nc.sbuf_tensor(U("ipo0"), [128, 512], F32))
        o1 = es.enter_context(nc.sbuf_tensor(U("ipo1"), [128, 512], F32))
        bfm = es.enter_context(nc.sbuf_tensor(U("ipbf"), [128, NFM], F32))
        btm = es.enter_context(nc.sbuf_tensor(U("ipbt"), [128, NTM], F32))
        bh = kb.buf()
        ws, bws = [w0, w1], [kb.buf(), kb.buf()]
        os_, bos = [o0, o1], [kb.buf() for _ in range(2)]
        bb = kb.buf()
        for kc in range(16):
            kb.dma("sp", hTs[:, kc, :], cx.hT_d[kc], [cx.bhT], [bh])
        kb.dma("sp", bfm[:], cx.I["b_fm"][l], [], [bb])
        kb.dma("sp", btm[:], cx.I["b_tm"][l], [], [bb])
        no = 0
        groups = [list(range(g, min(g + 4, NFM))) for g in range(0, NFM, 4)]
        for gi, chunks in enumerate(groups):
            w, bw = ws[gi % 2], bws[gi % 2]
            for ci, ch in enumerate(chunks):
                off = 0
                for (lo, n) in FM[ch]:
                    kb.dma("pool", w[:, :, ci * 128 + off:ci * 128 + off + n], win[:, :, lo:lo + n], [], [bw])
                    off += n
            for tb in range(8):
                for ci, ch in enumerate(chunks):
                    p, bp = cx.ps[no % 6], cx.pb[no % 6]
                    o, bo = os_[no % 2], bos[no % 2]
                    no += 1
                    for kc in range(16):
                        MM(kb, p[:, :], w[:, kc, ci * 128:(ci + 1) * 128], hTs[:, kc, tb * 512:(tb + 1) * 512], kc == 0, kc == 15, [bw, bh], [bp])
                    ACT(kb, o[:], p[:, :], AF.Identity, [bp, bb], [bo], bias=bfm[:, ch:ch + 1])
                    kb.dma("sp", cx.pfm_d[ch, :, tb * 512:(tb + 1) * 512], o[:], [bo], [cx.bpfm])
        for gi, (nm, lo, n) in enumerate(TM):
            w, bw = ws[gi % 2], bws[gi % 2]
            kb.dma("pool", w[:, :, 0:n], win[:, :, lo:lo + n], [], [bw])
            dst = cx.tm_d[nm]
            for tt in range(NT):
                p, bp = cx.ps[no % 6], cx.pb[no % 6]
                o, bo = os_[no % 2], bos[no % 2]
                no += 1
                for kc in range(16):
                    MM(kb, p[:, 0:n], hTs[:, kc, tt * 128:(tt + 1) * 128], w[:, kc, 0:n], kc == 0, kc == 15, [bh, bw], [bp])
                TT(kb, "dve", o[:, 0:n], p[:, 0:n], btm[:, TMOFF[nm]:TMOFF[nm] + n], ALU.add, [bp, bb], [bo])
                kb.dma("sp", dst[tt * 128:(tt + 1) * 128, :], o[:, 0:n], [bo], [cx.btm[nm]])


def host_inputs(inp, b):
    f = np.float32
    d = {}
    d["x"] = np.ascontiguousarray(inp["x"][b])
    d["c_col"] = np.ascontiguousarray(inp["c"][b].reshape(16, 128).T)
    d["ada_w"] = inp["ada_w"]
    d["ada_b_col"] = np.ascontiguousarray(inp["ada_b"].reshape(2, 96, 128).transpose(0, 2, 1))
    d["ada_b_rep"] = np.ascontiguousarray(np.broadcast_to(inp["ada_b"][:, None, :], (2, 128, 12288)))
    d["gmix_col"] = np.ascontiguousarray(inp["norm_mix_g"].reshape(2, 16, 128).transpose(0, 2, 1))
    d["gffn_col"] = np.ascontiguousarray(inp["norm_ffn_g"].reshape(2, 16, 128).transpose(0, 2, 1))
    d["w_in"] = inp["w_in"]
    bfm = np.zeros((2, 128, NFM), f)
    for ch, pieces in enumerate(FM):
        off = 0
        for lo, n in pieces:
            bfm[:, off:off + n, ch] = inp["b_in"][:, lo:lo + n]
            off += n
    d["b_fm"] = bfm
    btm = np.concatenate([inp["b_in"][:, lo:lo + n] for _, lo, n in TM], axis=1)
    d["b_tm"] = np.ascontiguousarray(np.broadcast_to(btm[:, None, :], (2, 128, NTM)))
    d["ident"] = np.eye(128, dtype=f)
    cols = np.zeros((2, 128, NCOL), f)
    for l in range(2):
        cols[l, :, CO["lru_cw"]:CO["lru_cw"] + 16] = inp["lru_conv_w"][l].reshape(4, 4, 128).transpose(2, 1, 0).reshape(128, 16)
        cols[l, :, CO["lru_cb"]:CO["lru_cb"] + 4] = inp["lru_conv_b"][l].reshape(4, 128).T
        cols[l, :, CO["lru_ba"]:CO["lru_ba"] + 8] = inp["lru_ba"][l].reshape(2, 4, 128).transpose(2, 0, 1).reshape(128, 8)
        cols[l, :, CO["lru_bx"]:CO["lru_bx"] + 8] = inp["lru_bx"][l].reshape(2, 4, 128).transpose(2, 0, 1).reshape(128, 8)
        cols[l, :, CO["lru_lam"]:CO["lru_lam"] + 8] = inp["lru_lambda"][l].reshape(2, 4, 128).transpose(2, 0, 1).reshape(128, 8)
        cols[l, :, CO["att_qg"]] = np.tile(inp["att_q_norm_g"][l], 2)
        cols[l, :, CO["att_kg"]] = np.tile(inp["att_k_norm_g"][l], 2)
        cols[l, :, CO["hy_cw"]:CO["hy_cw"] + 36] = inp["hy_conv_w"][l].reshape(3, 12, 128).transpose(2, 1, 0).reshape(128, 36)
        cols[l, :, CO["hy_cb"]:CO["hy_cb"] + 12] = inp["hy_conv_b"][l].reshape(12, 128).T
        cols[l, :64, CO["hy_b1"]] = inp["hy_b1"][l]
        cols[l, :64, CO["hy_b2"]] = inp["hy_b2"][l]
        cols[l, :64, CO["hy_sf"]] = inp["hy_sin_freq"][l]
    d["cols"] = cols
    bd = np.zeros((2, 2, 2, 4, 128, 128), f)
    for gi, nm in enumerate(("lru_wa", "lru_wx")):
        for c in range(4):
            for i in range(2):
                bd[:, :, gi, c, 64 * i:64 * i + 64, 64 * i:64 * i + 64] = inp[nm][:, :, 2 * c + i]
    d["lru_bd"] = bd
    t = np.arange(S)
    inv = (10000.0 ** (-np.arange(0, 32, 2, dtype=np.float64) / 32)).astype(np.float64)
    ang = np.zeros((64, S))
    for j in range(64):
        pos = (t // 64) if j < 32 else (t % 64)
        ang[j] = pos * inv[j % 16]
    ang = np.concatenate([ang, ang], 0)
    d["rope_cos"] = np.cos(ang).astype(f)
    d["rope_sin"] = np.sin(ang).astype(f)
    P = np.zeros((128, 128), f)
    for m in range(128):
        if m % 32 < 16:
            P[m, m + 16] = -1.0
        else:
            P[m, m - 16] = 1.0
    d["rope_PT"] = np.ascontiguousarray(P.T)
    b64 = np.zeros((128, 128), f)
    b64[:64, :64] = 1.0
    b64[64:, 64:] = 1.0
    d["bd64"] = b64
    s_, t_ = np.meshgrid(np.arange(128), np.arange(128), indexing="ij")
    mlc = np.zeros((4, 128, 128), f)
    mlc[0] = (s_ <= t_)
    mlc[1] = (s_ >= t_)
    mlc[2] = np.where(s_ <= t_, 0.0, -30000.0)
    mlc[3] = np.where(s_ >= t_, 0.0, -30000.0)
    d["mlc"] = mlc
    d["ml_g_rep"] = np.ascontiguousarray(np.broadcast_to(inp["ml_norm_g"][:, None, :], (2, 128, 512)))
    a = np.arange(NF * 128, dtype=np.int64)
    m = (a[:, None] * a[None, :]) % NDFT
    ang = 2.0 * np.pi * m.astype(np.float64) / NDFT
    valid = (a[:, None] <= 4096) & (a[None, :] <= 4096)
    for nm, fn in (("dft_c", np.cos), ("dft_s", np.sin)):
        full = np.where(valid, fn(ang), 0.0).astype(f)
        blk = full.reshape(NF, 128, NF, 128).transpose(2, 1, 0, 3)
        d[nm] = np.ascontiguousarray(blk).astype(ml_dtypes.bfloat16)
    wf = np.zeros(NF * 128, f)
    wf[0:4097] = 2.0 / NDFT
    wf[0] = 1.0 / NDFT
    wf[4096] = 1.0 / NDFT
    d["dft_wf"] = np.ascontiguousarray(wf.reshape(NF, 128).T)
    pos = np.arange(S, dtype=f)
    tt_ = pos / (S - 1)
    bands = np.linspace(1e-4, 7, 8, dtype=f)
    angf = (2.0 * np.pi * pos / S)[:, None] * bands
    d["hy_feat"] = np.ascontiguousarray(np.concatenate([tt_[:, None], np.cos(angf), -np.sin(angf)], axis=-1).T.astype(f))
    d["hy_negt"] = np.ascontiguousarray((-tt_).reshape(32, 128).T.astype(f))
    d["hy_w1"] = inp["hy_w1"]
    d["hy_w2"] = inp["hy_w2"]
    d["hy_w3"] = inp["hy_w3"]
    d["hy_decay_rep"] = np.ascontiguousarray(np.broadcast_to(inp["hy_decay"][:, None, :], (2, 128, 2048)))
    d["hy_skip_rep"] = np.ascontiguousarray(np.broadcast_to(inp["hy_skip"].reshape(2, 1, 1024), (2, 128, 1024)))
    d["w_out"] = inp["w_out"]
    d["ffn_w1"] = inp["ffn_w1"]
    d["ffn_w3"] = inp["ffn_w3"]
    d["ffn_w2"] = inp["ffn_w2"]
    d["final_g_rep"] = np.ascontiguousarray(np.broadcast_to(inp["final_g"][None, :], (128, D)))
    return d


IN_SPECS = {
    "x": ([S, D], F32), "c_col": ([128, 16], F32), "ada_w": ([2, D, 6 * D], F32),
    "ada_b_col": ([2, 128, 96], F32), "ada_b_rep": ([2, 128, 6 * D], F32),
    "gmix_col": ([2, 128, 16], F32), "gffn_col": ([2, 128, 16], F32),
    "w_in": ([2, D, DIN], F32), "b_fm": ([2, 128, NFM], F32), "b_tm": ([2, 128, NTM], F32),
    "ident": ([128, 128], F32),
    "cols": ([2, 128, NCOL], F32), "lru_bd": ([2, 2, 2, 4, 128, 128], F32),
    "rope_cos": ([128, S], F32), "rope_sin": ([128, S], F32), "rope_PT": ([128, 128], F32), "bd64": ([128, 128], F32),
    "w_out": ([2, D, D], F32), "ffn_w1": ([2, D, DFF], F32), "ffn_w3": ([2, D, DFF], F32), "ffn_w2": ([2, DFF, D], F32),
    "final_g_rep": ([128, D], F32),
    "dft_c": ([NF, 128, NF, 128], BF16), "dft_s": ([NF, 128, NF, 128], BF16), "dft_wf": ([128, NF], F32),
    "hy_feat": ([17, S], F32), "hy_negt": ([128, 32], F32), "hy_w1": ([2, 17, 64], F32), "hy_w2": ([2, 64, 64], F32),
    "hy_w3": ([2, 64, 2048], F32), "hy_decay_rep": ([2, 128, 2048], F32), "hy_skip_rep": ([2, 128, 1024], F32),
    "mlc": ([4, 128, 128], F32), "ml_g_rep": ([2, 128, 512], F32),
}


def build(stages, dbg=()):
    nc = bass.Bass("TRN2", target_bir_lowering=False)
    kb = KB(nc)
    cx = Ctx()
    cx.nc, cx.kb = nc, kb
    cx.I = {k: nc.dram_tensor(k, sh, dt, kind="ExternalInput").ap() for k, (sh, dt) in IN_SPECS.items()}

    def scratch(name, shape, dt):
        kind = "ExternalOutput" if name in dbg else "Internal"
        return nc.dram_tensor(name, shape, dt, kind=kind).ap()

    cx.hT_d = scratch("hT_d", [16, 128, S], BF16)
    cx.pfm_d = scratch("pfm_d", [NFM, 128, S], F32)
    cx.tm_d = {nm: scratch("tm_" + nm, [S, n], F32) for nm, lo, n in TM}
    cx.bhT, cx.bpfm = kb.buf(), kb.buf()
    cx.yT_d = scratch("yT_d", [16, 128, S], BF16)
    cx.byT = kb.buf()
    cx.ztok_d = scratch("ztok_d", [S, 1536], F32)
    cx.z1_d = scratch("z1_d", [S, 512], F32)
    cx.G_d = scratch("G_d", [2, S, 1024], BF16)
    cx.H_d = scratch("H_d", [2, NF, 128, 1024], F32)
    cx.bztok, cx.bz1, cx.bG_d, cx.bH_d = kb.buf(), kb.buf(), kb.buf(), kb.buf()
    cx.xa_d = scratch("xa_d", [S, D], F32)
    cx.xb_d = scratch("xb_d", [S, D], F32)
    cx.bxa, cx.bxb = kb.buf(), kb.buf()
    cx.out = nc.dram_tensor("out", [S, D], F32, kind="ExternalOutput").ap()
    cx.bout = kb.buf()
    cx.btm = {nm: kb.buf() for nm, _, _ in TM}
    cx.dbg_out = scratch("dbg_out", [128, 512], F32) if "dbg_out" in dbg else None
    finals = []
    with ExitStack() as es:
        ident = es.enter_context(nc.sbuf_tensor(U("ident"), [128, 128], F32))
        cact = es.enter_context(nc.sbuf_tensor(U("cact"), [128, 16], F32))
        ones = es.enter_context(nc.sbuf_tensor(U("ones"), [128, 128], F32))
        gcols = es.enter_context(nc.sbuf_tensor(U("gcols"), [128, 64], F32))
        cx.cols = es.enter_context(nc.sbuf_tensor(U("cols"), [128, NCOL], F32))
        cx.bcols = kb.buf()
        cx.epsc = es.enter_context(nc.sbuf_tensor(U("epsc"), [128, 2], F32))
        modT0 = es.enter_context(nc.sbuf_tensor(U("modT0"), [128, 96], F32))
        AB0 = es.enter_context(nc.sbuf_tensor(U("AB0"), [128, 64], F32))
        g1b0 = es.enter_context(nc.sbuf_tensor(U("g1b0"), [128, D], F32))
        g2b0 = es.enter_context(nc.sbuf_tensor(U("g2b0"), [128, D], F32))
        ps0 = es.enter_context(nc.psum_tensor(U("ps0"), [128, 512], F32))
        ps1 = es.enter_context(nc.psum_tensor(U("ps1"), [128, 512], F32))
        ps2 = es.enter_context(nc.psum_tensor(U("ps2"), [128, 512], F32))
        ps3 = es.enter_context(nc.psum_tensor(U("ps3"), [128, 512], F32))
        ps4 = es.enter_context(nc.psum_tensor(U("ps4"), [128, 512], F32))
        ps5 = es.enter_context(nc.psum_tensor(U("ps5"), [128, 512], F32))
        ps6 = es.enter_context(nc.psum_tensor(U("ps6"), [128, 512], F32))
        ps7 = es.enter_context(nc.psum_tensor(U("ps7"), [128, 512], F32))
        cx.ps = [ps0, ps1, ps2, ps3, ps4, ps5, ps6, ps7]
        cx.pb = [kb.buf() for _ in range(8)]
        cx.ident, cx.cact, cx.ones = ident, cact, ones
        cx.bconst, cx.bcact, cx.bcrep = kb.buf(), kb.buf(), kb.buf()
        _b = kb.buf()
        cx.modT, cx.bmod = [modT0, modT0], [_b, _b]
        _b = kb.buf()
        cx.AB, cx.bAB = [AB0, AB0], [_b, _b]
        cx.gb = [[g1b0, g2b0], [g1b0, g2b0]]
        _b = [kb.buf(), kb.buf()]
        cx.bgb = [_b, _b]
        cx.gmix = [gcols[:, 0:16], gcols[:, 16:32]]
        cx.gffn = [gcols[:, 32:48], gcols[:, 48:64]]
        kb.dma("sp", ident[:], cx.I["ident"], [], [cx.bconst])
        for l in range(2):
            kb.dma("sp", gcols[:, 16 * l:16 * l + 16], cx.I["gmix_col"][l], [], [cx.bconst])
            kb.dma("sp", gcols[:, 32 + 16 * l:48 + 16 * l], cx.I["gffn_col"][l], [], [cx.bconst])
        kb.dma("sp", cact[:], cx.I["c_col"], [], [cx.bcact])
        MEMSET(kb, "dve", ones[:], 1.0, [cx.bconst])
        MEMSET(kb, "dve", cx.epsc[:], EPS, [cx.bconst])
        kb.dma("sp", cx.cols[:], cx.I["cols"][0], [], [cx.bcols])
        ACT(kb, cact[:], cact[:], AF.Silu, [cx.bcact], [cx.bcact])
        if "full" in stages:
            bx0 = kb.buf()
            for l in range(2):
                kb.barrier()
                if l > 0:
                    kb.dma("sp", cx.cols[:], cx.I["cols"][l], [], [cx.bcols])
                phase_ada(cx, l)
                kb.barrier()
                xsrc, bxs = (cx.I["x"], bx0) if l == 0 else (cx.xb_d, cx.bxb)
                phase_norm(cx, xsrc, bxs, cx.AB[l][:, 0:16], cx.AB[l][:, 16:32], cx.bAB[l], cx.hT_d, cx.bhT)
                kb.barrier()
                phase_inproj(cx, l)
                kb.barrier()
                phase_lru(cx, l)
                kb.barrier()
                phase_att(cx, l)
                kb.barrier()
                phase_hyena(cx, l)
                kb.barrier()
                phase_mlstm(cx, l)
                kb.barrier()
                phase_outproj(cx, l, xsrc, bxs, cx.xa_d, cx.bxa)
                kb.barrier()
                phase_norm(cx, cx.xa_d, cx.bxa, cx.AB[l][:, 32:48], cx.AB[l][:, 48:64], cx.bAB[l], cx.hT_d, cx.bhT)
                kb.barrier()
                phase_ffn(cx, l, cx.xa_d, cx.bxa, cx.xb_d, cx.bxb)
            kb.barrier()
            phase_final(cx, cx.xb_d, cx.bxb, cx.out, cx.bout)
            finals.append(cx.bout)
        if "ada" in stages:
            phase_ada(cx, 0)
        kb.barrier()
        if "norm1" in stages:
            phase_norm(cx, cx.I["x"], kb.buf(), cx.AB[0][:, 0:16], cx.AB[0][:, 16:32], cx.bAB[0], cx.hT_d, cx.bhT)
            finals.append(cx.bhT)
        if "inproj" in stages:
            kb.barrier()
            phase_inproj(cx, 0)
            finals += [cx.bpfm] + list(cx.btm.values())
        if "lru" in stages:
            kb.barrier()
            phase_lru(cx, 0)
            finals.append(cx.byT)
        if "att" in stages:
            kb.barrier()
            phase_att(cx, 0)
            finals.append(cx.byT)
        if "hyena" in stages:
            kb.barrier()
            phase_hyena(cx, 0)
            finals.append(cx.byT)
        if "mlstm" in stages:
            kb.barrier()
            phase_mlstm(cx, 0)
            finals.append(cx.byT)
        if "dbgmod" in dbg:
            with nc.sbuf_tensor(U("dbgt"), [128, 512], F32) as dbgt:
                bd, bo = kb.buf(), kb.buf()
                MEMSET(kb, "dve", dbgt[:], 0.0, [bd])
                CP(kb, "dve", dbgt[:, 0:96], modT0[:], [cx.bmod[0]], [bd])
                CP(kb, "dve", dbgt[:, 96:160], AB0[:], [cx.bAB[0]], [bd])
                CP(kb, "dve", dbgt[:, 160:416], g1b0[:, 0:256], [cx.bgb[0][0]], [bd])
                kb.dma("sp", cx.dbg_out, dbgt[:], [bd], [bo])
                finals.append(bo)
        kb.finish(finals)
    return nc


def phase_outproj(cx, l, xsrc, bxsrc, xdst, bxdst):
    nc, kb = cx.nc, cx.kb
    wsrc = cx.I["w_out"][l].rearrange("(kc p) n -> p kc n", p=128)
    yview = cx.yT_d.rearrange("c p t -> p c t")
    with ExitStack() as es:
        wt = es.enter_context(nc.sbuf_tensor(U("opw"), [128, 16, D], BF16))
        yb0 = es.enter_context(nc.sbuf_tensor(U("opy0"), [128, 16, 512], BF16))
        yb1 = es.enter_context(nc.sbuf_tensor(U("opy1"), [128, 16, 512], BF16))
        xt0 = es.enter_context(nc.sbuf_tensor(U("opx0"), [128, D], F32))
        xt1 = es.enter_context(nc.sbuf_tensor(U("opx1"), [128, D], F32))
        tm0 = es.enter_context(nc.sbuf_tensor(U("opt0"), [128, 512], F32))
        tm1 = es.enter_context(nc.sbuf_tensor(U("opt1"), [128, 512], F32))
        bw = kb.buf()
        ybs, byb = [yb0, yb1], [kb.buf(), kb.buf()]
        xts, bxt = [xt0, xt1], [kb.buf(), kb.buf()]
        tms, btm_ = [tm0, tm1], [kb.buf(), kb.buf()]
        for q in range(4):
            kb.dma("pool", wt[:, :, q * 512:(q + 1) * 512], wsrc[:, :, q * 512:(q + 1) * 512], [], [bw])
        no = 0
        for tb in range(8):
            yb, by = ybs[tb % 2], byb[tb % 2]
            kb.dma("sp", yb[:], yview[:, :, tb * 512:(tb + 1) * 512], [cx.byT], [by])
            for tt in range(4):
                ti = tb * 4 + tt
                xt, bx = xts[ti % 2], bxt[ti % 2]
                kb.dma("sp", xt[:], xsrc[ti * 128:(ti + 1) * 128, :], [bxsrc], [bx])
                for db in range(4):
                    p, bp = cx.ps[no % 4], cx.pb[no % 4]
                    tm, bt = tms[no % 2], btm_[no % 2]
                    no += 1
                    for kc in range(16):
                        MM(kb, p[:, :], yb[:, kc, tt * 128:(tt + 1) * 128], wt[:, kc, db * 512:(db + 1) * 512], kc == 0, kc == 15, [by, bw], [bp])
                    TT(kb, "dve", tm[:], p[:, :], cx.gb[l][0][:, db * 512:(db + 1) * 512], ALU.mult, [bp, cx.bgb[l][0]], [bt])
                    TT(kb, "pool", xt[:, db * 512:(db + 1) * 512], xt[:, db * 512:(db + 1) * 512], tm[:], ALU.add, [bt, bx], [bx])
                kb.dma("sp", xdst[ti * 128:(ti + 1) * 128, :], xt[:], [bx], [bxdst])


def phase_ffn(cx, l, xsrc, bxsrc, xdst, bxdst):
    nc, kb = cx.nc, cx.kb
    w1s = cx.I["ffn_w1"][l].rearrange("(kc p) n -> p kc n", p=128)
    w3s = cx.I["ffn_w3"][l].rearrange("(kc p) n -> p kc n", p=128)
    w2s = cx.I["ffn_w2"][l].rearrange("(fc p) n -> p fc n", p=128)
    hview = cx.hT_d.rearrange("c p t -> p c t")
    xs4 = xsrc.rearrange("(n p) d -> p n d", p=128)
    xd4 = xdst.rearrange("(n p) d -> p n d", p=128)
    NG = DFF // 256
    with ExitStack() as es:
        h0 = es.enter_context(nc.sbuf_tensor(U("fh0"), [128, 16, 512], BF16))
        h1 = es.enter_context(nc.sbuf_tensor(U("fh1"), [128, 16, 512], BF16))
        uT = es.enter_context(nc.sbuf_tensor(U("fu"), [128, 44, 512], BF16))
        wa = [es.enter_context(nc.sbuf_tensor(U("fw1"), [128, 16, 256], BF16)) for _ in range(2)]
        wb = [es.enter_context(nc.sbuf_tensor(U("fw3"), [128, 16, 256], BF16)) for _ in range(2)]
        w2t = [es.enter_context(nc.sbuf_tensor(U("fw2"), [128, 44, 128], BF16)) for _ in range(2)]
        xr = es.enter_context(nc.sbuf_tensor(U("fx"), [128, 4, D], F32))
        tmp = [es.enter_context(nc.sbuf_tensor(U("ft"), [128, 512], F32)) for _ in range(2)]
        hs, bhs = [h0, h1], [kb.buf(), kb.buf()]
        bu = kb.buf()
        bwa, bwb, bw2 = [kb.buf(), kb.buf()], [kb.buf(), kb.buf()], [kb.buf(), kb.buf()]
        bxr = kb.buf()
        btmp = [kb.buf(), kb.buf()]
        no = 0
        nw = 0
        nw2 = 0
        for tb in range(8):
            h, bh = hs[tb % 2], bhs[tb % 2]
            kb.dma("sp", h[:], hview[:, :, tb * 512:(tb + 1) * 512], [cx.bhT], [bh])
            kb.dma("sp", xr[:], xs4[:, tb * 4:(tb + 1) * 4, :], [bxsrc], [bxr])
            for g in range(NG):
                a, ba = wa[nw % 2], bwa[nw % 2]
                b3, bb = wb[nw % 2], bwb[nw % 2]
                nw += 1
                kb.dma("pool", a[:], w1s[:, :, g * 256:(g + 1) * 256], [], [ba])
                kb.dma("pool", b3[:], w3s[:, :, g * 256:(g + 1) * 256], [], [bb])
                for cc in range(2):
                    fc = 2 * g + cc
                    pa, bpa = cx.ps[(no * 2) % 6], cx.pb[(no * 2) % 6]
                    pb_, bpb = cx.ps[(no * 2 + 1) % 6], cx.pb[(no * 2 + 1) % 6]
                    t, bt = tmp[no % 2], btmp[no % 2]
                    no += 1
                    for kc in range(16):
                        MM(kb, pa[:, :], a[:, kc, cc * 128:(cc + 1) * 128], h[:, kc, :], kc == 0, kc == 15, [ba, bh], [bpa])
                    for kc in range(16):
                        MM(kb, pb_[:, :], b3[:, kc, cc * 128:(cc + 1) * 128], h[:, kc, :], kc == 0, kc == 15, [bb, bh], [bpb])
                    ACT(kb, t[:], pa[:, :], AF.Silu, [bpa], [bt])
                    TT(kb, "dve", uT[:, fc, :], t[:], pb_[:, :], ALU.mult, [bt, bpb], [bu])
            for dq in range(16):
                w2, b2 = w2t[nw2 % 2], bw2[nw2 % 2]
                nw2 += 1
                kb.dma("pool", w2[:], w2s[:, :, dq * 128:(dq + 1) * 128], [], [b2])
                for tt in range(4):
                    p, bp = cx.ps[6 + (no % 2)], cx.pb[6 + (no % 2)]
                    t, bt = tmp[no % 2], btmp[no % 2]
                    no += 1
                    for fc in range(44):
                        MM(kb, p[:, 0:128], uT[:, fc, tt * 128:(tt + 1) * 128], w2[:, fc, :], fc == 0, fc == 43, [bu, b2], [bp])
                    TT(kb, "dve", t[:, 0:128], p[:, 0:128], cx.gb[l][1][:, dq * 128:(dq + 1) * 128], ALU.mult, [bp, cx.bgb[l][1]], [bt])
                    TT(kb, "pool", xr[:, tt, dq * 128:(dq + 1) * 128], xr[:, tt, dq * 128:(dq + 1) * 128], t[:, 0:128], ALU.add, [bt, bxr], [bxr])
            kb.dma("sp", xd4[:, tb * 4:(tb + 1) * 4, :], xr[:], [bxr], [bxdst])


def phase_final(cx, xsrc, bxsrc, out, bout):
    nc, kb = cx.nc, cx.kb
    with ExitStack() as es:
        xs = [es.enter_context(nc.sbuf_tensor(U("fnx"), [128, D], F32)) for _ in range(2)]
        sq = [es.enter_context(nc.sbuf_tensor(U("fnq"), [128, D], F32)) for _ in range(2)]
        fg = es.enter_context(nc.sbuf_tensor(U("fng"), [128, D], F32))
        ss = es.enter_context(nc.sbuf_tensor(U("fns"), [128, 8], F32))
        bxs, bsq, bss = [kb.buf(), kb.buf()], [kb.buf(), kb.buf()], [kb.buf(), kb.buf()]
        bfg = kb.buf()
        kb.dma("sp", fg[:], cx.I["final_g_rep"], [], [bfg])
        for ti in range(NT):
            u = ti % 2
            c0 = 4 * u
            kb.dma("sp", xs[u][:], xsrc[ti * 128:(ti + 1) * 128, :], [bxsrc], [bxs[u]])
            ACT(kb, sq[u][:], xs[u][:], AF.Square, [bxs[u]], [bsq[u]])
            kb.op("dve", lambda e: e.tensor_reduce(out=ss[:, c0:c0 + 1], in_=sq[u][:], axis=AX.X, op=ALU.add), [bsq[u]], [bss[u]])
            TS(kb, "dve", ss[:, c0 + 1:c0 + 2], ss[:, c0:c0 + 1], 1.0 / D, EPS, ALU.mult, ALU.add, [bss[u]], [bss[u]])
            ACT(kb, ss[:, c0 + 2:c0 + 3], ss[:, c0 + 1:c0 + 2], AF.Sqrt, [bss[u]], [bss[u]])
            RECIP(kb, ss[:, c0 + 3:c0 + 4], ss[:, c0 + 2:c0 + 3], [bss[u]], [bss[u]])
            STT(kb, sq[u][:], xs[u][:], ss[:, c0 + 3:c0 + 4], fg[:], ALU.mult, ALU.mult, [bxs[u], bss[u], bfg], [bsq[u]])
            kb.dma("sp", out[ti * 128:(ti + 1) * 128, :], sq[u][:], [bsq[u]], [bout])


def phase_lru(cx, l):
    nc, kb = cx.nc, cx.kb
    C = cx.cols
    bC = cx.bcols
    with ExitStack() as es:
        T = {nm: es.enter_context(nc.sbuf_tensor(U("lr" + nm), [128, S], F32)) for nm in ("ax", "ag", "xc", "r", "i", "t", "h", "hs")}
        B = {nm: kb.buf() for nm in T}
        ys = es.enter_context(nc.sbuf_tensor(U("lry"), [128, S], BF16))
        bys = kb.buf()
        wa = es.enter_context(nc.sbuf_tensor(U("lrwa"), [128, 128], F32))
        wx = es.enter_context(nc.sbuf_tensor(U("lrwx"), [128, 128], F32))
        cl = es.enter_context(nc.sbuf_tensor(U("lrcl"), [128, 8], F32))
        bwa, bwx, bcl = kb.buf(), kb.buf(), kb.buf()
        lam = C[:, CO["lru_lam"]:CO["lru_lam"] + 8]
        ACT(kb, cl[:], lam, AF.Exp, [bC], [bcl], scale=-1.0)
        ACT(kb, cl[:], cl[:], AF.Ln, [bcl], [bcl], bias=1.0)
        TS(kb, "dve", cl[:], cl[:], -8.0, None, ALU.mult, None, [bcl], [bcl])
        no = 0
        for c in range(4):
            ax, ag, xc, r, ii, t, h, hs = (T[k] for k in ("ax", "ag", "xc", "r", "i", "t", "h", "hs"))
            kb.dma("sp", ax[:], cx.pfm_d[c], [cx.bpfm], [B["ax"]])
            kb.dma("sp", ag[:], cx.pfm_d[4 + c], [cx.bpfm], [B["ag"]])
            cw = lambda j: C[:, CO["lru_cw"] + 4 * c + j:CO["lru_cw"] + 4 * c + j + 1]
            ACT(kb, xc[:], ax[:], AF.Identity, [B["ax"], bC], [B["xc"]], scale=cw(2), bias=C[:, CO["lru_cb"] + c:CO["lru_cb"] + c + 1])
            STT(kb, xc[:, 2:S], ax[:, 0:S - 2], cw(0), xc[:, 2:S], ALU.mult, ALU.add, [B["ax"], B["xc"], bC], [B["xc"]])
            STT(kb, xc[:, 1:S], ax[:, 0:S - 1], cw(1), xc[:, 1:S], ALU.mult, ALU.add, [B["ax"], B["xc"], bC], [B["xc"]])
            STT(kb, xc[:, 0:S - 1], ax[:, 1:S], cw(3), xc[:, 0:S - 1], ALU.mult, ALU.add, [B["ax"], B["xc"], bC], [B["xc"]])
            for dr in range(2):
                kb.dma("sp", wa[:], cx.I["lru_bd"][l, dr, 0, c], [], [bwa])
                kb.dma("sp", wx[:], cx.I["lru_bd"][l, dr, 1, c], [], [bwx])
                ba = C[:, CO["lru_ba"] + 4 * dr + c:CO["lru_ba"] + 4 * dr + c + 1]
                bx = C[:, CO["lru_bx"] + 4 * dr + c:CO["lru_bx"] + 4 * dr + c + 1]
                for tb in range(8):
                    sl = slice(tb * 512, (tb + 1) * 512)
                    p1, bp1 = cx.ps[no % 4], cx.pb[no % 4]
                    p2, bp2 = cx.ps[(no + 1) % 4], cx.pb[(no + 1) % 4]
                    no += 2
                    MM(kb, p1[:, :], wa[:], xc[:, sl], True, True, [bwa, B["xc"]], [bp1])
                    MM(kb, p2[:, :], wx[:], xc[:, sl], True, True, [bwx, B["xc"]], [bp2])
                    ACT(kb, r[:, sl], p1[:, :], AF.Sigmoid, [bp1, bC], [B["r"]], bias=ba)
                    ACT(kb, ii[:, sl], p2[:, :], AF.Sigmoid, [bp2, bC], [B["i"]], bias=bx)
                ACT(kb, r[:], r[:], AF.Exp, [B["r"], bcl], [B["r"]], scale=cl[:, 4 * dr + c:4 * dr + c + 1])
                TT(kb, "dve", t[:], r[:], r[:], ALU.mult, [B["r"]], [B["t"]])
                TS(kb, "dve", t[:], t[:], -1.0, 1.0, ALU.mult, ALU.add, [B["t"]], [B["t"]])
                TS(kb, "dve", t[:], t[:], 0.0, None, ALU.max, None, [B["t"]], [B["t"]])
                ACT(kb, t[:], t[:], AF.Sqrt, [B["t"]], [B["t"]])
                TT(kb, "dve", t[:], t[:], ii[:], ALU.mult, [B["t"], B["i"]], [B["t"]])
                TT(kb, "dve", t[:], t[:], xc[:], ALU.mult, [B["t"], B["xc"]], [B["t"]])
                dst = hs if dr == 0 else h
                bdst = B["hs"] if dr == 0 else B["h"]
                if dr == 0:
                    kb.op("dve", lambda e: e.tensor_tensor_scan(out=dst[:], data0=r[:], data1=t[:], initial=0.0, op0=ALU.mult, op1=ALU.add), [B["r"], B["t"]], [bdst])
                else:
                    kb.op("dve", lambda e: e.tensor_tensor_scan(out=dst[:, ::-1], data0=r[:, ::-1], data1=t[:, ::-1], initial=0.0, op0=ALU.mult, op1=ALU.add), [B["r"], B["t"]], [bdst])
                    TT(kb, "dve", hs[:], hs[:], h[:], ALU.add, [B["hs"], B["h"]], [B["hs"]])
            ACT(kb, t[:], ag[:], AF.Square, [B["ag"]], [B["t"]])
            TS(kb, "dve", t[:], t[:], 0.044715, 1.0, ALU.mult, ALU.add, [B["t"]], [B["t"]])
            TT(kb, "dve", t[:], t[:], ag[:], ALU.mult, [B["t"], B["ag"]], [B["t"]])
            ACT(kb, t[:], t[:], AF.Sigmoid, [B["t"]], [B["t"]], scale=1.5957691216057308)
            TT(kb, "dve", t[:], t[:], ag[:], ALU.mult, [B["t"], B["ag"]], [B["t"]])
            TT(kb, "dve", ys[:], t[:], hs[:], ALU.mult, [B["t"], B["hs"]], [bys])
            kb.dma("sp", cx.yT_d[c], ys[:], [bys], [cx.byT])


def phase_att(cx, l):
    nc, kb = cx.nc, cx.kb
    C, bC = cx.cols, cx.bcols
    with ExitStack() as es:
        cos = es.enter_context(nc.sbuf_tensor(U("atc"), [128, S], F32))
        sin = es.enter_context(nc.sbuf_tensor(U("ats"), [128, S], F32))
        PT = es.enter_context(nc.sbuf_tensor(U("atP"), [128, 128], F32))
        BD = es.enter_context(nc.sbuf_tensor(U("atB"), [128, 128], F32))
        qk = [es.enter_context(nc.sbuf_tensor(U("atq"), [128, S], BF16)) for _ in range(6)]
        bqk = [kb.buf() for _ in range(6)]
        bk_ = kb.buf()
        kb.dma("sp", cos[:], cx.I["rope_cos"], [], [bk_])
        kb.dma("sp", sin[:], cx.I["rope_sin"], [], [bk_])
        kb.dma("sp", PT[:], cx.I["rope_PT"], [], [bk_])
        kb.dma("sp", BD[:], cx.I["bd64"], [], [bk_])
        no = 0
        with ExitStack() as es2:
            raw = [es2.enter_context(nc.sbuf_tensor(U("atr"), [128, S], F32)) for _ in range(2)]
            braw = [kb.buf(), kb.buf()]
            tq = [es2.enter_context(nc.sbuf_tensor(U("att"), [128, 512], F32)) for _ in range(4)]
            btq = [kb.buf() for _ in range(4)]
            for ci in range(6):
                ch = 8 + ci
                rw, brw = raw[ci % 2], braw[ci % 2]
                kb.dma("sp", rw[:], cx.pfm_d[ch], [cx.bpfm], [brw])
                gcol = C[:, CO["att_qg"]:CO["att_qg"] + 1] if ci < 4 else C[:, CO["att_kg"]:CO["att_kg"] + 1]
                qs = 0.125 if ci < 4 else 1.0
                for tb in range(8):
                    sl = slice(tb * 512, (tb + 1) * 512)
                    p1, bp1 = cx.ps[no % 4], cx.pb[no % 4]
                    p2, bp2 = cx.ps[(no + 1) % 4], cx.pb[(no + 1) % 4]
                    no += 2
                    t0, t1, t2, t3 = tq
                    ACT(kb, t0[:], rw[:, sl], AF.Square, [brw], [btq[0]])
                    MM(kb, p1[:, :], BD[:], t0[:], True, True, [bk_, btq[0]], [bp1])
                    ACT(kb, t1[:], p1[:, :], AF.Sqrt, [bp1], [btq[1]], scale=1.0 / 64.0, bias=cx.epsc[:, 0:1])
                    RECIP(kb, t1[:], t1[:], [btq[1]], [btq[1]])
                    TT(kb, "dve", t2[:], rw[:, sl], t1[:], ALU.mult, [brw, btq[1]], [btq[2]])
                    TS(kb, "dve", t2[:], t2[:], gcol, qs, ALU.mult, ALU.mult, [btq[2], bC], [btq[2]])
                    MM(kb, p2[:, :], PT[:], t2[:], True, True, [bk_, btq[2]], [bp2])
                    TT(kb, "dve", t3[:], p2[:, :], sin[:, sl], ALU.mult, [bp2, bk_], [btq[3]])
                    TT(kb, "pool", t2[:], t2[:], cos[:, sl], ALU.mult, [btq[2], bk_], [btq[2]])
                    TT(kb, "pool", qk[ci][:, sl], t2[:], t3[:], ALU.add, [btq[2], btq[3]], [bqk[ci]])
        kb.barrier()
        with ExitStack() as es3:
            vraw = es3.enter_context(nc.sbuf_tensor(U("atv"), [128, 32, 128], F32))
            va = [es3.enter_context(nc.sbuf_tensor(U("atva"), [128, 32, 128], BF16)) for _ in range(2)]
            eb = [es3.enter_context(nc.sbuf_tensor(U("ate"), [128, 512], BF16)) for _ in range(3)]
            rt = [es3.enter_context(nc.sbuf_tensor(U("atrt"), [128, 512], F32)) for _ in range(2)]
            ys = [es3.enter_context(nc.sbuf_tensor(U("aty"), [128, S], BF16)) for _ in range(2)]
            bv, bva, beb, brt, bys = kb.buf(), [kb.buf(), kb.buf()], [kb.buf() for _ in range(3)], [kb.buf(), kb.buf()], [kb.buf(), kb.buf()]
            kb.dma("sp", vraw[:], cx.tm_d["vtm"].rearrange("(n p) d -> p n d", p=128), [cx.btm["vtm"]], [bv])
            for kv in range(2):
                MEMSET(kb, "pool", va[kv][:], 1.0, [bva[kv]])
                CP(kb, "dve", va[kv][:, :, 0:64], vraw[:, :, kv * 64:(kv + 1) * 64], [bv], [bva[kv]])
            ne = 0
            for hd in range(8):
                c, ph, kv = hd // 2, (hd % 2) * 64, hd // 4
                q_, bq_ = qk[c], bqk[c]
                k_, bk2 = qk[4 + kv], bqk[4 + kv]
                y_, by_ = ys[c % 2], bys[c % 2]
                for qb in range(8):
                    po, bpo = cx.ps[4 + (qb % 2)], cx.pb[4 + (qb % 2)]
                    for kc in range(32):
                        p, bp = cx.ps[ne % 4], cx.pb[ne % 4]
                        e_, be_ = eb[ne % 3], beb[ne % 3]
                        ne += 1
                        MM(kb, p[:, :], k_[ph:ph + 64, kc * 128:(kc + 1) * 128], q_[ph:ph + 64, qb * 512:(qb + 1) * 512], True, True, [bk2, bq_], [bp])
                        ACT(kb, e_[:], p[:, :], AF.Exp, [bp], [be_])
                        MM(kb, po[:, :], va[kv][:, kc, :], e_[:], kc == 0, kc == 31, [bva[kv], be_], [bpo])
                    r_, br_ = rt[qb % 2], brt[qb % 2]
                    RECIP(kb, r_[0:64, :], po[64:128, :], [bpo], [br_])
                    TT(kb, "dve", y_[ph:ph + 64, qb * 512:(qb + 1) * 512], po[0:64, :], r_[0:64, :], ALU.mult, [bpo, br_], [by_])
                if hd % 2 == 1:
                    kb.dma("sp", cx.yT_d[4 + c], y_[:], [by_], [cx.byT])


def phase_mlstm(cx, l):
    nc, kb = cx.nc, cx.kb
    with ExitStack() as es:
        def sb(name, shape, dt=F32):
            return es.enter_context(nc.sbuf_tensor(U(name), shape, dt))
        mlc = sb("mlc", [128, 4, 128])
        G = sb("mlG", [128, 32, 16])
        LF = sb("mlLF", [128, 32, 16])
        T1 = sb("mlT1", [128, 32, 16])
        BF_, BB_, BT_ = sb("mlBF", [128, 32, 16]), sb("mlBB", [128, 32, 16]), sb("mlBT", [128, 32, 16])
        BIAS = [sb("mlBI", [128, 32, 4]) for _ in range(2)]
        W = [sb("mlW", [128, 32, 4]) for _ in range(2)]
        EB = [sb("mlEB", [128, 32, 4]) for _ in range(2)]
        EBT = [sb("mlEBT", [128, 32, 4]) for _ in range(2)]
        gml = sb("mlg", [128, 512])
        bc, bG, bg = kb.buf(), kb.buf(), kb.buf()
        kb.dma("sp", mlc[:], cx.I["mlc"].rearrange("m p j -> p m j"), [], [bc])
        kb.dma("sp", G[:], cx.tm_d["g"].rearrange("(n p) c -> p n c", p=128), [cx.btm["g"]], [bG])
        kb.dma("sp", gml[:], cx.I["ml_g_rep"][l], [], [bg])
        STT(kb, T1[:], G[:], -1.0, G[:], ALU.mult, ALU.max, [bG], [bG])
        ACT(kb, T1[:], T1[:], AF.Exp, [bG], [bG], scale=-1.0)
        ACT(kb, T1[:], T1[:], AF.Ln, [bG], [bG], bias=1.0)
        TS(kb, "dve", LF[:], G[:], 0.0, None, ALU.min, None, [bG], [bG])
        TT(kb, "dve", LF[:], LF[:], T1[:], ALU.subtract, [bG], [bG])
        LF2 = LF[:].rearrange("p n c -> p (n c)")
        for mi, dst in ((0, BF_), (1, BB_), (None, BT_)):
            p, bp = cx.ps[0], cx.pb[0]
            lhs = mlc[:, mi, :] if mi is not None else cx.ones[:]
            MM(kb, p[:, :], lhs, LF2, True, True, [bc, bG, cx.bconst], [bp])
            CP(kb, "dve", dst[:].rearrange("p n c -> p (n c)"), p[:, :], [bp], [bG])
        for dr in range(2):
            Bx = BF_ if dr == 0 else BB_
            li = G[:, :, 8 * dr:8 * dr + 4]
            b4 = Bx[:, :, 8 * dr + 4:8 * dr + 8]
            bt4 = BT_[:, :, 8 * dr + 4:8 * dr + 8]
            TT(kb, "dve", BIAS[dr][:], li, b4, ALU.subtract, [bG], [bG])
            TT(kb, "dve", W[dr][:], bt4, BIAS[dr][:], ALU.add, [bG], [bG])
            ACT(kb, W[dr][:], W[dr][:], AF.Exp, [bG], [bG])
            ACT(kb, EB[dr][:], b4, AF.Exp, [bG], [bG])
            ACT(kb, EBT[dr][:], bt4, AF.Exp, [bG], [bG])
        raw = sb("mlraw", [128, S])
        qb = sb("mlq", [128, S], BF16)
        kbf = sb("mlk", [128, S], BF16)
        ktok = sb("mlkt", [128, 32, 128])
        vtok = sb("mlvt", [128, 32, 128])
        vaug = sb("mlva", [128, 32, 129], BF16)
        hF, hB = sb("mlhF", [128, 32, 128]), sb("mlhB", [128, 32, 128])
        ys = sb("mlys", [128, S], BF16)
        dg = [sb("mldg", [128, 128]) for _ in range(2)]
        DT = [sb("mlDT", [128, 128]) for _ in range(2)]
        PTt = [sb("mlPT", [128, 128], BF16) for _ in range(2)]
        ins = [sb("mlin", [128, 129]) for _ in range(2)]
        tot = [sb("mltot", [128, 129]) for _ in range(2)]
        den = [sb("mlden", [128, 2]) for _ in range(2)]
        kp = [sb("mlkp", [128, 128], BF16) for _ in range(2)]
        Cst = [sb("mlC", [128, 129]) for _ in range(2)]
        Cbf = [sb("mlCb", [128, 129], BF16) for _ in range(2)]
        rst = sb("mlrs", [128, 64])
        braw, bq, bk, bkt, bvt, bva, bys, brs = [kb.buf() for _ in range(8)]
        bh = [kb.buf(), kb.buf()]
        bdg, bDT, bPT, bin_, btot, bden, bkp = [[kb.buf(), kb.buf()] for _ in range(7)]
        bC, bCb = [kb.buf(), kb.buf()], [kb.buf(), kb.buf()]
        nps = [0]

        def nxt():
            i = nps[0] % 8
            nps[0] += 1
            return cx.ps[i], cx.pb[i]
        for hd in range(4):
            kb.dma("sp", raw[:], cx.pfm_d[26 + hd], [cx.bpfm], [braw])
            ACT(kb, qb[:], raw[:], AF.Identity, [braw], [bq], scale=128.0 ** -0.5)
            kb.dma("sp", raw[:], cx.pfm_d[30 + hd], [cx.bpfm], [braw])
            ACT(kb, kbf[:], raw[:], AF.Identity, [braw], [bk])
            kb.dma("sp", ktok[:], cx.tm_d["dk"].rearrange("(n p) d -> p n d", p=128)[:, :, hd * 128:(hd + 1) * 128], [cx.btm["dk"]], [bkt])
            kb.dma("sp", vtok[:], cx.tm_d["dv"].rearrange("(n p) d -> p n d", p=128)[:, :, hd * 128:(hd + 1) * 128], [cx.btm["dv"]], [bvt])
            MEMSET(kb, "pool", vaug[:], 1.0, [bva])
            CP(kb, "dve", vaug[:, :, 0:128], vtok[:], [bvt], [bva])
            for dr in range(2):
                MEMSET(kb, "dve", Cst[dr][:], 0.0, [bC[dr]])
                MEMSET(kb, "dve", Cbf[dr][:], 0.0, [bCb[dr]])
            for i in range(32):
                for dr in range(2):
                    n = i if dr == 0 else 31 - i
                    u = dr
                    cs = slice(n * 128, (n + 1) * 128)
                    Bx = BF_ if dr == 0 else BB_
                    lfc = 8 * dr + 4 + hd
                    hacc, bhh = (hF, bh[0]) if dr == 0 else (hB, bh[1])
                    TS(kb, "pool", dg[u][:], cx.ident[:], Bx[:, n, lfc:lfc + 1], None, ALU.mult, None, [cx.bconst, bG], [bdg[u]])
                    pB, bpB = nxt()
                    MM(kb, pB[:, 0:128], cx.ones[:], dg[u][:], True, False, [cx.bconst, bdg[u]], [bpB])
                    MM(kb, pB[:, 0:128], cx.ident[:], mlc[:, 2 + dr, :], False, True, [cx.bconst, bc], [bpB])
                    ACT(kb, DT[u][:], pB[:, 0:128], AF.Exp, [bpB, bG], [bDT[u]], bias=BIAS[dr][:, n, hd:hd + 1])
                    pS, bpS = nxt()
                    MM(kb, pS[:, 0:128], kbf[:, cs], qb[:, cs], True, True, [bk, bq], [bpS])
                    TT(kb, "dve", PTt[u][:], pS[:, 0:128], DT[u][:], ALU.mult, [bpS, bDT[u]], [bPT[u]])
                    pI, bpI = nxt()
                    MM(kb, pI[:, 0:129], PTt[u][:], vaug[:, n, :], True, True, [bPT[u], bva], [bpI])
                    pN, bpN = nxt()
                    MM(kb, pN[:, 0:129], qb[:, cs], Cbf[dr][:], True, True, [bq, bCb[dr]], [bpN])
                    ACT(kb, ins[u][:], pN[:, 0:129], AF.Identity, [bpN, bG], [bin_[u]], scale=EB[dr][:, n, hd:hd + 1])
                    TT(kb, "dve", tot[u][:], pI[:, 0:129], ins[u][:], ALU.add, [bpI, bin_[u]], [btot[u]])
                    STT(kb, den[u][:, 0:1], tot[u][:, 128:129], -1.0, tot[u][:, 128:129], ALU.mult, ALU.max, [btot[u]], [bden[u]])
                    TS(kb, "dve", den[u][:, 0:1], den[u][:, 0:1], 1.0, None, ALU.max, None, [bden[u]], [bden[u]])
                    RECIP(kb, den[u][:, 1:2], den[u][:, 0:1], [bden[u]], [bden[u]])
                    TS(kb, "dve", hacc[:, n, :], tot[u][:, 0:128], den[u][:, 1:2], None, ALU.mult, None, [btot[u], bden[u]], [bhh])
                    ACT(kb, kp[u][:], ktok[:, n, :], AF.Identity, [bkt, bG], [bkp[u]], scale=W[dr][:, n, hd:hd + 1])
                    pC, bpC = nxt()
                    MM(kb, pC[:, 0:129], kp[u][:], vaug[:, n, :], True, True, [bkp[u], bva], [bpC])
                    STT(kb, Cst[dr][:], Cst[dr][:], EBT[dr][:, n, hd:hd + 1], pC[:, 0:129], ALU.mult, ALU.add, [bC[dr], bG, bpC], [bC[dr]])
                    CP(kb, "pool", Cbf[dr][:], Cst[dr][:], [bC[dr]], [bCb[dr]])
            TT(kb, "dve", hF[:], hF[:], hB[:], ALU.add, [bh[0], bh[1]], [bh[0]])
            ACT(kb, hB[:], hF[:], AF.Square, [bh[0]], [bh[1]])
            kb.op("dve", lambda e: e.tensor_reduce(out=rst[:, 0:32], in_=hB[:], axis=AX.X, op=ALU.add), [bh[1]], [brs])
            TS(kb, "dve", rst[:, 0:32], rst[:, 0:32], 1.0 / 128.0, EPS, ALU.mult, ALU.add, [brs], [brs])
            ACT(kb, rst[:, 0:32], rst[:, 0:32], AF.Sqrt, [brs], [brs])
            RECIP(kb, rst[:, 32:64], rst[:, 0:32], [brs], [brs])
            kb.dma("sp", vtok[:], cx.tm_d["do"].rearrange("(n p) d -> p n d", p=128)[:, :, hd * 128:(hd + 1) * 128], [cx.btm["do"]], [bvt])
            ACT(kb, vtok[:], vtok[:], AF.Sigmoid, [bvt], [bvt])
            for n in range(32):
                STT(kb, hF[:, n, :], hF[:, n, :], rst[:, 32 + n:33 + n], gml[:, hd * 128:(hd + 1) * 128], ALU.mult, ALU.mult, [bh[0], brs, bg], [bh[0]])
            TT(kb, "dve", hF[:], hF[:], vtok[:], ALU.mult, [bh[0], bvt], [bh[0]])
            for n4 in range(8):
                p, bp = nxt()
                for j in range(4):
                    n = 4 * n4 + j
                    TR(kb, p[:, j * 128:(j + 1) * 128], hF[:, n, :], cx.ident[:], [bh[0], cx.bconst], [bp])
                CP(kb, "dve", ys[:, n4 * 512:(n4 + 1) * 512], p[:, :], [bp], [bys])
            kb.dma("sp", cx.yT_d[12 + hd], ys[:], [bys], [cx.byT])


def phase_hyena(cx, l):
    nc, kb = cx.nc, cx.kb
    C_, bC_ = cx.cols, cx.bcols
    zview = cx.ztok_d.rearrange("(n p) c -> p n c", p=128)
    nps = [0]

    def nxt():
        i = nps[0] % 6
        nps[0] += 1
        return cx.ps[i], cx.pb[i]
    with ExitStack() as es0:
        utok = es0.enter_context(nc.sbuf_tensor(U("hyu"), [128, 32, 512], BF16))
        RS = es0.enter_context(nc.sbuf_tensor(U("hyRS"), [128, 1024], F32))
        wfc = es0.enter_context(nc.sbuf_tensor(U("hywf"), [128, NF], F32))
        skr = es0.enter_context(nc.sbuf_tensor(U("hysk"), [128, 1024], F32))
        bu, bRS, bwf = kb.buf(), kb.buf(), kb.buf()
        kb.dma("sp", wfc[:], cx.I["dft_wf"], [], [bwf])
        kb.dma("sp", skr[:], cx.I["hy_skip_rep"][l], [], [bwf])
        with ExitStack() as es:
            raw = [es.enter_context(nc.sbuf_tensor(U("hyraw"), [128, S], F32)) for _ in range(2)]
            zc = [es.enter_context(nc.sbuf_tensor(U("hyz"), [128, S], F32)) for _ in range(2)]
            st = [es.enter_context(nc.sbuf_tensor(U("hyst"), [128, 32, 128], F32)) for _ in range(2)]
            braw, bz, bst = [kb.buf(), kb.buf()], [kb.buf(), kb.buf()], [kb.buf(), kb.buf()]
            for ch in range(12):
                u = ch % 2
                kb.dma("sp", raw[u][:], cx.pfm_d[14 + ch], [cx.bpfm], [braw[u]])
                cw = lambda j: C_[:, CO["hy_cw"] + 3 * ch + j:CO["hy_cw"] + 3 * ch + j + 1]
                ACT(kb, zc[u][:], raw[u][:], AF.Identity, [braw[u], bC_], [bz[u]], scale=cw(1), bias=C_[:, CO["hy_cb"] + ch:CO["hy_cb"] + ch + 1])
                STT(kb, zc[u][:, 1:S], raw[u][:, 0:S - 1], cw(0), zc[u][:, 1:S], ALU.mult, ALU.add, [braw[u], bz[u], bC_], [bz[u]])
                STT(kb, zc[u][:, 0:S - 1], raw[u][:, 1:S], cw(2), zc[u][:, 0:S - 1], ALU.mult, ALU.add, [braw[u], bz[u], bC_], [bz[u]])
                for n4 in range(8):
                    p, bp = nxt()
                    for j in range(4):
                        n = 4 * n4 + j
                        TR(kb, p[:, j * 128:(j + 1) * 128], zc[u][:, n * 128:(n + 1) * 128], cx.ident[:], [bz[u], cx.bconst], [bp])
                    CP(kb, "dve", st[u][:, 4 * n4:4 * n4 + 4, :], p[:, :].rearrange("p (j c) -> p j c", j=4), [bp], [bst[u]])
                    if ch < 4:
                        CP(kb, "pool", utok[:, 4 * n4:4 * n4 + 4, ch * 128:(ch + 1) * 128], st[u][:, 4 * n4:4 * n4 + 4, :], [bst[u]], [bu])
                kb.dma("sp", zview[:, :, ch * 128:(ch + 1) * 128], st[u][:], [bst[u]], [cx.bztok])
        kb.barrier()
        with ExitStack() as es:
            def sb(name, shape, dt=F32):
                return es.enter_context(nc.sbuf_tensor(U(name), shape, dt))
            feat = sb("hyfe", [128, S])
            h1 = sb("hyh1", [128, S])
            h2 = sb("hyh2", [128, S])
            w1s, w2s, w3s = sb("hyw1", [128, 64]), sb("hyw2", [128, 64]), sb("hyw3", [128, 2048])
            adec = sb("hyad", [128, 2048])
            tcol = sb("hytc", [128, 32])
            sfb = sb("hysfb", [128, 2])
            arg = [sb("hyarg", [128, 512]) for _ in range(2)]
            E = [sb("hyE", [128, 512]) for _ in range(2)]
            hq = [sb("hyhq", [128, 512]) for _ in range(4)]
            sq = [sb("hysq", [128, 512]) for _ in range(2)]
            gst = [sb("hygs", [128, 2, 1024], BF16) for _ in range(2)]
            bk, bh1, bh2, bsfb = kb.buf(), kb.buf(), kb.buf(), kb.buf()
            barg, bE, bsq, bgst = [kb.buf(), kb.buf()], [kb.buf(), kb.buf()], [kb.buf(), kb.buf()], [kb.buf(), kb.buf()]
            bhq = [kb.buf() for _ in range(4)]
            kb.dma("sp", feat[0:17, :], cx.I["hy_feat"], [], [bk])
            kb.dma("sp", w1s[0:17, :], cx.I["hy_w1"][l], [], [bk])
            kb.dma("sp", w2s[0:64, :], cx.I["hy_w2"][l], [], [bk])
            kb.dma("sp", w3s[0:64, :], cx.I["hy_w3"][l], [], [bk])
            kb.dma("sp", adec[:], cx.I["hy_decay_rep"][l], [], [bk])
            kb.dma("sp", tcol[:], cx.I["hy_negt"], [], [bk])
            ACT(kb, adec[:], adec[:], AF.Abs, [bk], [bk])
            sf = C_[0:64, CO["hy_sf"]:CO["hy_sf"] + 1]
            TT(kb, "dve", sfb[0:64, 0:1], C_[0:64, CO["hy_b1"]:CO["hy_b1"] + 1], sf, ALU.mult, [bC_], [bsfb])
            TT(kb, "dve", sfb[0:64, 1:2], C_[0:64, CO["hy_b2"]:CO["hy_b2"] + 1], sf, ALU.mult, [bC_], [bsfb])
            for li, (wsrc, K, src, bsrc, dst, bdst) in enumerate(((w1s, 17, feat, bk, h1, bh1), (w2s, 64, h1, bh1, h2, bh2))):
                for tb in range(8):
                    sl = slice(tb * 512, (tb + 1) * 512)
                    p, bp = nxt()
                    a, ba = arg[tb % 2], barg[tb % 2]
                    MM(kb, p[0:64, :], wsrc[0:K, 0:64], src[0:K, sl], True, True, [bk, bsrc], [bp])
                    ACT(kb, a[0:64, :], p[0:64, :], AF.Identity, [bp, bC_, bsfb], [ba], scale=sf, bias=sfb[0:64, li:li + 1])
                    m_ = E[tb % 2]
                    bm_ = bE[tb % 2]
                    for _rep in range(2):
                        TS(kb, "dve", m_[0:64, :], a[0:64, :], PI, 2 * PI, ALU.is_gt, ALU.mult, [ba], [bm_])
                        TT(kb, "dve", a[0:64, :], a[0:64, :], m_[0:64, :], ALU.subtract, [ba, bm_], [ba])
                        TS(kb, "dve", m_[0:64, :], a[0:64, :], -PI, 2 * PI, ALU.is_lt, ALU.mult, [ba], [bm_])
                        TT(kb, "dve", a[0:64, :], a[0:64, :], m_[0:64, :], ALU.add, [ba, bm_], [ba])
                    ACT(kb, dst[0:64, sl], a[0:64, :], AF.Sin, [ba], [bdst])
            pss = [(cx.ps[6], cx.pb[6]), (cx.ps[7], cx.pb[7])]
            for n in range(32):
                g, bg = gst[n % 2], bgst[n % 2]
                for q in range(4):
                    o, dr = q // 2, q % 2
                    p, bp = nxt()
                    e_, be_ = E[q % 2], bE[q % 2]
                    s_, bs_ = sq[q % 2], bsq[q % 2]
                    MM(kb, p[:, :], h2[0:64, n * 128:(n + 1) * 128], w3s[0:64, q * 512:(q + 1) * 512], True, True, [bh2, bk], [bp])
                    ACT(kb, e_[:], adec[:, q * 512:(q + 1) * 512], AF.Exp, [bk], [be_], scale=tcol[:, n:n + 1])
                    TT(kb, "dve", hq[q][:], p[:, :], e_[:], ALU.mult, [bp, be_], [bhq[q]])
                    ACT(kb, s_[:], hq[q][:], AF.Square, [bhq[q]], [bs_])
                    MM(kb, pss[o][0][:, :], cx.ones[:], s_[:], (n == 0 and dr == 0), (n == 31 and dr == 1), [cx.bconst, bs_], [pss[o][1]])
                    if n == 0 and dr == 1:
                        MEMSET(kb, "dve", hq[q][0:1, :], 0.0, [bhq[q]])
                for o in range(2):
                    TT(kb, "pool", g[:, 0, o * 512:(o + 1) * 512], hq[2 * o][:], hq[2 * o + 1][:], ALU.add, [bhq[2 * o], bhq[2 * o + 1]], [bg])
                    TT(kb, "pool", g[:, 1, o * 512:(o + 1) * 512], hq[2 * o + 1][:], hq[2 * o][:], ALU.subtract, [bhq[2 * o], bhq[2 * o + 1]], [bg])
                kb.dma("sp", cx.G_d.rearrange("r t c -> t r c")[n * 128:(n + 1) * 128], g[:], [bg], [cx.bG_d])
            for o in range(2):
                ACT(kb, RS[:, o * 512:(o + 1) * 512], pss[o][0][:, :], AF.Sqrt, [pss[o][1]], [bRS], bias=cx.epsc[:, 0:1])
            RECIP(kb, RS[:], RS[:], [bRS], [bRS])
        kb.barrier()
        with ExitStack() as es:
            Gs = es.enter_context(nc.sbuf_tensor(U("hyG"), [128, 32, 1024], BF16))
            ct = [es.enter_context(nc.sbuf_tensor(U("hyct"), [128, NF, 128], BF16)) for _ in range(2)]
            ho = [es.enter_context(nc.sbuf_tensor(U("hyho"), [128, 1024], F32)) for _ in range(2)]
            bGs, bct, bho = kb.buf(), [kb.buf(), kb.buf()], [kb.buf(), kb.buf()]
            k = 0
            for ri, blk in enumerate(("dft_c", "dft_s")):
                kb.dma("sp", Gs[:], cx.G_d[ri].rearrange("(n p) c -> p n c", p=128), [cx.bG_d], [bGs])
                for fc in range(NF):
                    c_, bc_ = ct[k % 2], bct[k % 2]
                    h_, bh_ = ho[k % 2], bho[k % 2]
                    k += 1
                    kb.dma("sp", c_[:], cx.I[blk][fc], [], [bc_])
                    for o in range(2):
                        p, bp = nxt()
                        for dc in range(32):
                            MM(kb, p[:, :], c_[:, dc, :], Gs[:, dc, o * 512:(o + 1) * 512], dc == 0, dc == 31, [bc_, bGs], [bp])
                        STT(kb, h_[:, o * 512:(o + 1) * 512], p[:, :], wfc[:, fc:fc + 1], RS[:, o * 512:(o + 1) * 512], ALU.mult, ALU.mult, [bp, bwf, bRS], [bh_])
                    kb.dma("sp", cx.H_d[ri, fc], h_[:], [bh_], [cx.bH_d])
        kb.barrier()
        with ExitStack() as es:
            def sb(name, shape, dt=F32):
                return es.enter_context(nc.sbuf_tensor(U(name), shape, dt))
            YA = sb("hyYA", [128, NF, 512], BF16)
            YB = sb("hyYB", [128, NF, 512], BF16)
            ct = [sb("hyc2", [128, NF, 128], BF16) for _ in range(2)]
            stt = [sb("hys2", [128, NF, 128], BF16) for _ in range(2)]
            Hr = [sb("hyHr", [128, 512]) for _ in range(2)]
            Hi = [sb("hyHi", [128, 512]) for _ in range(2)]
            tt = [sb("hyt", [128, 512]) for _ in range(4)]
            uu = [sb("hyuu", [128, 512]) for _ in range(2)]
            xx = [sb("hyxx", [128, 512]) for _ in range(2)]
            zz = [sb("hyzz", [128, 512]) for _ in range(2)]
            yst = [sb("hyys", [128, 4, 128], BF16) for _ in range(2)]
            bYA, bYB = kb.buf(), kb.buf()
            bct, bstt, bHr, bHi, buu, bxx, bzz, byst = [[kb.buf(), kb.buf()] for _ in range(8)]
            btt = [kb.buf() for _ in range(4)]
            k = 0
            for o in range(2):
                for fc in range(NF):
                    c_, bc_ = ct[k % 2], bct[k % 2]
                    s_, bs_ = stt[k % 2], bstt[k % 2]
                    hr, bhr = Hr[k % 2], bHr[k % 2]
                    hi, bhi = Hi[k % 2], bHi[k % 2]
                    k += 1
                    kb.dma("sp", c_[:], cx.I["dft_c"][fc], [], [bc_])
                    kb.dma("sp", s_[:], cx.I["dft_s"][fc], [], [bs_])
                    kb.dma("sp", hr[:], cx.H_d[0, fc, :, o * 512:(o + 1) * 512], [cx.bH_d], [bhr])
                    kb.dma("sp", hi[:], cx.H_d[1, fc, :, o * 512:(o + 1) * 512], [cx.bH_d], [bhi])
                    pA, bpA = nxt()
                    pB, bpB = nxt()
                    for dc in range(32):
                        MM(kb, pA[:, :], c_[:, dc, :], utok[:, dc, :], dc == 0, dc == 31, [bc_, bu], [bpA])
                    for dc in range(32):
                        MM(kb, pB[:, :], s_[:, dc, :], utok[:, dc, :], dc == 0, dc == 31, [bs_, bu], [bpB])
                    TT(kb, "dve", tt[0][:], pA[:, :], hr[:], ALU.mult, [bpA, bhr], [btt[0]])
                    TT(kb, "dve", tt[1][:], pB[:, :], hi[:], ALU.mult, [bpB, bhi], [btt[1]])
                    TT(kb, "dve", tt[2][:], pB[:, :], hr[:], ALU.mult, [bpB, bhr], [btt[2]])
                    TT(kb, "dve", tt[3][:], pA[:, :], hi[:], ALU.mult, [bpA, bhi], [btt[3]])
                    TT(kb, "pool", YA[:, fc, :], tt[0][:], tt[1][:], ALU.add, [btt[0], btt[1]], [bYA])
                    TT(kb, "pool", YB[:, fc, :], tt[2][:], tt[3][:], ALU.subtract, [btt[2], btt[3]], [bYB])
                for n in range(32):
                    c_, bc_ = ct[k % 2], bct[k % 2]
                    s_, bs_ = stt[k % 2], bstt[k % 2]
                    u_, bu_ = uu[k % 2], buu[k % 2]
                    x_, bx_ = xx[k % 2], bxx[k % 2]
                    z_, bz_ = zz[k % 2], bzz[k % 2]
                    k += 1
                    kb.dma("sp", c_[:], cx.I["dft_c"][n], [], [bc_])
                    kb.dma("sp", s_[:], cx.I["dft_s"][n], [], [bs_])
                    if o == 0:
                        kb.dma("sp", u_[:], cx.ztok_d[n * 128:(n + 1) * 128, 0:512], [cx.bztok], [bu_])
                    else:
                        kb.dma("sp", u_[:], cx.z1_d[n * 128:(n + 1) * 128, :], [cx.bz1], [bu_])
                    kb.dma("sp", x_[:], cx.ztok_d[n * 128:(n + 1) * 128, 512 * (o + 1):512 * (o + 2)], [cx.bztok], [bx_])
                    p, bp = nxt()
                    for fc in range(NF):
                        MM(kb, p[:, :], c_[:, fc, :], YA[:, fc, :], fc == 0, False, [bc_, bYA], [bp])
                    for fc in range(NF):
                        MM(kb, p[:, :], s_[:, fc, :], YB[:, fc, :], False, fc == NF - 1, [bs_, bYB], [bp])
                    TT(kb, "pool", u_[:], u_[:], skr[:, o * 512:(o + 1) * 512], ALU.mult, [bu_, bwf], [bu_])
                    TT(kb, "dve", z_[:], p[:, :], u_[:], ALU.add, [bp, bu_], [bz_])
                    TT(kb, "dve", z_[:], z_[:], x_[:], ALU.mult, [bz_, bx_], [bz_])
                    if o == 0:
                        kb.dma("sp", cx.z1_d[n * 128:(n + 1) * 128, :], z_[:], [bz_], [cx.bz1])
                        CP(kb, "pool", utok[:, n, :], z_[:], [bz_], [bu])
                    else:
                        ys_, bys_ = yst[n % 2], byst[n % 2]
                        pt, bpt = nxt()
                        for j in range(4):
                            TR(kb, pt[:, j * 128:(j + 1) * 128], z_[:, j * 128:(j + 1) * 128], cx.ident[:], [bz_, cx.bconst], [bpt])
                        CP(kb, "dve", ys_[:], pt[:, :].rearrange("p (j c) -> p j c", j=4), [bpt], [bys_])
                        kb.dma("sp", cx.yT_d[8:12].rearrange("c p t -> p c t")[:, :, n * 128:(n + 1) * 128], ys_[:], [bys_], [cx.byT])
                if o == 0:
                    kb.barrier()


def kernel(**inputs):
    inp = {k: np.asarray(v) for k, v in inputs.items()}
    nc = build({"full"})
    in_maps = [host_inputs(inp, b) for b in range(4)]
    res = run_bass_kernel_spmd(nc, in_maps, core_ids=[0, 1, 2, 3])
    return np.stack([np.asarray(r["out"]) for r in res.results], 0).astype(np.float32)
```

```python
import math
import ml_dtypes
from contextlib import ExitStack
from concourse.bass_utils import run_bass_kernel_spmd
import numpy as np
import concourse.bass as bass
import concourse.mybir as mybir

F32 = mybir.dt.float32
BF16 = mybir.dt.bfloat16
AF = mybir.ActivationFunctionType
ALU = mybir.AluOpType
AX = mybir.AxisListType


NPOOL = 84


class Buf:
    __slots__ = ("name", "lw", "rd", "sem", "cnt")

    def __init__(self, name=""):
        self.name = name
        self.lw = None
        self.rd = {}
        self.sem = None
        self.cnt = 0


class KB:
    def __init__(self, nc):
        self.nc = nc
        self.eng = {"pe": nc.tensor, "dve": nc.vector, "act": nc.scalar,
                    "pool": nc.gpsimd, "sp": nc.sync}
        self.sems = {}
        self.cnt = {}
        for k in self.eng:
            self.sems[k] = nc.alloc_semaphore("c_" + k)
            self.cnt[k] = 0
        self.waited = {k: {} for k in self.eng}
        self.ndma = 0
        self.nbuf = 0
        self.dmacnt = {}
        self.bufs = []
        self.pool = [nc.alloc_semaphore("d%d" % i) for i in range(NPOOL)]
        for h in list(self.sems.values()) + self.pool:
            nc.gpsimd.sem_clear(h)
        nc.all_engine_barrier()

    def buf(self, name=""):
        self.nbuf += 1
        b = Buf(name or ("b%d" % self.nbuf))
        self.bufs.append(b)
        return b

    def _wait(self, e, reads, writes):
        need = {}
        for b in reads:
            if b.lw is not None:
                k, v = b.lw
                if need.get(k, 0) < v:
                    need[k] = v
        for b in writes:
            if b.lw is not None:
                k, v = b.lw
                if need.get(k, 0) < v:
                    need[k] = v
            for k, v in b.rd.items():
                if need.get(k, 0) < v:
                    need[k] = v
        w = self.waited[e]
        for k, v in need.items():
            if k == e and (e == "pe" or v > self.cnt[e]):
                continue
            if w.get(k, 0) < v:
                self.eng[e].wait_ge(self.sems[k], v)
                w[k] = v

    def op(self, e, fn, reads=(), writes=(), inc=True):
        self._wait(e, reads, writes)
        ins = fn(self.eng[e])
        if inc:
            self.cnt[e] += 1
            ins.then_inc(self.sems[e], 1)
            tok = (e, self.cnt[e])
        else:
            tok = (e, self.cnt[e] + 1)
        for b in writes:
            b.lw = tok
            b.rd = {}
        for b in reads:
            if b.rd.get(tok[0], 0) < tok[1]:
                b.rd[tok[0]] = tok[1]
        return ins

    def dma(self, q, out, in_, reads, writes, **kw):
        self._wait(q, reads, writes)
        wb = writes[0]
        if wb.sem is None:
            key = "d%d" % self.ndma
            self.sems[key] = self.pool[self.ndma]
            self.ndma += 1
            wb.sem = key
        self.dmacnt[wb.sem] = self.dmacnt.get(wb.sem, 0) + 16
        self.eng[q].dma_start(out=out, in_=in_, **kw).then_inc(self.sems[wb.sem], 16)
        tok = (wb.sem, self.dmacnt[wb.sem])
        for b in writes:
            b.lw = tok
            b.rd = {}
        for b in reads:
            if b.rd.get(tok[0], 0) < tok[1]:
                b.rd[tok[0]] = tok[1]

    def barrier(self):
        cur = dict(self.cnt)
        cur.update(self.dmacnt)
        for e in self.eng:
            w = self.waited[e]
            for k, v in cur.items():
                if v <= 0 or (k == e and e == "pe"):
                    continue
                if w.get(k, 0) < v:
                    self.eng[e].wait_ge(self.sems[k], v)
                    w[k] = v
        for b in self.bufs:
            b.lw = None
            b.rd = {}
            b.sem = None
        self.ndma = 0

    def finish(self, bufs):
        self._wait("sp", bufs, [])


S = 4096
D = 2048
NT = 32
DIN = 5392
DFF = 5632
EPS = 1e-6
NF = 33
NDFT = 8192
PI = math.pi

SH = S // 2
NTH = SH // 128
NCK = 2
HW = 256


def half_cols(r):
    c = []
    c += list(range(0 + HW * r, 0 + HW * r + HW))
    c += list(range(512 + HW * r, 512 + HW * r + HW))
    c += list(range(1024 + HW * r, 1024 + HW * r + HW))
    k = list(range(1536 + 64 * r, 1536 + 64 * r + 64))
    c += k + k
    for seg in range(3):
        c += list(range(1792 + 512 * seg + HW * r, 1792 + 512 * seg + HW * r + HW))
    c += list(range(3328 + HW * r, 3328 + HW * r + HW))
    c += list(range(3840 + HW * r, 3840 + HW * r + HW))
    nfm = len(c)
    c += list(range(1664 + 64 * r, 1664 + 64 * r + 64))
    c += list(range(3840 + HW * r, 3840 + HW * r + HW))
    c += list(range(4352 + HW * r, 4352 + HW * r + HW))
    c += list(range(4864 + HW * r, 4864 + HW * r + HW))
    for g in range(4):
        c += [5376 + 4 * g + 2 * r, 5376 + 4 * g + 2 * r + 1]
    return np.array(c), nfm


NFM = 17
TM = [("vtm", 2176, 64), ("dk", 2240, 256), ("dv", 2496, 256), ("do", 2752, 256), ("g", 3008, 8)]
NCH = 3016
TMOFF = {}
_o = 0
for nm, lo, n in TM:
    TMOFF[nm] = _o
    _o += n
NTM = _o


CO = {}
_o = 0
for nm, n in (("lru_cw", 8), ("lru_cb", 2), ("lru_ba", 4), ("lru_bx", 4), ("lru_lam", 4), ("att_qg", 1), ("att_kg", 1),
              ("hy_cw", 18), ("hy_cb", 6), ("hy_b1", 1), ("hy_b2", 1), ("hy_sf", 1)):
    CO[nm] = _o
    _o += n
NCOL = _o


class Ctx:
    pass


_UC = [0]


def U(name):
    _UC[0] += 1
    return "%s_%d" % (name, _UC[0])


def ACT(kb, out, in_, func, rd, wr, **kw):
    return kb.op("act", lambda e: e.activation(out=out, in_=in_, func=func, **kw), rd, wr)


def MM(kb, out, lhsT, rhs, start, stop, rd, wr):
    return kb.op("pe", lambda e: e.matmul(out, lhsT=lhsT, rhs=rhs, start=start, stop=stop), rd, wr, inc=stop)


def TR(kb, out, in_, ident, rd, wr):
    return kb.op("pe", lambda e: e.transpose(out=out, in_=in_, identity=ident), rd, wr)


def TT(kb, eng, out, in0, in1, op, rd, wr):
    return kb.op(eng, lambda e: e.tensor_tensor(out=out, in0=in0, in1=in1, op=op), rd, wr)


def TS(kb, eng, out, in0, s1, s2, op0, op1, rd, wr):
    if op1 is None:
        return kb.op(eng, lambda e: e.tensor_scalar(out=out, in0=in0, scalar1=s1, scalar2=None, op0=op0), rd, wr)
    return kb.op(eng, lambda e: e.tensor_scalar(out=out, in0=in0, scalar1=s1, scalar2=s2, op0=op0, op1=op1), rd, wr)


def STT(kb, out, in0, scalar, in1, op0, op1, rd, wr):
    return kb.op("dve", lambda e: e.scalar_tensor_tensor(out=out, in0=in0, scalar=scalar, in1=in1, op0=op0, op1=op1), rd, wr)


def CP(kb, eng, out, in_, rd, wr):
    return kb.op(eng, lambda e: e.tensor_copy(out=out, in_=in_), rd, wr)


def RECIP(kb, out, in_, rd, wr):
    return kb.op("dve", lambda e: e.reciprocal(out=out, in_=in_), rd, wr)


def MEMSET(kb, eng, ap, val, wr):
    return kb.op(eng, lambda e: e.memset(ap, val), [], wr)


def phase_ada(cx, l):
    nc, kb = cx.nc, cx.kb
    adaw = cx.I["ada_w"][l].rearrange("(kc p) n -> p kc n", p=128)
    with ExitStack() as es:
        aw0 = es.enter_context(nc.sbuf_tensor(U("adaw0"), [128, 16, 512], F32))
        aw1 = es.enter_context(nc.sbuf_tensor(U("adaw1"), [128, 16, 512], F32))
        adab = es.enter_context(nc.sbuf_tensor(U("adab"), [128, 96], F32))
        gbias = es.enter_context(nc.sbuf_tensor(U("gbias"), [128, 512], F32))
        crep = es.enter_context(nc.sbuf_tensor(U("crep"), [128, 16, 128], F32))
        cx.crep = crep
        cx.bcrep = kb.buf()
        for kc in range(16):
            TS(kb, "dve", crep[:, kc, :], cx.ones[:], cx.cact[:, kc:kc + 1], None, ALU.mult, None, [cx.bconst, cx.bcact], [cx.bcrep])
        aws = [aw0, aw1]
        baw = [kb.buf(), kb.buf()]
        bab, bgb = kb.buf(), kb.buf()
        kb.dma("sp", adab[:], cx.I["ada_b_col"][l], [], [bab])
        psm, bpsm = cx.ps[7], cx.pb[7]
        for j in range(24):
            a, ba = aws[j % 2], baw[j % 2]
            kb.dma("sp", a[:], adaw[:, :, j * 512:(j + 1) * 512], [], [ba])
            which = {2: 0, 5: 1}.get(j // 4)
            if which is not None:
                p, bp = cx.ps[j % 2], cx.pb[j % 2]
                for kc in range(16):
                    MM(kb, p[:, :], cx.crep[:, kc, :], a[:, kc, :], kc == 0, kc == 15, [cx.bcrep, ba], [bp])
                kb.dma("sp", gbias[:], cx.I["ada_b_rep"][l][:, j * 512:(j + 1) * 512], [], [bgb])
                TT(kb, "dve", cx.gb[l][which][:, (j % 4) * 512:(j % 4 + 1) * 512], p[:, :], gbias[:], ALU.add, [bp, bgb], [cx.bgb[l][which]])
            else:
                for cc in range(4):
                    jj = 4 * j + cc
                    for kc in range(16):
                        MM(kb, psm[:, jj:jj + 1], a[:, kc, cc * 128:(cc + 1) * 128], cx.cact[:, kc:kc + 1], kc == 0, kc == 15, [ba, cx.bcact], [bpsm])
        for lo, hi in ((0, 32), (48, 80)):
            TT(kb, "dve", cx.modT[l][:, lo:hi], psm[:, lo:hi], adab[:, lo:hi], ALU.add, [bpsm, bab], [cx.bmod[l]])
        for s, (sc_lo, sh_lo, gcol) in enumerate(((16, 0, cx.gmix[l]), (64, 48, cx.gffn[l]))):
            TS(kb, "dve", cx.AB[l][:, 32 * s:32 * s + 16], cx.modT[l][:, sc_lo:sc_lo + 16], 1.0, None, ALU.add, None, [cx.bmod[l]], [cx.bAB[l]])
            TT(kb, "dve", cx.AB[l][:, 32 * s:32 * s + 16], cx.AB[l][:, 32 * s:32 * s + 16], gcol, ALU.mult, [cx.bAB[l], cx.bconst], [cx.bAB[l]])
            CP(kb, "dve", cx.AB[l][:, 32 * s + 16:32 * s + 32], cx.modT[l][:, sh_lo:sh_lo + 16], [cx.bmod[l]], [cx.bAB[l]])


def phase_norm(cx, xsrc, bxsrc, Acol, Bcol, bAB, hT_d, bhT, ntiles=NT, rowmap=None):
    nc, kb = cx.nc, cx.kb
    hview = hT_d.rearrange("c p t -> p c t")
    with ExitStack() as es:
        x0 = es.enter_context(nc.sbuf_tensor(U("nx0"), [128, D], F32))
        x1 = es.enter_context(nc.sbuf_tensor(U("nx1"), [128, D], F32))
        xh0 = es.enter_context(nc.sbuf_tensor(U("nxh0"), [128, D], F32))
        xh1 = es.enter_context(nc.sbuf_tensor(U("nxh1"), [128, D], F32))
        st0 = es.enter_context(nc.sbuf_tensor(U("nst0"), [128, 16, 512], BF16))
        st1 = es.enter_context(nc.sbuf_tensor(U("nst1"), [128, 16, 512], BF16))
        ss = es.enter_context(nc.sbuf_tensor(U("nss"), [128, 8], F32))
        xs, bxs = [x0, x1], [kb.buf(), kb.buf()]
        xhs, bxh = [xh0, xh1], [kb.buf(), kb.buf()]
        sts, bst = [st0, st1], [kb.buf(), kb.buf()]
        bss = [kb.buf(), kb.buf()]
        for tb in range(ntiles // 4):
            st, bs = sts[tb % 2], bst[tb % 2]
            for tt in range(4):
                ti = tb * 4 + tt
                u = ti % 2
                r0 = ti * 128 if rowmap is None else rowmap(ti)
                kb.dma("sp", xs[u][:], xsrc[r0:r0 + 128, :], [bxsrc], [bxs[u]])
                c0 = 4 * u
                ACT(kb, xhs[u][:], xs[u][:], AF.Square, [bxs[u]], [bxh[u]])
                kb.op("dve", lambda e: e.tensor_reduce(out=ss[:, c0:c0 + 1], in_=xhs[u][:], axis=AX.X, op=ALU.add), [bxh[u]], [bss[u]])
                TS(kb, "dve", ss[:, c0 + 1:c0 + 2], ss[:, c0:c0 + 1], 1.0 / D, EPS, ALU.mult, ALU.add, [bss[u]], [bss[u]])
                ACT(kb, ss[:, c0 + 2:c0 + 3], ss[:, c0 + 1:c0 + 2], AF.Sqrt, [bss[u]], [bss[u]])
                RECIP(kb, ss[:, c0 + 3:c0 + 4], ss[:, c0 + 2:c0 + 3], [bss[u]], [bss[u]])
                ACT(kb, xhs[u][:], xs[u][:], AF.Identity, [bxs[u], bss[u]], [bxh[u]], scale=ss[:, c0 + 3:c0 + 4])
                for q in range(4):
                    p, bp = cx.ps[(ti * 4 + q) % 4], cx.pb[(ti * 4 + q) % 4]
                    for r in range(4):
                        kc = 4 * q + r
                        TR(kb, p[:, r * 128:(r + 1) * 128], xhs[u][:, kc * 128:(kc + 1) * 128], cx.ident[:], [bxh[u], cx.bconst], [bp])
                    for r in range(4):
                        kc = 4 * q + r
                        ACT(kb, st[:, kc, tt * 128:(tt + 1) * 128], p[:, r * 128:(r + 1) * 128], AF.Identity, [bp, bAB], [bs],
                            scale=Acol[:, kc:kc + 1], bias=Bcol[:, kc:kc + 1])
            kb.dma("sp", hview[:, :, tb * 512:(tb + 1) * 512], st[:], [bs], [bhT])


def phase_inproj(cx, l):
    nc, kb = cx.nc, cx.kb
    win = cx.I["w_in_h"][l].rearrange("(kc p) n -> p kc n", p=128)
    with ExitStack() as es:
        hTs = es.enter_context(nc.sbuf_tensor(U("hTs"), [128, 16, S], BF16))
        w0 = es.enter_context(nc.sbuf_tensor(U("ipw0"), [128, 16, 512], BF16))
        w1 = es.enter_context(nc.sbuf_tensor(U("ipw1"), [128, 16, 512], BF16))
        o0 = es.enter_context(nc.sbuf_tensor(U("ipo0"), [128, 512], F32))
        o1 = es.enter_context(nc.sbuf_tensor(U("ipo1"), [128, 512], F32))
        bfm = es.enter_context(nc.sbuf_tensor(U("ipbf"), [128, NFM], F32))
        btm = es.enter_context(nc.sbuf_tensor(U("ipbt"), [128, NTM], F32))
        bh = kb.buf()
        ws, bws = [w0, w1], [kb.buf(), kb.buf()]
        os_, bos = [o0, o1], [kb.buf() for _ in range(2)]
        bb = kb.buf()
        for kc in range(16):
            kb.dma("sp", hTs[:, kc, :], cx.hT_d[kc], [cx.bhT], [bh])
        kb.dma("sp", bfm[:], cx.I["b_fm"][l], [], [bb])
        kb.dma("sp", btm[:], cx.I["b_tm"][l], [], [bb])
        no = 0
        groups = [list(range(g, min(g + 4, NFM))) for g in range(0, NFM, 4)]
        for gi, chunks in enumerate(groups):
            w, bw = ws[gi % 2], bws[gi % 2]
            nw = 128 * len(chunks)
            kb.dma("pool", w[:, :, 0:nw], win[:, :, chunks[0] * 128:chunks[0] * 128 + nw], [], [bw])
            for tb in range(8):
                for ci, ch in enumerate(chunks):
                    p, bp = cx.ps[no % 6], cx.pb[no % 6]
                    o, bo = os_[no % 2], bos[no % 2]
                    no += 1
                    for kc in range(16):
                        MM(kb, p[:, :], w[:, kc, ci * 128:(ci + 1) * 128], hTs[:, kc, tb * 512:(tb + 1) * 512], kc == 0, kc == 15, [bw, bh], [bp])
                    ACT(kb, o[:], p[:, :], AF.Identity, [bp, bb], [bo], bias=bfm[:, ch:ch + 1])
                    kb.dma("sp", cx.pfm_d[ch, :, tb * 512:(tb + 1) * 512], o[:], [bo], [cx.bpfm])
        for gi, (nm, lo, n) in enumerate(TM):
            w, bw = ws[(gi + 1) % 2], bws[(gi + 1) % 2]
            kb.dma("pool", w[:, :, 0:n], win[:, :, lo:lo + n], [], [bw])
            dst = cx.tm_d[nm]
            for tt in range(NT):
                p, bp = cx.ps[no % 6], cx.pb[no % 6]
                o, bo = os_[no % 2], bos[no % 2]
                no += 1
                for kc in range(16):
                    MM(kb, p[:, 0:n], hTs[:, kc, tt * 128:(tt + 1) * 128], w[:, kc, 0:n], kc == 0, kc == 15, [bh, bw], [bp])
                TT(kb, "dve", o[:, 0:n], p[:, 0:n], btm[:, TMOFF[nm]:TMOFF[nm] + n], ALU.add, [bp, bb], [bo])
                kb.dma("sp", dst[tt * 128:(tt + 1) * 128, :], o[:, 0:n], [bo], [cx.btm[nm]])


def host_shared(inp):
    f = np.float32
    d = {}
    d["ada_w"] = inp["ada_w"]
    d["ada_b_col"] = np.ascontiguousarray(inp["ada_b"].reshape(2, 96, 128).transpose(0, 2, 1))
    d["ada_b_rep"] = np.ascontiguousarray(np.broadcast_to(inp["ada_b"][:, None, :], (2, 128, 12288)))
    d["gmix_col"] = np.ascontiguousarray(inp["norm_mix_g"].reshape(2, 16, 128).transpose(0, 2, 1))
    d["gffn_col"] = np.ascontiguousarray(inp["norm_ffn_g"].reshape(2, 16, 128).transpose(0, 2, 1))
    d["ident"] = np.eye(128, dtype=f)
    t = np.arange(S)
    inv = (10000.0 ** (-np.arange(0, 32, 2, dtype=np.float64) / 32)).astype(np.float64)
    ang = np.zeros((64, S))
    for j in range(64):
        pos = (t // 64) if j < 32 else (t % 64)
        ang[j] = pos * inv[j % 16]
    ang = np.concatenate([ang, ang], 0)
    d["rope_cos"] = np.cos(ang).astype(f)
    d["rope_sin"] = np.sin(ang).astype(f)
    P = np.zeros((128, 128), f)
    for m in range(128):
        if m % 32 < 16:
            P[m, m + 16] = -1.0
        else:
            P[m, m - 16] = 1.0
    d["rope_PT"] = np.ascontiguousarray(P.T)
    b64 = np.zeros((128, 128), f)
    b64[:64, :64] = 1.0
    b64[64:, 64:] = 1.0
    d["bd64"] = b64
    s_, t_ = np.meshgrid(np.arange(128), np.arange(128), indexing="ij")
    mlc = np.zeros((4, 128, 128), f)
    mlc[0] = (s_ <= t_)
    mlc[1] = (s_ >= t_)
    mlc[2] = np.where(s_ <= t_, 0.0, -30000.0)
    mlc[3] = np.where(s_ >= t_, 0.0, -30000.0)
    d["mlc"] = mlc
    a = np.arange(NF * 128, dtype=np.int64)
    m = (a[:, None] * a[None, :]) % NDFT
    ang = 2.0 * np.pi * m.astype(np.float64) / NDFT
    valid = (a[:, None] <= 4096) & (a[None, :] <= 4096)
    for nm, fn in (("dft_c", np.cos), ("dft_s", np.sin)):
        full = np.where(valid, fn(ang), 0.0).astype(f)
        blk = full.reshape(NF, 128, NF, 128).transpose(2, 1, 0, 3)
        d[nm] = np.ascontiguousarray(blk).astype(ml_dtypes.bfloat16)
    wf = np.zeros(NF * 128, f)
    wf[0:4097] = 2.0 / NDFT
    wf[0] = 1.0 / NDFT
    wf[4096] = 1.0 / NDFT
    d["dft_wf"] = np.ascontiguousarray(wf.reshape(NF, 128).T)
    pos = np.arange(S, dtype=f)
    tt_ = pos / (S - 1)
    bands = np.linspace(1e-4, 7, 8, dtype=f)
    angf = (2.0 * np.pi * pos / S)[:, None] * bands
    d["hy_feat"] = np.ascontiguousarray(np.concatenate([tt_[:, None], np.cos(angf), -np.sin(angf)], axis=-1).T.astype(f))
    d["hy_negt"] = np.ascontiguousarray((-tt_).reshape(32, 128).T.astype(f))
    d["hy_w1"] = inp["hy_w1"]
    d["hy_w2"] = inp["hy_w2"]
    d["ffn_w1"] = inp["ffn_w1"]
    d["ffn_w3"] = inp["ffn_w3"]
    d["ffn_w2"] = inp["ffn_w2"]
    d["final_g_rep"] = np.ascontiguousarray(np.broadcast_to(inp["final_g"][None, :], (128, D)))
    return d


def host_half(inp, r):
    f = np.float32
    d = {}
    ci, nfm = half_cols(r)
    assert nfm == NFM * 128 and len(ci) == NCH
    d["w_in_h"] = np.ascontiguousarray(inp["w_in"][:, :, ci])
    bh = inp["b_in"][:, ci]
    d["b_fm"] = np.ascontiguousarray(bh[:, :nfm].reshape(2, NFM, 128).transpose(0, 2, 1))
    d["b_tm"] = np.ascontiguousarray(np.broadcast_to(bh[:, None, nfm:], (2, 128, NTM)))
    cols = np.zeros((2, 128, NCOL), f)
    for l in range(2):
        cw = inp["lru_conv_w"][l].reshape(4, 4, 128)[:, 2 * r:2 * r + 2]
        cols[l, :, CO["lru_cw"]:CO["lru_cw"] + 8] = cw.transpose(2, 1, 0).reshape(128, 8)
        cols[l, :, CO["lru_cb"]:CO["lru_cb"] + 2] = inp["lru_conv_b"][l].reshape(4, 128)[2 * r:2 * r + 2].T
        for nm in ("lru_ba", "lru_bx"):
            v = inp[nm][l].reshape(2, 4, 128)[:, 2 * r:2 * r + 2]
            cols[l, :, CO[nm]:CO[nm] + 4] = v.transpose(2, 0, 1).reshape(128, 4)
        v = inp["lru_lambda"][l].reshape(2, 4, 128)[:, 2 * r:2 * r + 2]
        cols[l, :, CO["lru_lam"]:CO["lru_lam"] + 4] = v.transpose(2, 0, 1).reshape(128, 4)
        cols[l, :, CO["att_qg"]] = np.tile(inp["att_q_norm_g"][l], 2)
        cols[l, :, CO["att_kg"]] = np.tile(inp["att_k_norm_g"][l], 2)
        hw = inp["hy_conv_w"][l].reshape(3, 3, 4, 128)[:, :, 2 * r:2 * r + 2]
        cols[l, :, CO["hy_cw"]:CO["hy_cw"] + 18] = hw.transpose(3, 1, 2, 0).reshape(128, 18)
        hb = inp["hy_conv_b"][l].reshape(3, 4, 128)[:, 2 * r:2 * r + 2]
        cols[l, :, CO["hy_cb"]:CO["hy_cb"] + 6] = hb.transpose(2, 0, 1).reshape(128, 6)
        cols[l, :64, CO["hy_b1"]] = inp["hy_b1"][l]
        cols[l, :64, CO["hy_b2"]] = inp["hy_b2"][l]
        cols[l, :64, CO["hy_sf"]] = inp["hy_sin_freq"][l]
    d["cols"] = cols
    bd = np.zeros((2, 2, 2, 2, 128, 128), f)
    for gi, nm in enumerate(("lru_wa", "lru_wx")):
        for c in range(2):
            for i in range(2):
                bd[:, :, gi, c, 64 * i:64 * i + 64, 64 * i:64 * i + 64] = inp[nm][:, :, 2 * (2 * r + c) + i]
    d["lru_bd"] = bd
    d["ml_g_rep"] = np.ascontiguousarray(np.broadcast_to(inp["ml_norm_g"][:, None, HW * r:HW * r + HW], (2, 128, HW)))
    w3 = inp["hy_w3"].reshape(2, 64, 4, 512)[:, :, :, HW * r:HW * r + HW]
    d["hy_w3"] = np.ascontiguousarray(w3.reshape(2, 64, 4 * HW))
    dec = inp["hy_decay"].reshape(2, 4, 512)[:, :, HW * r:HW * r + HW].reshape(2, 1, 4 * HW)
    d["hy_decay_rep"] = np.ascontiguousarray(np.broadcast_to(dec, (2, 128, 4 * HW)))
    sk = inp["hy_skip"][:, :, HW * r:HW * r + HW].reshape(2, 1, 2 * HW)
    d["hy_skip_rep"] = np.ascontiguousarray(np.broadcast_to(sk, (2, 128, 2 * HW)))
    rows = np.concatenate([np.arange(512 * g + HW * r, 512 * g + HW * r + HW) for g in range(4)])
    d["w_out_h"] = np.ascontiguousarray(inp["w_out"][:, rows, :])
    return d


def host_inputs(inp, core, shared, halves):
    b, r = core // 2, core % 2
    d = dict(shared)
    d.update(halves[r])
    d["x"] = np.ascontiguousarray(inp["x"][b])
    d["x_half"] = np.ascontiguousarray(inp["x"][b, r * SH:(r + 1) * SH])
    d["c_col"] = np.ascontiguousarray(inp["c"][b].reshape(16, 128).T)
    return d


IN_SPECS = {
    "x": ([S, D], F32), "x_half": ([SH, D], F32), "c_col": ([128, 16], F32), "ada_w": ([2, D, 6 * D], F32),
    "ada_b_col": ([2, 128, 96], F32), "ada_b_rep": ([2, 128, 6 * D], F32),
    "gmix_col": ([2, 128, 16], F32), "gffn_col": ([2, 128, 16], F32),
    "w_in_h": ([2, D, NCH], F32), "b_fm": ([2, 128, NFM], F32), "b_tm": ([2, 128, NTM], F32),
    "ident": ([128, 128], F32),
    "cols": ([2, 128, NCOL], F32), "lru_bd": ([2, 2, 2, 2, 128, 128], F32),
    "rope_cos": ([128, S], F32), "rope_sin": ([128, S], F32), "rope_PT": ([128, 128], F32), "bd64": ([128, 128], F32),
    "w_out_h": ([2, D // 2, D], F32), "ffn_w1": ([2, D, DFF], F32), "ffn_w3": ([2, D, DFF], F32), "ffn_w2": ([2, DFF, D], F32),
    "final_g_rep": ([128, D], F32),
    "dft_c": ([NF, 128, NF, 128], BF16), "dft_s": ([NF, 128, NF, 128], BF16), "dft_wf": ([128, NF], F32),
    "hy_feat": ([17, S], F32), "hy_negt": ([128, 32], F32), "hy_w1": ([2, 17, 64], F32), "hy_w2": ([2, 64, 64], F32),
    "hy_w3": ([2, 64, 4 * HW], F32), "hy_decay_rep": ([2, 128, 4 * HW], F32), "hy_skip_rep": ([2, 128, 2 * HW], F32),
    "mlc": ([4, 128, 128], F32), "ml_g_rep": ([2, 128, HW], F32),
}


def build(stages, dbg=()):
    nc = bass.Bass("TRN2", target_bir_lowering=False)
    kb = KB(nc)
    cx = Ctx()
    cx.nc, cx.kb = nc, kb
    cx.I = {k: nc.dram_tensor(k, sh, dt, kind="ExternalInput").ap() for k, (sh, dt) in IN_SPECS.items()}

    def scratch(name, shape, dt):
        kind = "ExternalOutput" if name in dbg else "Internal"
        return nc.dram_tensor(name, shape, dt, kind=kind).ap()

    cx.hT_d = scratch("hT_d", [16, 128, S], BF16)
    cx.pfm_d = scratch("pfm_d", [NFM, 128, S], F32)
    cx.tm_d = {nm: scratch("tm_" + nm, [S, n], F32) for nm, lo, n in TM}
    cx.bhT, cx.bpfm = kb.buf(), kb.buf()
    cx.yT_d = scratch("yT_d", [8, 128, S], BF16)
    cx.byT = kb.buf()
    cx.ztok_d = scratch("ztok_d", [S, 3 * HW], F32)
    cx.z1_d = scratch("z1_d", [S, HW], F32)
    cx.G_d = scratch("G_d", [2, S, 2 * HW], BF16)
    cx.H_d = scratch("H_d", [2, NF, 128, 2 * HW], F32)
    cx.bztok, cx.bz1, cx.bG_d, cx.bH_d = kb.buf(), kb.buf(), kb.buf(), kb.buf()
    cx.xa_d = scratch("xa_d", [SH, D], F32)
    cx.xb_d = scratch("xb_d", [SH, D], F32)
    cx.part_d = scratch("part_d", [S, D], F32)
    cx.sum_d = scratch("sum_d", [SH, D], F32)
    cx.xfull_d = scratch("xfull_d", [S, D], F32)
    cx.h2T_d = scratch("h2T_d", [16, 128, SH], BF16)
    cx.bxa, cx.bxb, cx.bpart, cx.bsum, cx.bxfull, cx.bh2T = [kb.buf() for _ in range(6)]
    cx.out = nc.dram_tensor("out", [SH, D], F32, kind="ExternalOutput").ap()
    cx.bout = kb.buf()
    cx.btm = {nm: kb.buf() for nm, _, _ in TM}
    cx.dbg_out = scratch("dbg_out", [128, 512], F32) if "dbg_out" in dbg else None
    finals = []
    with ExitStack() as es:
        ident = es.enter_context(nc.sbuf_tensor(U("ident"), [128, 128], F32))
        cact = es.enter_context(nc.sbuf_tensor(U("cact"), [128, 16], F32))
        ones = es.enter_context(nc.sbuf_tensor(U("ones"), [128, 128], F32))
        gcols = es.enter_context(nc.sbuf_tensor(U("gcols"), [128, 64], F32))
        cx.cols = es.enter_context(nc.sbuf_tensor(U("cols"), [128, NCOL], F32))
        cx.bcols = kb.buf()
        cx.epsc = es.enter_context(nc.sbuf_tensor(U("epsc"), [128, 2], F32))
        cx.ccsem = nc.alloc_semaphore(U("ccsem"))
        cx.ccdummy = es.enter_context(nc.sbuf_tensor(U("ccd"), [128, 2], F32))
        cx.bccd = kb.buf()
        modT0 = es.enter_context(nc.sbuf_tensor(U("modT0"), [128, 96], F32))
        AB0 = es.enter_context(nc.sbuf_tensor(U("AB0"), [128, 64], F32))
        g1b0 = es.enter_context(nc.sbuf_tensor(U("g1b0"), [128, D], F32))
        g2b0 = es.enter_context(nc.sbuf_tensor(U("g2b0"), [128, D], F32))
        ps0 = es.enter_context(nc.psum_tensor(U("ps0"), [128, 512], F32))
        ps1 = es.enter_context(nc.psum_tensor(U("ps1"), [128, 512], F32))
        ps2 = es.enter_context(nc.psum_tensor(U("ps2"), [128, 512], F32))
        ps3 = es.enter_context(nc.psum_tensor(U("ps3"), [128, 512], F32))
        ps4 = es.enter_context(nc.psum_tensor(U("ps4"), [128, 512], F32))
        ps5 = es.enter_context(nc.psum_tensor(U("ps5"), [128, 512], F32))
        ps6 = es.enter_context(nc.psum_tensor(U("ps6"), [128, 512], F32))
        ps7 = es.enter_context(nc.psum_tensor(U("ps7"), [128, 512], F32))
        cx.ps = [ps0, ps1, ps2, ps3, ps4, ps5, ps6, ps7]
        cx.pb = [kb.buf() for _ in range(8)]
        cx.ident, cx.cact, cx.ones = ident, cact, ones
        cx.bconst, cx.bcact, cx.bcrep = kb.buf(), kb.buf(), kb.buf()
        _b = kb.buf()
        cx.modT, cx.bmod = [modT0, modT0], [_b, _b]
        _b = kb.buf()
        cx.AB, cx.bAB = [AB0, AB0], [_b, _b]
        cx.gb = [[g1b0, g2b0], [g1b0, g2b0]]
        _b = [kb.buf(), kb.buf()]
        cx.bgb = [_b, _b]
        cx.gmix = [gcols[:, 0:16], gcols[:, 16:32]]
        cx.gffn = [gcols[:, 32:48], gcols[:, 48:64]]
        kb.dma("sp", ident[:], cx.I["ident"], [], [cx.bconst])
        for l in range(2):
            kb.dma("sp", gcols[:, 16 * l:16 * l + 16], cx.I["gmix_col"][l], [], [cx.bconst])
            kb.dma("sp", gcols[:, 32 + 16 * l:48 + 16 * l], cx.I["gffn_col"][l], [], [cx.bconst])
        kb.dma("sp", cact[:], cx.I["c_col"], [], [cx.bcact])
        MEMSET(kb, "dve", ones[:], 1.0, [cx.bconst])
        MEMSET(kb, "dve", cx.epsc[:], EPS, [cx.bconst])
        kb.dma("sp", cx.cols[:], cx.I["cols"][0], [], [cx.bcols])
        ACT(kb, cact[:], cact[:], AF.Silu, [cx.bcact], [cx.bcact])
        if "full" in stages:
            bx0, bxh0 = kb.buf(), kb.buf()
            for l in range(2):
                kb.barrier()
                if l > 0:
                    kb.dma("sp", cx.cols[:], cx.I["cols"][l], [], [cx.bcols])
                phase_ada(cx, l)
                kb.barrier()
                xsrc, bxs = (cx.I["x"], bx0) if l == 0 else (cx.xfull_d, cx.bxfull)
                phase_norm(cx, xsrc, bxs, cx.AB[l][:, 0:16], cx.AB[l][:, 16:32], cx.bAB[l], cx.hT_d, cx.bhT, rowmap=(None if l == 0 else pair_row))
                kb.barrier()
                phase_inproj(cx, l)
                kb.barrier()
                phase_lru(cx, l)
                kb.barrier()
                phase_att(cx, l)
                kb.barrier()
                phase_hyena(cx, l)
                kb.barrier()
                phase_mlstm(cx, l)
                kb.barrier()
                phase_outproj(cx, l)
                collective(cx, "ReduceScatter", ALU.add, cx.part_d, cx.sum_d)
                xh, bxh = (cx.I["x_half"], bxh0) if l == 0 else (cx.xb_d, cx.bxb)
                phase_resid(cx, l, xh, bxh)
                kb.barrier()
                phase_norm(cx, cx.xa_d, cx.bxa, cx.AB[l][:, 32:48], cx.AB[l][:, 48:64], cx.bAB[l], cx.h2T_d, cx.bh2T, ntiles=NTH)
                kb.barrier()
                phase_ffn(cx, l, cx.xa_d, cx.bxa, cx.xb_d, cx.bxb)
                if l == 0:
                    collective(cx, "AllGather", ALU.bypass, cx.xb_d, cx.xfull_d)
            kb.barrier()
            phase_final(cx, cx.xb_d, cx.bxb, cx.out, cx.bout)
            finals.append(cx.bout)
        if "ada" in stages:
            phase_ada(cx, 0)
        kb.barrier()
        if "norm1" in stages:
            phase_norm(cx, cx.I["x"], kb.buf(), cx.AB[0][:, 0:16], cx.AB[0][:, 16:32], cx.bAB[0], cx.hT_d, cx.bhT)
            finals.append(cx.bhT)
        if "inproj" in stages:
            kb.barrier()
            phase_inproj(cx, 0)
            finals += [cx.bpfm] + list(cx.btm.values())
        if "lru" in stages:
            kb.barrier()
            phase_lru(cx, 0)
            finals.append(cx.byT)
        if "att" in stages:
            kb.barrier()
            phase_att(cx, 0)
            finals.append(cx.byT)
        if "hyena" in stages:
            kb.barrier()
            phase_hyena(cx, 0)
            finals.append(cx.byT)
        if "mlstm" in stages:
            kb.barrier()
            phase_mlstm(cx, 0)
            finals.append(cx.byT)
        if "dbgmod" in dbg:
            with nc.sbuf_tensor(U("dbgt"), [128, 512], F32) as dbgt:
                bd, bo = kb.buf(), kb.buf()
                MEMSET(kb, "dve", dbgt[:], 0.0, [bd])
                CP(kb, "dve", dbgt[:, 0:96], modT0[:], [cx.bmod[0]], [bd])
                CP(kb, "dve", dbgt[:, 96:160], AB0[:], [cx.bAB[0]], [bd])
                CP(kb, "dve", dbgt[:, 160:416], g1b0[:, 0:256], [cx.bgb[0][0]], [bd])
                kb.dma("sp", cx.dbg_out, dbgt[:], [bd], [bo])
                finals.append(bo)
        kb.finish(finals)
    return nc


def phase_outproj(cx, l):
    nc, kb = cx.nc, cx.kb
    wsrc = cx.I["w_out_h"][l].rearrange("(kc p) n -> p kc n", p=128)
    yview = cx.yT_d.rearrange("c p t -> p c t")
    KC = 8
    with ExitStack() as es:
        wt = es.enter_context(nc.sbuf_tensor(U("opw"), [128, KC, D], BF16))
        ybs = [es.enter_context(nc.sbuf_tensor(U("opy"), [128, KC, 512], BF16)) for _ in range(2)]
        xts = [es.enter_context(nc.sbuf_tensor(U("opx"), [128, D], F32)) for _ in range(2)]
        bw = kb.buf()
        byb, bxt = [kb.buf(), kb.buf()], [kb.buf(), kb.buf()]
        for q in range(4):
            kb.dma("pool", wt[:, :, q * 512:(q + 1) * 512], wsrc[:, :, q * 512:(q + 1) * 512], [], [bw])
        no = 0
        for tb in range(8):
            yb, by = ybs[tb % 2], byb[tb % 2]
            kb.dma("sp", yb[:], yview[:, :, tb * 512:(tb + 1) * 512], [cx.byT], [by])
            for tt in range(4):
                ti = tb * 4 + tt
                xt, bx = xts[ti % 2], bxt[ti % 2]
                for db in range(4):
                    p, bp = cx.ps[no % 4], cx.pb[no % 4]
                    no += 1
                    for kc in range(KC):
                        MM(kb, p[:, :], yb[:, kc, tt * 128:(tt + 1) * 128], wt[:, kc, db * 512:(db + 1) * 512], kc == 0, kc == KC - 1, [by, bw], [bp])
                    if db % 2 == 0:
                        CP(kb, "dve", xt[:, db * 512:(db + 1) * 512], p[:, :], [bp], [bx])
                    else:
                        ACT(kb, xt[:, db * 512:(db + 1) * 512], p[:, :], AF.Identity, [bp], [bx])
                r0 = pair_row(ti)
                kb.dma("sp", cx.part_d[r0:r0 + 128, :], xt[:], [bx], [cx.bpart])


def pair_row(ti):
    return (ti % NTH) * 256 + (ti // NTH) * 128


def collective(cx, kind, op, src, dst):
    nc, kb = cx.nc, cx.kb
    kb.barrier()
    for j in range(NTH):
        big = slice(j * 256, (j + 1) * 256)
        small = slice(j * 128, (j + 1) * 128)
        i_ap, o_ap = (src[big, :], dst[small, :]) if kind == "ReduceScatter" else (src[small, :], dst[big, :])
        nc.gpsimd.sem_clear(cx.ccsem)
        nc.gpsimd.collective_compute(kind, op, replica_groups=[[0, 1], [2, 3], [4, 5], [6, 7]],
                                     ins=[i_ap], outs=[o_ap]).then_inc(cx.ccsem)
        nc.gpsimd.wait_ge(cx.ccsem, 1)
    MEMSET(kb, "pool", cx.ccdummy[:], 0.0, [cx.bccd])
    kb.barrier()


def phase_resid(cx, l, xsrc, bxsrc):
    nc, kb = cx.nc, cx.kb
    with ExitStack() as es:
        xs = [es.enter_context(nc.sbuf_tensor(U("rsx"), [128, D], F32)) for _ in range(2)]
        ss = [es.enter_context(nc.sbuf_tensor(U("rss"), [128, D], F32)) for _ in range(2)]
        bxs, bss = [kb.buf(), kb.buf()], [kb.buf(), kb.buf()]
        for ti in range(NTH):
            u = ti % 2
            kb.dma("sp", xs[u][:], xsrc[ti * 128:(ti + 1) * 128, :], [bxsrc], [bxs[u]])
            kb.dma("sp", ss[u][:], cx.sum_d[ti * 128:(ti + 1) * 128, :], [cx.bsum], [bss[u]])
            TT(kb, "dve", ss[u][:], ss[u][:], cx.gb[l][0][:], ALU.mult, [bss[u], cx.bgb[l][0]], [bss[u]])
            TT(kb, "pool", xs[u][:], xs[u][:], ss[u][:], ALU.add, [bxs[u], bss[u]], [bxs[u]])
            kb.dma("sp", cx.xa_d[ti * 128:(ti + 1) * 128, :], xs[u][:], [bxs[u]], [cx.bxa])


def phase_ffn(cx, l, xsrc, bxsrc, xdst, bxdst):
    nc, kb = cx.nc, cx.kb
    w1s = cx.I["ffn_w1"][l].rearrange("(kc p) n -> p kc n", p=128)
    w3s = cx.I["ffn_w3"][l].rearrange("(kc p) n -> p kc n", p=128)
    w2s = cx.I["ffn_w2"][l].rearrange("(fc p) n -> p fc n", p=128)
    hview = cx.h2T_d.rearrange("c p t -> p c t")
    xs4 = xsrc.rearrange("(n p) d -> p n d", p=128)
    xd4 = xdst.rearrange("(n p) d -> p n d", p=128)
    NG = DFF // 256
    with ExitStack() as es:
        h0 = es.enter_context(nc.sbuf_tensor(U("fh0"), [128, 16, 512], BF16))
        h1 = es.enter_context(nc.sbuf_tensor(U("fh1"), [128, 16, 512], BF16))
        uT = es.enter_context(nc.sbuf_tensor(U("fu"), [128, 44, 512], BF16))
        wa = [es.enter_context(nc.sbuf_tensor(U("fw1"), [128, 16, 256], BF16)) for _ in range(2)]
        wb = [es.enter_context(nc.sbuf_tensor(U("fw3"), [128, 16, 256], BF16)) for _ in range(2)]
        w2t = [es.enter_context(nc.sbuf_tensor(U("fw2"), [128, 44, 128], BF16)) for _ in range(2)]
        xr = es.enter_context(nc.sbuf_tensor(U("fx"), [128, 4, D], F32))
        tmp = [es.enter_context(nc.sbuf_tensor(U("ft"), [128, 512], F32)) for _ in range(2)]
        hs, bhs = [h0, h1], [kb.buf(), kb.buf()]
        bu = kb.buf()
        bwa, bwb, bw2 = [kb.buf(), kb.buf()], [kb.buf(), kb.buf()], [kb.buf(), kb.buf()]
        bxr = kb.buf()
        btmp = [kb.buf(), kb.buf()]
        no = 0
        nw = 0
        nw2 = 0
        for tb in range(SH // 512):
            h, bh = hs[tb % 2], bhs[tb % 2]
            kb.dma("sp", h[:], hview[:, :, tb * 512:(tb + 1) * 512], [cx.bh2T], [bh])
            kb.dma("sp", xr[:], xs4[:, tb * 4:(tb + 1) * 4, :], [bxsrc], [bxr])
            for g in range(NG):
                a, ba = wa[nw % 2], bwa[nw % 2]
                b3, bb = wb[nw % 2], bwb[nw % 2]
                nw += 1
                kb.dma("pool", a[:], w1s[:, :, g * 256:(g + 1) * 256], [], [ba])
                kb.dma("pool", b3[:], w3s[:, :, g * 256:(g + 1) * 256], [], [bb])
                for cc in range(2):
                    fc = 2 * g + cc
                    pa, bpa = cx.ps[(no * 2) % 6], cx.pb[(no * 2) % 6]
                    pb_, bpb = cx.ps[(no * 2 + 1) % 6], cx.pb[(no * 2 + 1) % 6]
                    t, bt = tmp[no % 2], btmp[no % 2]
                    no += 1
                    for kc in range(16):
                        MM(kb, pa[:, :], a[:, kc, cc * 128:(cc + 1) * 128], h[:, kc, :], kc == 0, kc == 15, [ba, bh], [bpa])
                    for kc in range(16):
                        MM(kb, pb_[:, :], b3[:, kc, cc * 128:(cc + 1) * 128], h[:, kc, :], kc == 0, kc == 15, [bb, bh], [bpb])
                    ACT(kb, t[:], pa[:, :], AF.Silu, [bpa], [bt])
                    TT(kb, "dve", uT[:, fc, :], t[:], pb_[:, :], ALU.mult, [bt, bpb], [bu])
            for dq in range(16):
                w2, b2 = w2t[nw2 % 2], bw2[nw2 % 2]
                nw2 += 1
                kb.dma("pool", w2[:], w2s[:, :, dq * 128:(dq + 1) * 128], [], [b2])
                for tt in range(4):
                    p, bp = cx.ps[6 + (no % 2)], cx.pb[6 + (no % 2)]
                    t, bt = tmp[no % 2], btmp[no % 2]
                    no += 1
                    for fc in range(44):
                        MM(kb, p[:, 0:128], uT[:, fc, tt * 128:(tt + 1) * 128], w2[:, fc, :], fc == 0, fc == 43, [bu, b2], [bp])
                    TT(kb, "dve", t[:, 0:128], p[:, 0:128], cx.gb[l][1][:, dq * 128:(dq + 1) * 128], ALU.mult, [bp, cx.bgb[l][1]], [bt])
                    TT(kb, "pool", xr[:, tt, dq * 128:(dq + 1) * 128], xr[:, tt, dq * 128:(dq + 1) * 128], t[:, 0:128], ALU.add, [bt, bxr], [bxr])
            kb.dma("sp", xd4[:, tb * 4:(tb + 1) * 4, :], xr[:], [bxr], [bxdst])


def phase_final(cx, xsrc, bxsrc, out, bout):
    nc, kb = cx.nc, cx.kb
    with ExitStack() as es:
        xs = [es.enter_context(nc.sbuf_tensor(U("fnx"), [128, D], F32)) for _ in range(2)]
        sq = [es.enter_context(nc.sbuf_tensor(U("fnq"), [128, D], F32)) for _ in range(2)]
        fg = es.enter_context(nc.sbuf_tensor(U("fng"), [128, D], F32))
        ss = es.enter_context(nc.sbuf_tensor(U("fns"), [128, 8], F32))
        bxs, bsq, bss = [kb.buf(), kb.buf()], [kb.buf(), kb.buf()], [kb.buf(), kb.buf()]
        bfg = kb.buf()
        kb.dma("sp", fg[:], cx.I["final_g_rep"], [], [bfg])
        for ti in range(NTH):
            u = ti % 2
            c0 = 4 * u
            kb.dma("sp", xs[u][:], xsrc[ti * 128:(ti + 1) * 128, :], [bxsrc], [bxs[u]])
            ACT(kb, sq[u][:], xs[u][:], AF.Square, [bxs[u]], [bsq[u]])
            kb.op("dve", lambda e: e.tensor_reduce(out=ss[:, c0:c0 + 1], in_=sq[u][:], axis=AX.X, op=ALU.add), [bsq[u]], [bss[u]])
            TS(kb, "dve", ss[:, c0 + 1:c0 + 2], ss[:, c0:c0 + 1], 1.0 / D, EPS, ALU.mult, ALU.add, [bss[u]], [bss[u]])
            ACT(kb, ss[:, c0 + 2:c0 + 3], ss[:, c0 + 1:c0 + 2], AF.Sqrt, [bss[u]], [bss[u]])
            RECIP(kb, ss[:, c0 + 3:c0 + 4], ss[:, c0 + 2:c0 + 3], [bss[u]], [bss[u]])
            STT(kb, sq[u][:], xs[u][:], ss[:, c0 + 3:c0 + 4], fg[:], ALU.mult, ALU.mult, [bxs[u], bss[u], bfg], [bsq[u]])
            kb.dma("sp", out[ti * 128:(ti + 1) * 128, :], sq[u][:], [bsq[u]], [bout])


def phase_lru(cx, l):
    nc, kb = cx.nc, cx.kb
    C = cx.cols
    bC = cx.bcols
    with ExitStack() as es:
        T = {nm: es.enter_context(nc.sbuf_tensor(U("lr" + nm), [128, S], F32)) for nm in ("ax", "ag", "xc", "r", "i", "t", "h", "hs")}
        B = {nm: kb.buf() for nm in T}
        ys = es.enter_context(nc.sbuf_tensor(U("lry"), [128, S], BF16))
        bys = kb.buf()
        wa = es.enter_context(nc.sbuf_tensor(U("lrwa"), [128, 128], F32))
        wx = es.enter_context(nc.sbuf_tensor(U("lrwx"), [128, 128], F32))
        cl = es.enter_context(nc.sbuf_tensor(U("lrcl"), [128, 4], F32))
        bwa, bwx, bcl = kb.buf(), kb.buf(), kb.buf()
        lam = C[:, CO["lru_lam"]:CO["lru_lam"] + 4]
        ACT(kb, cl[:], lam, AF.Exp, [bC], [bcl], scale=-1.0)
        ACT(kb, cl[:], cl[:], AF.Ln, [bcl], [bcl], bias=1.0)
        TS(kb, "dve", cl[:], cl[:], -8.0, None, ALU.mult, None, [bcl], [bcl])
        no = 0
        for c in range(NCK):
            ax, ag, xc, r, ii, t, h, hs = (T[k] for k in ("ax", "ag", "xc", "r", "i", "t", "h", "hs"))
            kb.dma("sp", ax[:], cx.pfm_d[c], [cx.bpfm], [B["ax"]])
            kb.dma("sp", ag[:], cx.pfm_d[NCK + c], [cx.bpfm], [B["ag"]])
            cw = lambda j: C[:, CO["lru_cw"] + 4 * c + j:CO["lru_cw"] + 4 * c + j + 1]
            ACT(kb, xc[:], ax[:], AF.Identity, [B["ax"], bC], [B["xc"]], scale=cw(2), bias=C[:, CO["lru_cb"] + c:CO["lru_cb"] + c + 1])
            STT(kb, xc[:, 2:S], ax[:, 0:S - 2], cw(0), xc[:, 2:S], ALU.mult, ALU.add, [B["ax"], B["xc"], bC], [B["xc"]])
            STT(kb, xc[:, 1:S], ax[:, 0:S - 1], cw(1), xc[:, 1:S], ALU.mult, ALU.add, [B["ax"], B["xc"], bC], [B["xc"]])
            STT(kb, xc[:, 0:S - 1], ax[:, 1:S], cw(3), xc[:, 0:S - 1], ALU.mult, ALU.add, [B["ax"], B["xc"], bC], [B["xc"]])
            for dr in range(2):
                kb.dma("sp", wa[:], cx.I["lru_bd"][l, dr, 0, c], [], [bwa])
                kb.dma("sp", wx[:], cx.I["lru_bd"][l, dr, 1, c], [], [bwx])
                ba = C[:, CO["lru_ba"] + 2 * dr + c:CO["lru_ba"] + 2 * dr + c + 1]
                bx = C[:, CO["lru_bx"] + 2 * dr + c:CO["lru_bx"] + 2 * dr + c + 1]
                for tb in range(8):
                    sl = slice(tb * 512, (tb + 1) * 512)
                    p1, bp1 = cx.ps[no % 4], cx.pb[no % 4]
                    p2, bp2 = cx.ps[(no + 1) % 4], cx.pb[(no + 1) % 4]
                    no += 2
                    MM(kb, p1[:, :], wa[:], xc[:, sl], True, True, [bwa, B["xc"]], [bp1])
                    MM(kb, p2[:, :], wx[:], xc[:, sl], True, True, [bwx, B["xc"]], [bp2])
                    ACT(kb, r[:, sl], p1[:, :], AF.Sigmoid, [bp1, bC], [B["r"]], bias=ba)
                    ACT(kb, ii[:, sl], p2[:, :], AF.Sigmoid, [bp2, bC], [B["i"]], bias=bx)
                ACT(kb, r[:], r[:], AF.Exp, [B["r"], bcl], [B["r"]], scale=cl[:, 2 * dr + c:2 * dr + c + 1])
                TT(kb, "dve", t[:], r[:], r[:], ALU.mult, [B["r"]], [B["t"]])
                TS(kb, "dve", t[:], t[:], -1.0, 1.0, ALU.mult, ALU.add, [B["t"]], [B["t"]])
                TS(kb, "dve", t[:], t[:], 0.0, None, ALU.max, None, [B["t"]], [B["t"]])
                ACT(kb, t[:], t[:], AF.Sqrt, [B["t"]], [B["t"]])
                TT(kb, "dve", t[:], t[:], ii[:], ALU.mult, [B["t"], B["i"]], [B["t"]])
                TT(kb, "dve", t[:], t[:], xc[:], ALU.mult, [B["t"], B["xc"]], [B["t"]])
                dst = hs if dr == 0 else h
                bdst = B["hs"] if dr == 0 else B["h"]
                if dr == 0:
                    kb.op("dve", lambda e: e.tensor_tensor_scan(out=dst[:], data0=r[:], data1=t[:], initial=0.0, op0=ALU.mult, op1=ALU.add), [B["r"], B["t"]], [bdst])
                else:
                    kb.op("dve", lambda e: e.tensor_tensor_scan(out=dst[:, ::-1], data0=r[:, ::-1], data1=t[:, ::-1], initial=0.0, op0=ALU.mult, op1=ALU.add), [B["r"], B["t"]], [bdst])
                    TT(kb, "dve", hs[:], hs[:], h[:], ALU.add, [B["hs"], B["h"]], [B["hs"]])
            ACT(kb, t[:], ag[:], AF.Square, [B["ag"]], [B["t"]])
            TS(kb, "dve", t[:], t[:], 0.044715, 1.0, ALU.mult, ALU.add, [B["t"]], [B["t"]])
            TT(kb, "dve", t[:], t[:], ag[:], ALU.mult, [B["t"], B["ag"]], [B["t"]])
            ACT(kb, t[:], t[:], AF.Sigmoid, [B["t"]], [B["t"]], scale=1.5957691216057308)
            TT(kb, "dve", t[:], t[:], ag[:], ALU.mult, [B["t"], B["ag"]], [B["t"]])
            TT(kb, "dve", ys[:], t[:], hs[:], ALU.mult, [B["t"], B["hs"]], [bys])
            kb.dma("sp", cx.yT_d[c], ys[:], [bys], [cx.byT])


def phase_att(cx, l):
    nc, kb = cx.nc, cx.kb
    C, bC = cx.cols, cx.bcols
    with ExitStack() as es:
        cos = es.enter_context(nc.sbuf_tensor(U("atc"), [128, S], F32))
        sin = es.enter_context(nc.sbuf_tensor(U("ats"), [128, S], F32))
        PT = es.enter_context(nc.sbuf_tensor(U("atP"), [128, 128], F32))
        BD = es.enter_context(nc.sbuf_tensor(U("atB"), [128, 128], F32))
        qk = [es.enter_context(nc.sbuf_tensor(U("atq"), [128, S], BF16)) for _ in range(3)]
        bqk = [kb.buf() for _ in range(3)]
        bk_ = kb.buf()
        kb.dma("sp", cos[:], cx.I["rope_cos"], [], [bk_])
        kb.dma("sp", sin[:], cx.I["rope_sin"], [], [bk_])
        kb.dma("sp", PT[:], cx.I["rope_PT"], [], [bk_])
        kb.dma("sp", BD[:], cx.I["bd64"], [], [bk_])
        no = 0
        with ExitStack() as es2:
            raw = [es2.enter_context(nc.sbuf_tensor(U("atr"), [128, S], F32)) for _ in range(2)]
            braw = [kb.buf(), kb.buf()]
            tq = [es2.enter_context(nc.sbuf_tensor(U("att"), [128, 512], F32)) for _ in range(4)]
            btq = [kb.buf() for _ in range(4)]
            for ci in range(3):
                ch = 4 + ci
                rw, brw = raw[ci % 2], braw[ci % 2]
                kb.dma("sp", rw[:], cx.pfm_d[ch], [cx.bpfm], [brw])
                gcol = C[:, CO["att_qg"]:CO["att_qg"] + 1] if ci < 2 else C[:, CO["att_kg"]:CO["att_kg"] + 1]
                qs = 0.125 if ci < 2 else 1.0
                for tb in range(8):
                    sl = slice(tb * 512, (tb + 1) * 512)
                    p1, bp1 = cx.ps[no % 4], cx.pb[no % 4]
                    p2, bp2 = cx.ps[(no + 1) % 4], cx.pb[(no + 1) % 4]
                    no += 2
                    t0, t1, t2, t3 = tq
                    ACT(kb, t0[:], rw[:, sl], AF.Square, [brw], [btq[0]])
                    MM(kb, p1[:, :], BD[:], t0[:], True, True, [bk_, btq[0]], [bp1])
                    ACT(kb, t1[:], p1[:, :], AF.Sqrt, [bp1], [btq[1]], scale=1.0 / 64.0, bias=cx.epsc[:, 0:1])
                    RECIP(kb, t1[:], t1[:], [btq[1]], [btq[1]])
                    TT(kb, "dve", t2[:], rw[:, sl], t1[:], ALU.mult, [brw, btq[1]], [btq[2]])
                    TS(kb, "dve", t2[:], t2[:], gcol, qs, ALU.mult, ALU.mult, [btq[2], bC], [btq[2]])
                    MM(kb, p2[:, :], PT[:], t2[:], True, True, [bk_, btq[2]], [bp2])
                    TT(kb, "dve", t3[:], p2[:, :], sin[:, sl], ALU.mult, [bp2, bk_], [btq[3]])
                    TT(kb, "pool", t2[:], t2[:], cos[:, sl], ALU.mult, [btq[2], bk_], [btq[2]])
                    TT(kb, "pool", qk[ci][:, sl], t2[:], t3[:], ALU.add, [btq[2], btq[3]], [bqk[ci]])
        kb.barrier()
        with ExitStack() as es3:
            vraw = es3.enter_context(nc.sbuf_tensor(U("atv"), [128, 32, 64], F32))
            va = [es3.enter_context(nc.sbuf_tensor(U("atva"), [128, 32, 128], BF16)) for _ in range(2)]
            eb = [es3.enter_context(nc.sbuf_tensor(U("ate"), [128, 512], BF16)) for _ in range(3)]
            rt = [es3.enter_context(nc.sbuf_tensor(U("atrt"), [128, 512], F32)) for _ in range(2)]
            ys = [es3.enter_context(nc.sbuf_tensor(U("aty"), [128, S], BF16)) for _ in range(2)]
            bv, bva, beb, brt, bys = kb.buf(), [kb.buf(), kb.buf()], [kb.buf() for _ in range(3)], [kb.buf(), kb.buf()], [kb.buf(), kb.buf()]
            kb.dma("sp", vraw[:], cx.tm_d["vtm"].rearrange("(n p) d -> p n d", p=128), [cx.btm["vtm"]], [bv])
            for kv in range(1):
                MEMSET(kb, "pool", va[kv][:], 1.0, [bva[kv]])
                CP(kb, "dve", va[kv][:, :, 0:64], vraw[:, :, kv * 64:(kv + 1) * 64], [bv], [bva[kv]])
            ne = 0
            for hd in range(4):
                c, ph, kv = hd // 2, (hd % 2) * 64, 0
                q_, bq_ = qk[c], bqk[c]
                k_, bk2 = qk[2 + kv], bqk[2 + kv]
                y_, by_ = ys[c % 2], bys[c % 2]
                for qb in range(8):
                    po, bpo = cx.ps[4 + (qb % 2)], cx.pb[4 + (qb % 2)]
                    for kc in range(32):
                        p, bp = cx.ps[ne % 4], cx.pb[ne % 4]
                        e_, be_ = eb[ne % 3], beb[ne % 3]
                        ne += 1
                        MM(kb, p[:, :], k_[ph:ph + 64, kc * 128:(kc + 1) * 128], q_[ph:ph + 64, qb * 512:(qb + 1) * 512], True, True, [bk2, bq_], [bp])
                        ACT(kb, e_[:], p[:, :], AF.Exp, [bp], [be_])
                        MM(kb, po[:, :], va[kv][:, kc, :], e_[:], kc == 0, kc == 31, [bva[kv], be_], [bpo])
                    r_, br_ = rt[qb % 2], brt[qb % 2]
                    RECIP(kb, r_[0:64, :], po[64:128, :], [bpo], [br_])
                    TT(kb, "dve", y_[ph:ph + 64, qb * 512:(qb + 1) * 512], po[0:64, :], r_[0:64, :], ALU.mult, [bpo, br_], [by_])
                if hd % 2 == 1:
                    kb.dma("sp", cx.yT_d[NCK + c], y_[:], [by_], [cx.byT])


def phase_mlstm(cx, l):
    nc, kb = cx.nc, cx.kb
    with ExitStack() as es:
        def sb(name, shape, dt=F32):
            return es.enter_context(nc.sbuf_tensor(U(name), shape, dt))
        mlc = sb("mlc", [128, 4, 128])
        G = sb("mlG", [128, 32, 8])
        LF = sb("mlLF", [128, 32, 8])
        T1 = sb("mlT1", [128, 32, 8])
        BF_, BB_, BT_ = sb("mlBF", [128, 32, 8]), sb("mlBB", [128, 32, 8]), sb("mlBT", [128, 32, 8])
        BIAS = [sb("mlBI", [128, 32, 2]) for _ in range(2)]
        W = [sb("mlW", [128, 32, 2]) for _ in range(2)]
        EB = [sb("mlEB", [128, 32, 2]) for _ in range(2)]
        EBT = [sb("mlEBT", [128, 32, 2]) for _ in range(2)]
        gml = sb("mlg", [128, HW])
        bc, bG, bg = kb.buf(), kb.buf(), kb.buf()
        kb.dma("sp", mlc[:], cx.I["mlc"].rearrange("m p j -> p m j"), [], [bc])
        kb.dma("sp", G[:], cx.tm_d["g"].rearrange("(n p) c -> p n c", p=128), [cx.btm["g"]], [bG])
        kb.dma("sp", gml[:], cx.I["ml_g_rep"][l], [], [bg])
        STT(kb, T1[:], G[:], -1.0, G[:], ALU.mult, ALU.max, [bG], [bG])
        ACT(kb, T1[:], T1[:], AF.Exp, [bG], [bG], scale=-1.0)
        ACT(kb, T1[:], T1[:], AF.Ln, [bG], [bG], bias=1.0)
        TS(kb, "dve", LF[:], G[:], 0.0, None, ALU.min, None, [bG], [bG])
        TT(kb, "dve", LF[:], LF[:], T1[:], ALU.subtract, [bG], [bG])
        LF2 = LF[:].rearrange("p n c -> p (n c)")
        for mi, dst in ((0, BF_), (1, BB_), (None, BT_)):
            p, bp = cx.ps[0], cx.pb[0]
            lhs = mlc[:, mi, :] if mi is not None else cx.ones[:]
            MM(kb, p[:, 0:256], lhs, LF2, True, True, [bc, bG, cx.bconst], [bp])
            CP(kb, "dve", dst[:].rearrange("p n c -> p (n c)"), p[:, 0:256], [bp], [bG])
        for dr in range(2):
            Bx = BF_ if dr == 0 else BB_
            li = G[:, :, 4 * dr:4 * dr + 2]
            b4 = Bx[:, :, 4 * dr + 2:4 * dr + 4]
            bt4 = BT_[:, :, 4 * dr + 2:4 * dr + 4]
            TT(kb, "dve", BIAS[dr][:], li, b4, ALU.subtract, [bG], [bG])
            TT(kb, "dve", W[dr][:], bt4, BIAS[dr][:], ALU.add, [bG], [bG])
            ACT(kb, W[dr][:], W[dr][:], AF.Exp, [bG], [bG])
            ACT(kb, EB[dr][:], b4, AF.Exp, [bG], [bG])
            ACT(kb, EBT[dr][:], bt4, AF.Exp, [bG], [bG])
        raw = sb("mlraw", [128, S])
        qb = sb("mlq", [128, S], BF16)
        kbf = sb("mlk", [128, S], BF16)
        ktok = sb("mlkt", [128, 32, 128])
        vtok = sb("mlvt", [128, 32, 128])
        vaug = sb("mlva", [128, 32, 129], BF16)
        hF, hB = sb("mlhF", [128, 32, 128]), sb("mlhB", [128, 32, 128])
        ys = sb("mlys", [128, S], BF16)
        dg = [sb("mldg", [128, 128]) for _ in range(2)]
        DT = [sb("mlDT", [128, 128]) for _ in range(2)]
        PTt = [sb("mlPT", [128, 128], BF16) for _ in range(2)]
        ins = [sb("mlin", [128, 129]) for _ in range(2)]
        tot = [sb("mltot", [128, 129]) for _ in range(2)]
        den = [sb("mlden", [128, 2]) for _ in range(2)]
        kp = [sb("mlkp", [128, 128], BF16) for _ in range(2)]
        Cst = [sb("mlC", [128, 129]) for _ in range(2)]
        Cbf = [sb("mlCb", [128, 129], BF16) for _ in range(2)]
        rst = sb("mlrs", [128, 64])
        braw, bq, bk, bkt, bvt, bva, bys, brs = [kb.buf() for _ in range(8)]
        bh = [kb.buf(), kb.buf()]
        bdg, bDT, bPT, bin_, btot, bden, bkp = [[kb.buf(), kb.buf()] for _ in range(7)]
        bC, bCb = [kb.buf(), kb.buf()], [kb.buf(), kb.buf()]
        nps = [0]

        def nxt():
            i = nps[0] % 8
            nps[0] += 1
            return cx.ps[i], cx.pb[i]
        for hd in range(2):
            kb.dma("sp", raw[:], cx.pfm_d[13 + hd], [cx.bpfm], [braw])
            ACT(kb, qb[:], raw[:], AF.Identity, [braw], [bq], scale=128.0 ** -0.5)
            kb.dma("sp", raw[:], cx.pfm_d[15 + hd], [cx.bpfm], [braw])
            ACT(kb, kbf[:], raw[:], AF.Identity, [braw], [bk])
            kb.dma("sp", ktok[:], cx.tm_d["dk"].rearrange("(n p) d -> p n d", p=128)[:, :, hd * 128:(hd + 1) * 128], [cx.btm["dk"]], [bkt])
            kb.dma("sp", vtok[:], cx.tm_d["dv"].rearrange("(n p) d -> p n d", p=128)[:, :, hd * 128:(hd + 1) * 128], [cx.btm["dv"]], [bvt])
            MEMSET(kb, "pool", vaug[:], 1.0, [bva])
            CP(kb, "dve", vaug[:, :, 0:128], vtok[:], [bvt], [bva])
            for dr in range(2):
                MEMSET(kb, "dve", Cst[dr][:], 0.0, [bC[dr]])
                MEMSET(kb, "dve", Cbf[dr][:], 0.0, [bCb[dr]])
            for i in range(32):
                for dr in range(2):
                    n = i if dr == 0 else 31 - i
                    u = dr
                    cs = slice(n * 128, (n + 1) * 128)
                    Bx = BF_ if dr == 0 else BB_
                    lfc = 4 * dr + 2 + hd
                    hacc, bhh = (hF, bh[0]) if dr == 0 else (hB, bh[1])
                    TS(kb, "pool", dg[u][:], cx.ident[:], Bx[:, n, lfc:lfc + 1], None, ALU.mult, None, [cx.bconst, bG], [bdg[u]])
                    pB, bpB = nxt()
                    MM(kb, pB[:, 0:128], cx.ones[:], dg[u][:], True, False, [cx.bconst, bdg[u]], [bpB])
                    MM(kb, pB[:, 0:128], cx.ident[:], mlc[:, 2 + dr, :], False, True, [cx.bconst, bc], [bpB])
                    ACT(kb, DT[u][:], pB[:, 0:128], AF.Exp, [bpB, bG], [bDT[u]], bias=BIAS[dr][:, n, hd:hd + 1])
                    pS, bpS = nxt()
                    MM(kb, pS[:, 0:128], kbf[:, cs], qb[:, cs], True, True, [bk, bq], [bpS])
                    TT(kb, "dve", PTt[u][:], pS[:, 0:128], DT[u][:], ALU.mult, [bpS, bDT[u]], [bPT[u]])
                    pI, bpI = nxt()
                    MM(kb, pI[:, 0:129], PTt[u][:], vaug[:, n, :], True, True, [bPT[u], bva], [bpI])
                    pN, bpN = nxt()
                    MM(kb, pN[:, 0:129], qb[:, cs], Cbf[dr][:], True, True, [bq, bCb[dr]], [bpN])
                    ACT(kb, ins[u][:], pN[:, 0:129], AF.Identity, [bpN, bG], [bin_[u]], scale=EB[dr][:, n, hd:hd + 1])
                    TT(kb, "dve", tot[u][:], pI[:, 0:129], ins[u][:], ALU.add, [bpI, bin_[u]], [btot[u]])
                    STT(kb, den[u][:, 0:1], tot[u][:, 128:129], -1.0, tot[u][:, 128:129], ALU.mult, ALU.max, [btot[u]], [bden[u]])
                    TS(kb, "dve", den[u][:, 0:1], den[u][:, 0:1], 1.0, None, ALU.max, None, [bden[u]], [bden[u]])
                    RECIP(kb, den[u][:, 1:2], den[u][:, 0:1], [bden[u]], [bden[u]])
                    TS(kb, "dve", hacc[:, n, :], tot[u][:, 0:128], den[u][:, 1:2], None, ALU.mult, None, [btot[u], bden[u]], [bhh])
                    ACT(kb, kp[u][:], ktok[:, n, :], AF.Identity, [bkt, bG], [bkp[u]], scale=W[dr][:, n, hd:hd + 1])
                    pC, bpC = nxt()
                    MM(kb, pC[:, 0:129], kp[u][:], vaug[:, n, :], True, True, [bkp[u], bva], [bpC])
                    STT(kb, Cst[dr][:], Cst[dr][:], EBT[dr][:, n, hd:hd + 1], pC[:, 0:129], ALU.mult, ALU.add, [bC[dr], bG, bpC], [bC[dr]])
                    CP(kb, "pool", Cbf[dr][:], Cst[dr][:], [bC[dr]], [bCb[dr]])
            TT(kb, "dve", hF[:], hF[:], hB[:], ALU.add, [bh[0], bh[1]], [bh[0]])
            ACT(kb, hB[:], hF[:], AF.Square, [bh[0]], [bh[1]])
            kb.op("dve", lambda e: e.tensor_reduce(out=rst[:, 0:32], in_=hB[:], axis=AX.X, op=ALU.add), [bh[1]], [brs])
            TS(kb, "dve", rst[:, 0:32], rst[:, 0:32], 1.0 / 128.0, EPS, ALU.mult, ALU.add, [brs], [brs])
            ACT(kb, rst[:, 0:32], rst[:, 0:32], AF.Sqrt, [brs], [brs])
            RECIP(kb, rst[:, 32:64], rst[:, 0:32], [brs], [brs])
            kb.dma("sp", vtok[:], cx.tm_d["do"].rearrange("(n p) d -> p n d", p=128)[:, :, hd * 128:(hd + 1) * 128], [cx.btm["do"]], [bvt])
            ACT(kb, vtok[:], vtok[:], AF.Sigmoid, [bvt], [bvt])
            for n in range(32):
                STT(kb, hF[:, n, :], hF[:, n, :], rst[:, 32 + n:33 + n], gml[:, hd * 128:(hd + 1) * 128], ALU.mult, ALU.mult, [bh[0], brs, bg], [bh[0]])
            TT(kb, "dve", hF[:], hF[:], vtok[:], ALU.mult, [bh[0], bvt], [bh[0]])
            for n4 in range(8):
                p, bp = nxt()
                for j in range(4):
                    n = 4 * n4 + j
                    TR(kb, p[:, j * 128:(j + 1) * 128], hF[:, n, :], cx.ident[:], [bh[0], cx.bconst], [bp])
                CP(kb, "dve", ys[:, n4 * 512:(n4 + 1) * 512], p[:, :], [bp], [bys])
            kb.dma("sp", cx.yT_d[6 + hd], ys[:], [bys], [cx.byT])


def phase_hyena(cx, l):
    nc, kb = cx.nc, cx.kb
    C_, bC_ = cx.cols, cx.bcols
    zview = cx.ztok_d.rearrange("(n p) c -> p n c", p=128)
    nps = [0]

    def nxt():
        i = nps[0] % 6
        nps[0] += 1
        return cx.ps[i], cx.pb[i]
    with ExitStack() as es0:
        utok = es0.enter_context(nc.sbuf_tensor(U("hyu"), [128, 32, HW], BF16))
        RS = es0.enter_context(nc.sbuf_tensor(U("hyRS"), [128, 2 * HW], F32))
        wfc = es0.enter_context(nc.sbuf_tensor(U("hywf"), [128, NF], F32))
        skr = es0.enter_context(nc.sbuf_tensor(U("hysk"), [128, 2 * HW], F32))
        bu, bRS, bwf = kb.buf(), kb.buf(), kb.buf()
        kb.dma("sp", wfc[:], cx.I["dft_wf"], [], [bwf])
        kb.dma("sp", skr[:], cx.I["hy_skip_rep"][l], [], [bwf])
        with ExitStack() as es:
            raw = [es.enter_context(nc.sbuf_tensor(U("hyraw"), [128, S], F32)) for _ in range(2)]
            zc = [es.enter_context(nc.sbuf_tensor(U("hyz"), [128, S], F32)) for _ in range(2)]
            st = [es.enter_context(nc.sbuf_tensor(U("hyst"), [128, 32, 128], F32)) for _ in range(2)]
            braw, bz, bst = [kb.buf(), kb.buf()], [kb.buf(), kb.buf()], [kb.buf(), kb.buf()]
            for ch in range(6):
                u = ch % 2
                kb.dma("sp", raw[u][:], cx.pfm_d[7 + ch], [cx.bpfm], [braw[u]])
                cw = lambda j: C_[:, CO["hy_cw"] + 3 * ch + j:CO["hy_cw"] + 3 * ch + j + 1]
                ACT(kb, zc[u][:], raw[u][:], AF.Identity, [braw[u], bC_], [bz[u]], scale=cw(1), bias=C_[:, CO["hy_cb"] + ch:CO["hy_cb"] + ch + 1])
                STT(kb, zc[u][:, 1:S], raw[u][:, 0:S - 1], cw(0), zc[u][:, 1:S], ALU.mult, ALU.add, [braw[u], bz[u], bC_], [bz[u]])
                STT(kb, zc[u][:, 0:S - 1], raw[u][:, 1:S], cw(2), zc[u][:, 0:S - 1], ALU.mult, ALU.add, [braw[u], bz[u], bC_], [bz[u]])
                for n4 in range(8):
                    p, bp = nxt()
                    for j in range(4):
                        n = 4 * n4 + j
                        TR(kb, p[:, j * 128:(j + 1) * 128], zc[u][:, n * 128:(n + 1) * 128], cx.ident[:], [bz[u], cx.bconst], [bp])
                    CP(kb, "dve", st[u][:, 4 * n4:4 * n4 + 4, :], p[:, :].rearrange("p (j c) -> p j c", j=4), [bp], [bst[u]])
                    if ch < 2:
                        CP(kb, "pool", utok[:, 4 * n4:4 * n4 + 4, ch * 128:(ch + 1) * 128], st[u][:, 4 * n4:4 * n4 + 4, :], [bst[u]], [bu])
                kb.dma("sp", zview[:, :, ch * 128:(ch + 1) * 128], st[u][:], [bst[u]], [cx.bztok])
        kb.barrier()
        with ExitStack() as es:
            def sb(name, shape, dt=F32):
                return es.enter_context(nc.sbuf_tensor(U(name), shape, dt))
            feat = sb("hyfe", [128, S])
            h1 = sb("hyh1", [128, S])
            h2 = sb("hyh2", [128, S])
            w1s, w2s, w3s = sb("hyw1", [128, 64]), sb("hyw2", [128, 64]), sb("hyw3", [128, 4 * HW])
            adec = sb("hyad", [128, 4 * HW])
            tcol = sb("hytc", [128, 32])
            sfb = sb("hysfb", [128, 2])
            arg = [sb("hyarg", [128, 512]) for _ in range(2)]
            E = [sb("hyE", [128, 512]) for _ in range(2)]
            hq = [sb("hyhq", [128, 512]) for _ in range(4)]
            sq = [sb("hysq", [128, 512]) for _ in range(2)]
            gst = [sb("hygs", [128, 2, 2 * HW], BF16) for _ in range(2)]
            bk, bh1, bh2, bsfb = kb.buf(), kb.buf(), kb.buf(), kb.buf()
            barg, bE, bsq, bgst = [kb.buf(), kb.buf()], [kb.buf(), kb.buf()], [kb.buf(), kb.buf()], [kb.buf(), kb.buf()]
            bhq = [kb.buf() for _ in range(4)]
            kb.dma("sp", feat[0:17, :], cx.I["hy_feat"], [], [bk])
            kb.dma("sp", w1s[0:17, :], cx.I["hy_w1"][l], [], [bk])
            kb.dma("sp", w2s[0:64, :], cx.I["hy_w2"][l], [], [bk])
            kb.dma("sp", w3s[0:64, :], cx.I["hy_w3"][l], [], [bk])
            kb.dma("sp", adec[:], cx.I["hy_decay_rep"][l], [], [bk])
            kb.dma("sp", tcol[:], cx.I["hy_negt"], [], [bk])
            ACT(kb, adec[:], adec[:], AF.Abs, [bk], [bk])
            sf = C_[0:64, CO["hy_sf"]:CO["hy_sf"] + 1]
            TT(kb, "dve", sfb[0:64, 0:1], C_[0:64, CO["hy_b1"]:CO["hy_b1"] + 1], sf, ALU.mult, [bC_], [bsfb])
            TT(kb, "dve", sfb[0:64, 1:2], C_[0:64, CO["hy_b2"]:CO["hy_b2"] + 1], sf, ALU.mult, [bC_], [bsfb])
            for li, (wsrc, K, src, bsrc, dst, bdst) in enumerate(((w1s, 17, feat, bk, h1, bh1), (w2s, 64, h1, bh1, h2, bh2))):
                for tb in range(8):
                    sl = slice(tb * 512, (tb + 1) * 512)
                    p, bp = nxt()
                    a, ba = arg[tb % 2], barg[tb % 2]
                    MM(kb, p[0:64, :], wsrc[0:K, 0:64], src[0:K, sl], True, True, [bk, bsrc], [bp])
                    ACT(kb, a[0:64, :], p[0:64, :], AF.Identity, [bp, bC_, bsfb], [ba], scale=sf, bias=sfb[0:64, li:li + 1])
                    m_ = E[tb % 2]
                    bm_ = bE[tb % 2]
                    for _rep in range(2):
                        TS(kb, "dve", m_[0:64, :], a[0:64, :], PI, 2 * PI, ALU.is_gt, ALU.mult, [ba], [bm_])
                        TT(kb, "dve", a[0:64, :], a[0:64, :], m_[0:64, :], ALU.subtract, [ba, bm_], [ba])
                        TS(kb, "dve", m_[0:64, :], a[0:64, :], -PI, 2 * PI, ALU.is_lt, ALU.mult, [ba], [bm_])
                        TT(kb, "dve", a[0:64, :], a[0:64, :], m_[0:64, :], ALU.add, [ba, bm_], [ba])
                    ACT(kb, dst[0:64, sl], a[0:64, :], AF.Sin, [ba], [bdst])
            pss = [(cx.ps[6], cx.pb[6]), (cx.ps[7], cx.pb[7])]
            for n in range(32):
                g, bg = gst[n % 2], bgst[n % 2]
                for q in range(4):
                    o, dr = q // 2, q % 2
                    p, bp = nxt()
                    e_, be_ = E[q % 2], bE[q % 2]
                    s_, bs_ = sq[q % 2], bsq[q % 2]
                    MM(kb, p[:, 0:HW], h2[0:64, n * 128:(n + 1) * 128], w3s[0:64, q * HW:(q + 1) * HW], True, True, [bh2, bk], [bp])
                    ACT(kb, e_[:, 0:HW], adec[:, q * HW:(q + 1) * HW], AF.Exp, [bk], [be_], scale=tcol[:, n:n + 1])
                    TT(kb, "dve", hq[q][:, 0:HW], p[:, 0:HW], e_[:, 0:HW], ALU.mult, [bp, be_], [bhq[q]])
                    ACT(kb, s_[:, 0:HW], hq[q][:, 0:HW], AF.Square, [bhq[q]], [bs_])
                    MM(kb, pss[o][0][:, 0:HW], cx.ones[:], s_[:, 0:HW], (n == 0 and dr == 0), (n == 31 and dr == 1), [cx.bconst, bs_], [pss[o][1]])
                    if n == 0 and dr == 1:
                        MEMSET(kb, "dve", hq[q][0:1, 0:HW], 0.0, [bhq[q]])
                for o in range(2):
                    TT(kb, "pool", g[:, 0, o * HW:(o + 1) * HW], hq[2 * o][:, 0:HW], hq[2 * o + 1][:, 0:HW], ALU.add, [bhq[2 * o], bhq[2 * o + 1]], [bg])
                    TT(kb, "pool", g[:, 1, o * HW:(o + 1) * HW], hq[2 * o + 1][:, 0:HW], hq[2 * o][:, 0:HW], ALU.subtract, [bhq[2 * o], bhq[2 * o + 1]], [bg])
                kb.dma("sp", cx.G_d.rearrange("r t c -> t r c")[n * 128:(n + 1) * 128], g[:], [bg], [cx.bG_d])
            for o in range(2):
                ACT(kb, RS[:, o * HW:(o + 1) * HW], pss[o][0][:, 0:HW], AF.Sqrt, [pss[o][1]], [bRS], bias=cx.epsc[:, 0:1])
            RECIP(kb, RS[:], RS[:], [bRS], [bRS])
        kb.barrier()
        with ExitStack() as es:
            Gs = es.enter_context(nc.sbuf_tensor(U("hyG"), [128, 32, 2 * HW], BF16))
            ct = [es.enter_context(nc.sbuf_tensor(U("hyct"), [128, NF, 128], BF16)) for _ in range(2)]
            ho = [es.enter_context(nc.sbuf_tensor(U("hyho"), [128, 2 * HW], F32)) for _ in range(2)]
            bGs, bct, bho = kb.buf(), [kb.buf(), kb.buf()], [kb.buf(), kb.buf()]
            k = 0
            for ri, blk in enumerate(("dft_c", "dft_s")):
                kb.dma("sp", Gs[:], cx.G_d[ri].rearrange("(n p) c -> p n c", p=128), [cx.bG_d], [bGs])
                for fc in range(NF):
                    c_, bc_ = ct[k % 2], bct[k % 2]
                    h_, bh_ = ho[k % 2], bho[k % 2]
                    k += 1
                    kb.dma("sp", c_[:], cx.I[blk][fc], [], [bc_])
                    for o in range(2):
                        p, bp = nxt()
                        for dc in range(32):
                            MM(kb, p[:, 0:HW], c_[:, dc, :], Gs[:, dc, o * HW:(o + 1) * HW], dc == 0, dc == 31, [bc_, bGs], [bp])
                        STT(kb, h_[:, o * HW:(o + 1) * HW], p[:, 0:HW], wfc[:, fc:fc + 1], RS[:, o * HW:(o + 1) * HW], ALU.mult, ALU.mult, [bp, bwf, bRS], [bh_])
                    kb.dma("sp", cx.H_d[ri, fc], h_[:], [bh_], [cx.bH_d])
        kb.barrier()
        with ExitStack() as es:
            def sb(name, shape, dt=F32):
                return es.enter_context(nc.sbuf_tensor(U(name), shape, dt))
            YA = sb("hyYA", [128, NF, HW], BF16)
            YB = sb("hyYB", [128, NF, HW], BF16)
            ct = [sb("hyc2", [128, NF, 128], BF16) for _ in range(2)]
            stt = [sb("hys2", [128, NF, 128], BF16) for _ in range(2)]
            Hr = [sb("hyHr", [128, HW]) for _ in range(2)]
            Hi = [sb("hyHi", [128, HW]) for _ in range(2)]
            tt = [sb("hyt", [128, HW]) for _ in range(4)]
            uu = [sb("hyuu", [128, HW]) for _ in range(2)]
            xx = [sb("hyxx", [128, HW]) for _ in range(2)]
            zz = [sb("hyzz", [128, HW]) for _ in range(2)]
            yst = [sb("hyys", [128, 2, 128], BF16) for _ in range(2)]
            bYA, bYB = kb.buf(), kb.buf()
            bct, bstt, bHr, bHi, buu, bxx, bzz, byst = [[kb.buf(), kb.buf()] for _ in range(8)]
            btt = [kb.buf() for _ in range(4)]
            k = 0
            for o in range(2):
                for fc in range(NF):
                    c_, bc_ = ct[k % 2], bct[k % 2]
                    s_, bs_ = stt[k % 2], bstt[k % 2]
                    hr, bhr = Hr[k % 2], bHr[k % 2]
                    hi, bhi = Hi[k % 2], bHi[k % 2]
                    k += 1
                    kb.dma("sp", c_[:], cx.I["dft_c"][fc], [], [bc_])
                    kb.dma("sp", s_[:], cx.I["dft_s"][fc], [], [bs_])
                    kb.dma("sp", hr[:], cx.H_d[0, fc, :, o * HW:(o + 1) * HW], [cx.bH_d], [bhr])
                    kb.dma("sp", hi[:], cx.H_d[1, fc, :, o * HW:(o + 1) * HW], [cx.bH_d], [bhi])
                    pA, bpA = nxt()
                    pB, bpB = nxt()
                    for dc in range(32):
                        MM(kb, pA[:, 0:HW], c_[:, dc, :], utok[:, dc, :], dc == 0, dc == 31, [bc_, bu], [bpA])
                    for dc in range(32):
                        MM(kb, pB[:, 0:HW], s_[:, dc, :], utok[:, dc, :], dc == 0, dc == 31, [bs_, bu], [bpB])
                    TT(kb, "dve", tt[0][:], pA[:, 0:HW], hr[:], ALU.mult, [bpA, bhr], [btt[0]])
                    TT(kb, "dve", tt[1][:], pB[:, 0:HW], hi[:], ALU.mult, [bpB, bhi], [btt[1]])
                    TT(kb, "dve", tt[2][:], pB[:, 0:HW], hr[:], ALU.mult, [bpB, bhr], [btt[2]])
                    TT(kb, "dve", tt[3][:], pA[:, 0:HW], hi[:], ALU.mult, [bpA, bhi], [btt[3]])
                    TT(kb, "pool", YA[:, fc, :], tt[0][:], tt[1][:], ALU.add, [btt[0], btt[1]], [bYA])
                    TT(kb, "pool", YB[:, fc, :], tt[2][:], tt[3][:], ALU.subtract, [btt[2], btt[3]], [bYB])
                for n in range(32):
                    c_, bc_ = ct[k % 2], bct[k % 2]
                    s_, bs_ = stt[k % 2], bstt[k % 2]
                    u_, bu_ = uu[k % 2], buu[k % 2]
                    x_, bx_ = xx[k % 2], bxx[k % 2]
                    z_, bz_ = zz[k % 2], bzz[k % 2]
                    k += 1
                    kb.dma("sp", c_[:], cx.I["dft_c"][n], [], [bc_])
                    kb.dma("sp", s_[:], cx.I["dft_s"][n], [], [bs_])
                    if o == 0:
                        kb.dma("sp", u_[:], cx.ztok_d[n * 128:(n + 1) * 128, 0:HW], [cx.bztok], [bu_])
                    else:
                        kb.dma("sp", u_[:], cx.z1_d[n * 128:(n + 1) * 128, :], [cx.bz1], [bu_])
                    kb.dma("sp", x_[:], cx.ztok_d[n * 128:(n + 1) * 128, HW * (o + 1):HW * (o + 2)], [cx.bztok], [bx_])
                    p, bp = nxt()
                    for fc in range(NF):
                        MM(kb, p[:, 0:HW], c_[:, fc, :], YA[:, fc, :], fc == 0, False, [bc_, bYA], [bp])
                    for fc in range(NF):
                        MM(kb, p[:, 0:HW], s_[:, fc, :], YB[:, fc, :], False, fc == NF - 1, [bs_, bYB], [bp])
                    TT(kb, "pool", u_[:], u_[:], skr[:, o * HW:(o + 1) * HW], ALU.mult, [bu_, bwf], [bu_])
                    TT(kb, "dve", z_[:], p[:, 0:HW], u_[:], ALU.add, [bp, bu_], [bz_])
                    TT(kb, "dve", z_[:], z_[:], x_[:], ALU.mult, [bz_, bx_], [bz_])
                    if o == 0:
                        kb.dma("sp", cx.z1_d[n * 128:(n + 1) * 128, :], z_[:], [bz_], [cx.bz1])
                        CP(kb, "pool", utok[:, n, :], z_[:], [bz_], [bu])
                    else:
                        ys_, bys_ = yst[n % 2], byst[n % 2]
                        pt, bpt = nxt()
                        for j in range(2):
                            TR(kb, pt[:, j * 128:(j + 1) * 128], z_[:, j * 128:(j + 1) * 128], cx.ident[:], [bz_, cx.bconst], [bpt])
                        CP(kb, "dve", ys_[:], pt[:, 0:256].rearrange("p (j c) -> p j c", j=2), [bpt], [bys_])
                        kb.dma("sp", cx.yT_d[4:6].rearrange("c p t -> p c t")[:, :, n * 128:(n + 1) * 128], ys_[:], [bys_], [cx.byT])
                if o == 0:
                    kb.barrier()


def kernel(**inputs):
    inp = {k: np.asarray(v) for k, v in inputs.items()}
    nc = build({"full"})
    shared = host_shared(inp)
    halves = [host_half(inp, 0), host_half(inp, 1)]
    in_maps = [host_inputs(inp, c, shared, halves) for c in range(8)]
    res = run_bass_kernel_spmd(nc, in_maps, core_ids=list(range(8)))
    out = np.zeros((4, S, D), np.float32)
    for c in range(8):
        out[c // 2, (c % 2) * SH:(c % 2 + 1) * SH] = np.asarray(res.results[c]["out"])
    return out
```

```python
import math
import ml_dtypes
from contextlib import ExitStack
from concourse.bass_utils import run_bass_kernel_spmd
import numpy as np
import concourse.bass as bass
import concourse.mybir as mybir

F32 = mybir.dt.float32
BF16 = mybir.dt.bfloat16
AF = mybir.ActivationFunctionType
ALU = mybir.AluOpType
AX = mybir.AxisListType


NPOOL = 84


class Buf:
    __slots__ = ("name", "lw", "rd", "sem", "cnt")

    def __init__(self, name=""):
        self.name = name
        self.lw = None
        self.rd = {}
        self.sem = None
        self.cnt = 0


class KB:
    def __init__(self, nc):
        self.nc = nc
        self.eng = {"pe": nc.tensor, "dve": nc.vector, "act": nc.scalar,
                    "pool": nc.gpsimd, "sp": nc.sync}
        self.sems = {}
        self.cnt = {}
        for k in self.eng:
            self.sems[k] = nc.alloc_semaphore("c_" + k)
            self.cnt[k] = 0
        self.waited = {k: {} for k in self.eng}
        self.ndma = 0
        self.nbuf = 0
        self.dmacnt = {}
        self.bufs = []
        self.pool = [nc.alloc_semaphore("d%d" % i) for i in range(NPOOL)]
        for h in list(self.sems.values()) + self.pool:
            nc.gpsimd.sem_clear(h)
        nc.all_engine_barrier()

    def buf(self, name=""):
        self.nbuf += 1
        b = Buf(name or ("b%d" % self.nbuf))
        self.bufs.append(b)
        return b

    def _wait(self, e, reads, writes):
        need = {}
        for b in reads:
            if b.lw is not None:
                k, v = b.lw
                if need.get(k, 0) < v:
                    need[k] = v
        for b in writes:
            if b.lw is not None:
                k, v = b.lw
                if need.get(k, 0) < v:
                    need[k] = v
            for k, v in b.rd.items():
                if need.get(k, 0) < v:
                    need[k] = v
        w = self.waited[e]
        for k, v in need.items():
            if k == e and (e == "pe" or v > self.cnt[e]):
                continue
            if w.get(k, 0) < v:
                self.eng[e].wait_ge(self.sems[k], v)
                w[k] = v

    def op(self, e, fn, reads=(), writes=(), inc=True):
        self._wait(e, reads, writes)
        ins = fn(self.eng[e])
        if inc:
            self.cnt[e] += 1
            ins.then_inc(self.sems[e], 1)
            tok = (e, self.cnt[e])
        else:
            tok = (e, self.cnt[e] + 1)
        for b in writes:
            b.lw = tok
            b.rd = {}
        for b in reads:
            if b.rd.get(tok[0], 0) < tok[1]:
                b.rd[tok[0]] = tok[1]
        return ins

    def dma(self, q, out, in_, reads, writes, **kw):
        self._wait(q, reads, writes)
        wb = writes[0]
        if wb.sem is None:
            key = "d%d" % self.ndma
            self.sems[key] = self.pool[self.ndma]
            self.ndma += 1
            wb.sem = key
        self.dmacnt[wb.sem] = self.dmacnt.get(wb.sem, 0) + 16
        self.eng[q].dma_start(out=out, in_=in_, **kw).then_inc(self.sems[wb.sem], 16)
        tok = (wb.sem, self.dmacnt[wb.sem])
        for b in writes:
            b.lw = tok
            b.rd = {}
        for b in reads:
            if b.rd.get(tok[0], 0) < tok[1]:
                b.rd[tok[0]] = tok[1]

    def barrier(self):
        cur = dict(self.cnt)
        cur.update(self.dmacnt)
        for e in self.eng:
            w = self.waited[e]
            for k, v in cur.items():
                if v <= 0 or (k == e and e == "pe"):
                    continue
                if w.get(k, 0) < v:
                    self.eng[e].wait_ge(self.sems[k], v)
                    w[k] = v
        for b in self.bufs:
            b.lw = None
            b.rd = {}
            b.sem = None
        self.ndma = 0

    def finish(self, bufs):
        self._wait("sp", bufs, [])


S = 4096
D = 2048
NT = 32
DIN = 5392
DFF = 5632
EPS = 1e-6
NF = 33
NDFT = 8192
PI = math.pi

SH = S // 2
NTH = SH // 128
NCK = 2
HW = 256


def half_cols(r):
    c = []
    c += list(range(0 + HW * r, 0 + HW * r + HW))
    c += list(range(512 + HW * r, 512 + HW * r + HW))
    c += list(range(1024 + HW * r, 1024 + HW * r + HW))
    k = list(range(1536 + 64 * r, 1536 + 64 * r + 64))
    c += k + k
    for seg in range(3):
        c += list(range(1792 + 512 * seg + HW * r, 1792 + 512 * seg + HW * r + HW))
    c += list(range(3328 + HW * r, 3328 + HW * r + HW))
    c += list(range(3840 + HW * r, 3840 + HW * r + HW))
    nfm = len(c)
    c += list(range(1664 + 64 * r, 1664 + 64 * r + 64))
    c += list(range(3840 + HW * r, 3840 + HW * r + HW))
    c += list(range(4352 + HW * r, 4352 + HW * r + HW))
    c += list(range(4864 + HW * r, 4864 + HW * r + HW))
    for g in range(4):
        c += [5376 + 4 * g + 2 * r, 5376 + 4 * g + 2 * r + 1]
    return np.array(c), nfm


NFM = 17
TM = [("vtm", 2176, 64), ("dk", 2240, 256), ("dv", 2496, 256), ("do", 2752, 256), ("g", 3008, 8)]
NCH = 3016
TMOFF = {}
_o = 0
for nm, lo, n in TM:
    TMOFF[nm] = _o
    _o += n
NTM = _o


CO = {}
_o = 0
for nm, n in (("lru_cw", 8), ("lru_cb", 2), ("lru_ba", 4), ("lru_bx", 4), ("lru_lam", 4), ("att_qg", 1), ("att_kg", 1),
              ("hy_cw", 18), ("hy_cb", 6), ("hy_b1", 1), ("hy_b2", 1), ("hy_sf", 1)):
    CO[nm] = _o
    _o += n
NCOL = _o


class Ctx:
    pass


_UC = [0]


def U(name):
    _UC[0] += 1
    return "%s_%d" % (name, _UC[0])


def ACT(kb, out, in_, func, rd, wr, **kw):
    return kb.op("act", lambda e: e.activation(out=out, in_=in_, func=func, **kw), rd, wr)


def MM(kb, out, lhsT, rhs, start, stop, rd, wr):
    return kb.op("pe", lambda e: e.matmul(out, lhsT=lhsT, rhs=rhs, start=start, stop=stop), rd, wr, inc=stop)


def TR(kb, out, in_, ident, rd, wr):
    return kb.op("pe", lambda e: e.transpose(out=out, in_=in_, identity=ident), rd, wr)


def TT(kb, eng, out, in0, in1, op, rd, wr):
    return kb.op(eng, lambda e: e.tensor_tensor(out=out, in0=in0, in1=in1, op=op), rd, wr)


def TS(kb, eng, out, in0, s1, s2, op0, op1, rd, wr):
    if op1 is None:
        return kb.op(eng, lambda e: e.tensor_scalar(out=out, in0=in0, scalar1=s1, scalar2=None, op0=op0), rd, wr)
    return kb.op(eng, lambda e: e.tensor_scalar(out=out, in0=in0, scalar1=s1, scalar2=s2, op0=op0, op1=op1), rd, wr)


def STT(kb, out, in0, scalar, in1, op0, op1, rd, wr):
    return kb.op("dve", lambda e: e.scalar_tensor_tensor(out=out, in0=in0, scalar=scalar, in1=in1, op0=op0, op1=op1), rd, wr)


def CP(kb, eng, out, in_, rd, wr):
    return kb.op(eng, lambda e: e.tensor_copy(out=out, in_=in_), rd, wr)


def RECIP(kb, out, in_, rd, wr):
    return kb.op("dve", lambda e: e.reciprocal(out=out, in_=in_), rd, wr)


def MEMSET(kb, eng, ap, val, wr):
    return kb.op(eng, lambda e: e.memset(ap, val), [], wr)


def phase_ada(cx, l):
    nc, kb = cx.nc, cx.kb
    adaw = cx.I["ada_w"][l].rearrange("(kc p) n -> p kc n", p=128)
    with ExitStack() as es:
        aw0 = es.enter_context(nc.sbuf_tensor(U("adaw0"), [128, 16, 512], F32))
        aw1 = es.enter_context(nc.sbuf_tensor(U("adaw1"), [128, 16, 512], F32))
        adab = es.enter_context(nc.sbuf_tensor(U("adab"), [128, 96], F32))
        gbias = es.enter_context(nc.sbuf_tensor(U("gbias"), [128, 512], F32))
        crep = es.enter_context(nc.sbuf_tensor(U("crep"), [128, 16, 128], F32))
        cx.crep = crep
        cx.bcrep = kb.buf()
        for kc in range(16):
            TS(kb, "dve", crep[:, kc, :], cx.ones[:], cx.cact[:, kc:kc + 1], None, ALU.mult, None, [cx.bconst, cx.bcact], [cx.bcrep])
        aws = [aw0, aw1]
        baw = [kb.buf(), kb.buf()]
        bab, bgb = kb.buf(), kb.buf()
        kb.dma("sp", adab[:], cx.I["ada_b_col"][l], [], [bab])
        psm, bpsm = cx.ps[7], cx.pb[7]
        for j in range(24):
            a, ba = aws[j % 2], baw[j % 2]
            kb.dma("sp", a[:], adaw[:, :, j * 512:(j + 1) * 512], [], [ba])
            which = {2: 0, 5: 1}.get(j // 4)
            if which is not None:
                p, bp = cx.ps[j % 2], cx.pb[j % 2]
                for kc in range(16):
                    MM(kb, p[:, :], cx.crep[:, kc, :], a[:, kc, :], kc == 0, kc == 15, [cx.bcrep, ba], [bp])
                kb.dma("sp", gbias[:], cx.I["ada_b_rep"][l][:, j * 512:(j + 1) * 512], [], [bgb])
                TT(kb, "dve", cx.gb[l][which][:, (j % 4) * 512:(j % 4 + 1) * 512], p[:, :], gbias[:], ALU.add, [bp, bgb], [cx.bgb[l][which]])
            else:
                for cc in range(4):
                    jj = 4 * j + cc
                    for kc in range(16):
                        MM(kb, psm[:, jj:jj + 1], a[:, kc, cc * 128:(cc + 1) * 128], cx.cact[:, kc:kc + 1], kc == 0, kc == 15, [ba, cx.bcact], [bpsm])
        for lo, hi in ((0, 32), (48, 80)):
            TT(kb, "dve", cx.modT[l][:, lo:hi], psm[:, lo:hi], adab[:, lo:hi], ALU.add, [bpsm, bab], [cx.bmod[l]])
        for s, (sc_lo, sh_lo, gcol) in enumerate(((16, 0, cx.gmix[l]), (64, 48, cx.gffn[l]))):
            TS(kb, "dve", cx.AB[l][:, 32 * s:32 * s + 16], cx.modT[l][:, sc_lo:sc_lo + 16], 1.0, None, ALU.add, None, [cx.bmod[l]], [cx.bAB[l]])
            TT(kb, "dve", cx.AB[l][:, 32 * s:32 * s + 16], cx.AB[l][:, 32 * s:32 * s + 16], gcol, ALU.mult, [cx.bAB[l], cx.bconst], [cx.bAB[l]])
            CP(kb, "dve", cx.AB[l][:, 32 * s + 16:32 * s + 32], cx.modT[l][:, sh_lo:sh_lo + 16], [cx.bmod[l]], [cx.bAB[l]])


def phase_norm(cx, xsrc, bxsrc, Acol, Bcol, bAB, hT_d, bhT, ntiles=NT, rowmap=None):
    nc, kb = cx.nc, cx.kb
    hview = hT_d.rearrange("c p t -> p c t")
    with ExitStack() as es:
        x0 = es.enter_context(nc.sbuf_tensor(U("nx0"), [128, D], F32))
        x1 = es.enter_context(nc.sbuf_tensor(U("nx1"), [128, D], F32))
        xh0 = es.enter_context(nc.sbuf_tensor(U("nxh0"), [128, D], F32))
        xh1 = es.enter_context(nc.sbuf_tensor(U("nxh1"), [128, D], F32))
        st0 = es.enter_context(nc.sbuf_tensor(U("nst0"), [128, 16, 512], BF16))
        st1 = es.enter_context(nc.sbuf_tensor(U("nst1"), [128, 16, 512], BF16))
        ss = es.enter_context(nc.sbuf_tensor(U("nss"), [128, 8], F32))
        xs, bxs = [x0, x1], [kb.buf(), kb.buf()]
        xhs, bxh = [xh0, xh1], [kb.buf(), kb.buf()]
        sts, bst = [st0, st1], [kb.buf(), kb.buf()]
        bss = [kb.buf(), kb.buf()]
        for tb in range(ntiles // 4):
            st, bs = sts[tb % 2], bst[tb % 2]
            for tt in range(4):
                ti = tb * 4 + tt
                u = ti % 2
                r0 = ti * 128 if rowmap is None else rowmap(ti)
                kb.dma("sp", xs[u][:], xsrc[r0:r0 + 128, :], [bxsrc], [bxs[u]])
                c0 = 4 * u
                ACT(kb, xhs[u][:], xs[u][:], AF.Square, [bxs[u]], [bxh[u]])
                kb.op("dve", lambda e: e.tensor_reduce(out=ss[:, c0:c0 + 1], in_=xhs[u][:], axis=AX.X, op=ALU.add), [bxh[u]], [bss[u]])
                TS(kb, "dve", ss[:, c0 + 1:c0 + 2], ss[:, c0:c0 + 1], 1.0 / D, EPS, ALU.mult, ALU.add, [bss[u]], [bss[u]])
                ACT(kb, ss[:, c0 + 2:c0 + 3], ss[:, c0 + 1:c0 + 2], AF.Sqrt, [bss[u]], [bss[u]])
                RECIP(kb, ss[:, c0 + 3:c0 + 4], ss[:, c0 + 2:c0 + 3], [bss[u]], [bss[u]])
                ACT(kb, xhs[u][:], xs[u][:], AF.Identity, [bxs[u], bss[u]], [bxh[u]], scale=ss[:, c0 + 3:c0 + 4])
                for q in range(4):
                    p, bp = cx.ps[(ti * 4 + q) % 4], cx.pb[(ti * 4 + q) % 4]
                    for r in range(4):
                        kc = 4 * q + r
                        TR(kb, p[:, r * 128:(r + 1) * 128], xhs[u][:, kc * 128:(kc + 1) * 128], cx.ident[:], [bxh[u], cx.bconst], [bp])
                    for r in range(4):
                        kc = 4 * q + r
                        ACT(kb, st[:, kc, tt * 128:(tt + 1) * 128], p[:, r * 128:(r + 1) * 128], AF.Identity, [bp, bAB], [bs],
                            scale=Acol[:, kc:kc + 1], bias=Bcol[:, kc:kc + 1])
            kb.dma("pool", hview[:, :, tb * 512:(tb + 1) * 512], st[:], [bs], [bhT])


def phase_inproj(cx, l):
    nc, kb = cx.nc, cx.kb
    win = cx.I["w_in_h"][l].rearrange("(kc p) n -> p kc n", p=128)
    with ExitStack() as es:
        hTs = es.enter_context(nc.sbuf_tensor(U("hTs"), [128, 16, S], BF16))
        w0 = es.enter_context(nc.sbuf_tensor(U("ipw0"), [128, 16, 512], BF16))
        w1 = es.enter_context(nc.sbuf_tensor(U("ipw1"), [128, 16, 512], BF16))
        o0 = es.enter_context(nc.sbuf_tensor(U("ipo0"), [128, 512], F32))
        o1 = es.enter_context(nc.sbuf_tensor(U("ipo1"), [128, 512], F32))
        bfm = es.enter_context(nc.sbuf_tensor(U("ipbf"), [128, NFM], F32))
        btm = es.enter_context(nc.sbuf_tensor(U("ipbt"), [128, NTM], F32))
        bh = kb.buf()
        ws, bws = [w0, w1], [kb.buf(), kb.buf()]
        os_, bos = [o0, o1], [kb.buf() for _ in range(2)]
        bb = kb.buf()
        for kc in range(16):
            kb.dma("sp", hTs[:, kc, :], cx.hT_d[kc], [cx.bhT], [bh])
        kb.dma("sp", bfm[:], cx.I["b_fm"][l], [], [bb])
        kb.dma("sp", btm[:], cx.I["b_tm"][l], [], [bb])
        no = 0
        groups = [list(range(g, min(g + 4, NFM))) for g in range(0, NFM, 4)]
        for gi, chunks in enumerate(groups):
            w, bw = ws[gi % 2], bws[gi % 2]
            nw = 128 * len(chunks)
            kb.dma("pool", w[:, :, 0:nw], win[:, :, chunks[0] * 128:chunks[0] * 128 + nw], [], [bw])
            for tb in range(8):
                for ci, ch in enumerate(chunks):
                    p, bp = cx.ps[no % 6], cx.pb[no % 6]
                    o, bo = os_[no % 2], bos[no % 2]
                    no += 1
                    for kc in range(16):
                        MM(kb, p[:, :], w[:, kc, ci * 128:(ci + 1) * 128], hTs[:, kc, tb * 512:(tb + 1) * 512], kc == 0, kc == 15, [bw, bh], [bp])
                    ACT(kb, o[:], p[:, :], AF.Identity, [bp, bb], [bo], bias=bfm[:, ch:ch + 1])
                    kb.dma("sp", cx.pfm_d[ch, :, tb * 512:(tb + 1) * 512], o[:], [bo], [cx.bpfm])
        for gi, (nm, lo, n) in enumerate(TM):
            w, bw = ws[(gi + 1) % 2], bws[(gi + 1) % 2]
            kb.dma("pool", w[:, :, 0:n], win[:, :, lo:lo + n], [], [bw])
            dst = cx.tm_d[nm]
            for tt in range(NT):
                p, bp = cx.ps[no % 6], cx.pb[no % 6]
                o, bo = os_[no % 2], bos[no % 2]
                no += 1
                for kc in range(16):
                    MM(kb, p[:, 0:n], hTs[:, kc, tt * 128:(tt + 1) * 128], w[:, kc, 0:n], kc == 0, kc == 15, [bh, bw], [bp])
                TT(kb, "dve", o[:, 0:n], p[:, 0:n], btm[:, TMOFF[nm]:TMOFF[nm] + n], ALU.add, [bp, bb], [bo])
                kb.dma("sp", dst[tt * 128:(tt + 1) * 128, :], o[:, 0:n], [bo], [cx.btm[nm]])


def host_shared(inp):
    f = np.float32
    d = {}
    d["ada_w"] = inp["ada_w"]
    d["ada_b_col"] = np.ascontiguousarray(inp["ada_b"].reshape(2, 96, 128).transpose(0, 2, 1))
    d["ada_b_rep"] = np.ascontiguousarray(np.broadcast_to(inp["ada_b"][:, None, :], (2, 128, 12288)))
    d["gmix_col"] = np.ascontiguousarray(inp["norm_mix_g"].reshape(2, 16, 128).transpose(0, 2, 1))
    d["gffn_col"] = np.ascontiguousarray(inp["norm_ffn_g"].reshape(2, 16, 128).transpose(0, 2, 1))
    d["ident"] = np.eye(128, dtype=f)
    t = np.arange(S)
    inv = (10000.0 ** (-np.arange(0, 32, 2, dtype=np.float64) / 32)).astype(np.float64)
    ang = np.zeros((64, S))
    for j in range(64):
        pos = (t // 64) if j < 32 else (t % 64)
        ang[j] = pos * inv[j % 16]
    ang = np.concatenate([ang, ang], 0)
    d["rope_cos"] = np.cos(ang).astype(f)
    d["rope_sin"] = np.sin(ang).astype(f)
    P = np.zeros((128, 128), f)
    for m in range(128):
        if m % 32 < 16:
            P[m, m + 16] = -1.0
        else:
            P[m, m - 16] = 1.0
    d["rope_PT"] = np.ascontiguousarray(P.T)
    b64 = np.zeros((128, 128), f)
    b64[:64, :64] = 1.0
    b64[64:, 64:] = 1.0
    d["bd64"] = b64
    s_, t_ = np.meshgrid(np.arange(128), np.arange(128), indexing="ij")
    mlc = np.zeros((4, 128, 128), f)
    mlc[0] = (s_ <= t_)
    mlc[1] = (s_ >= t_)
    mlc[2] = np.where(s_ <= t_, 0.0, -30000.0)
    mlc[3] = np.where(s_ >= t_, 0.0, -30000.0)
    d["mlc"] = mlc
    a = np.arange(NF * 128, dtype=np.int64)
    m = (a[:, None] * a[None, :]) % NDFT
    ang = 2.0 * np.pi * m.astype(np.float64) / NDFT
    valid = (a[:, None] <= 4096) & (a[None, :] <= 4096)
    for nm, fn in (("dft_c", np.cos), ("dft_s", np.sin)):
        full = np.where(valid, fn(ang), 0.0).astype(f)
        blk = full.reshape(NF, 128, NF, 128).transpose(2, 1, 0, 3)
        d[nm] = np.ascontiguousarray(blk).astype(ml_dtypes.bfloat16)
    wf = np.zeros(NF * 128, f)
    wf[0:4097] = 2.0 / NDFT
    wf[0] = 1.0 / NDFT
    wf[4096] = 1.0 / NDFT
    d["dft_wf"] = np.ascontiguousarray(wf.reshape(NF, 128).T)
    pos = np.arange(S, dtype=f)
    tt_ = pos / (S - 1)
    bands = np.linspace(1e-4, 7, 8, dtype=f)
    angf = (2.0 * np.pi * pos / S)[:, None] * bands
    d["hy_feat"] = np.ascontiguousarray(np.concatenate([tt_[:, None], np.cos(angf), -np.sin(angf)], axis=-1).T.astype(f))
    d["hy_negt"] = np.ascontiguousarray((-tt_).reshape(32, 128).T.astype(f))
    d["hy_w1"] = inp["hy_w1"]
    d["hy_w2"] = inp["hy_w2"]
    d["ffn_w1"] = inp["ffn_w1"]
    d["ffn_w3"] = inp["ffn_w3"]
    d["ffn_w2"] = inp["ffn_w2"]
    d["final_g_rep"] = np.ascontiguousarray(np.broadcast_to(inp["final_g"][None, :], (128, D)))
    return d


def host_half(inp, r):
    f = np.float32
    d = {}
    ci, nfm = half_cols(r)
    assert nfm == NFM * 128 and len(ci) == NCH
    d["w_in_h"] = np.ascontiguousarray(inp["w_in"][:, :, ci])
    bh = inp["b_in"][:, ci]
    d["b_fm"] = np.ascontiguousarray(bh[:, :nfm].reshape(2, NFM, 128).transpose(0, 2, 1))
    d["b_tm"] = np.ascontiguousarray(np.broadcast_to(bh[:, None, nfm:], (2, 128, NTM)))
    cols = np.zeros((2, 128, NCOL), f)
    for l in range(2):
        cw = inp["lru_conv_w"][l].reshape(4, 4, 128)[:, 2 * r:2 * r + 2]
        cols[l, :, CO["lru_cw"]:CO["lru_cw"] + 8] = cw.transpose(2, 1, 0).reshape(128, 8)
        cols[l, :, CO["lru_cb"]:CO["lru_cb"] + 2] = inp["lru_conv_b"][l].reshape(4, 128)[2 * r:2 * r + 2].T
        for nm in ("lru_ba", "lru_bx"):
            v = inp[nm][l].reshape(2, 4, 128)[:, 2 * r:2 * r + 2]
            cols[l, :, CO[nm]:CO[nm] + 4] = v.transpose(2, 0, 1).reshape(128, 4)
        v = inp["lru_lambda"][l].reshape(2, 4, 128)[:, 2 * r:2 * r + 2]
        cols[l, :, CO["lru_lam"]:CO["lru_lam"] + 4] = v.transpose(2, 0, 1).reshape(128, 4)
        cols[l, :, CO["att_qg"]] = np.tile(inp["att_q_norm_g"][l], 2)
        cols[l, :, CO["att_kg"]] = np.tile(inp["att_k_norm_g"][l], 2)
        hw = inp["hy_conv_w"][l].reshape(3, 3, 4, 128)[:, :, 2 * r:2 * r + 2]
        cols[l, :, CO["hy_cw"]:CO["hy_cw"] + 18] = hw.transpose(3, 1, 2, 0).reshape(128, 18)
        hb = inp["hy_conv_b"][l].reshape(3, 4, 128)[:, 2 * r:2 * r + 2]
        cols[l, :, CO["hy_cb"]:CO["hy_cb"] + 6] = hb.transpose(2, 0, 1).reshape(128, 6)
        cols[l, :64, CO["hy_b1"]] = inp["hy_b1"][l]
        cols[l, :64, CO["hy_b2"]] = inp["hy_b2"][l]
        cols[l, :64, CO["hy_sf"]] = inp["hy_sin_freq"][l]
    d["cols"] = cols
    bd = np.zeros((2, 2, 2, 2, 128, 128), f)
    for gi, nm in enumerate(("lru_wa", "lru_wx")):
        for c in range(2):
            for i in range(2):
                bd[:, :, gi, c, 64 * i:64 * i + 64, 64 * i:64 * i + 64] = inp[nm][:, :, 2 * (2 * r + c) + i]
    d["lru_bd"] = bd
    d["ml_g_rep"] = np.ascontiguousarray(np.broadcast_to(inp["ml_norm_g"][:, None, HW * r:HW * r + HW], (2, 128, HW)))
    w3 = inp["hy_w3"].reshape(2, 64, 4, 512)[:, :, :, HW * r:HW * r + HW]
    d["hy_w3"] = np.ascontiguousarray(w3.reshape(2, 64, 4 * HW))
    dec = inp["hy_decay"].reshape(2, 4, 512)[:, :, HW * r:HW * r + HW].reshape(2, 1, 4 * HW)
    d["hy_decay_rep"] = np.ascontiguousarray(np.broadcast_to(dec, (2, 128, 4 * HW)))
    sk = inp["hy_skip"][:, :, HW * r:HW * r + HW].reshape(2, 1, 2 * HW)
    d["hy_skip_rep"] = np.ascontiguousarray(np.broadcast_to(sk, (2, 128, 2 * HW)))
    rows = np.concatenate([np.arange(512 * g + HW * r, 512 * g + HW * r + HW) for g in range(4)])
    d["w_out_h"] = np.ascontiguousarray(inp["w_out"][:, rows, :])
    return d


def host_inputs(inp, core, shared, halves):
    b, r = core // 2, core % 2
    d = dict(shared)
    d.update(halves[r])
    d["x"] = np.ascontiguousarray(inp["x"][b])
    d["x_half"] = np.ascontiguousarray(inp["x"][b, r * SH:(r + 1) * SH])
    d["c_col"] = np.ascontiguousarray(inp["c"][b].reshape(16, 128).T)
    return d


IN_SPECS = {
    "x": ([S, D], F32), "x_half": ([SH, D], F32), "c_col": ([128, 16], F32), "ada_w": ([2, D, 6 * D], F32),
    "ada_b_col": ([2, 128, 96], F32), "ada_b_rep": ([2, 128, 6 * D], F32),
    "gmix_col": ([2, 128, 16], F32), "gffn_col": ([2, 128, 16], F32),
    "w_in_h": ([2, D, NCH], F32), "b_fm": ([2, 128, NFM], F32), "b_tm": ([2, 128, NTM], F32),
    "ident": ([128, 128], F32),
    "cols": ([2, 128, NCOL], F32), "lru_bd": ([2, 2, 2, 2, 128, 128], F32),
    "rope_cos": ([128, S], F32), "rope_sin": ([128, S], F32), "rope_PT": ([128, 128], F32), "bd64": ([128, 128], F32),
    "w_out_h": ([2, D // 2, D], F32), "ffn_w1": ([2, D, DFF], F32), "ffn_w3": ([2, D, DFF], F32), "ffn_w2": ([2, DFF, D], F32),
    "final_g_rep": ([128, D], F32),
    "dft_c": ([NF, 128, NF, 128], BF16), "dft_s": ([NF, 128, NF, 128], BF16), "dft_wf": ([128, NF], F32),
    "hy_feat": ([17, S], F32), "hy_negt": ([128, 32], F32), "hy_w1": ([2, 17, 64], F32), "hy_w2": ([2, 64, 64], F32),
    "hy_w3": ([2, 64, 4 * HW], F32), "hy_decay_rep": ([2, 128, 4 * HW], F32), "hy_skip_rep": ([2, 128, 2 * HW], F32),
    "mlc": ([4, 128, 128], F32), "ml_g_rep": ([2, 128, HW], F32),
}


def build(stages, dbg=()):
    nc = bass.Bass("TRN2", target_bir_lowering=False)
    kb = KB(nc)
    cx = Ctx()
    cx.nc, cx.kb = nc, kb
    cx.I = {k: nc.dram_tensor(k, sh, dt, kind="ExternalInput").ap() for k, (sh, dt) in IN_SPECS.items()}

    def scratch(name, shape, dt):
        kind = "ExternalOutput" if name in dbg else "Internal"
        return nc.dram_tensor(name, shape, dt, kind=kind).ap()

    cx.hT_d = scratch("hT_d", [16, 128, S], BF16)
    cx.pfm_d = scratch("pfm_d", [NFM, 128, S], F32)
    cx.tm_d = {nm: scratch("tm_" + nm, [S, n], F32) for nm, lo, n in TM}
    cx.bhT, cx.bpfm = kb.buf(), kb.buf()
    cx.yT_d = scratch("yT_d", [8, 128, S], BF16)
    cx.byT = kb.buf()
    cx.ztok_d = scratch("ztok_d", [S, 3 * HW], F32)
    cx.z1_d = scratch("z1_d", [S, HW], F32)
    cx.G_d = scratch("G_d", [2, S, 2 * HW], BF16)
    cx.H_d = scratch("H_d", [2, NF, 128, 2 * HW], F32)
    cx.bztok, cx.bz1, cx.bG_d, cx.bH_d = kb.buf(), kb.buf(), kb.buf(), kb.buf()
    cx.xa_d = scratch("xa_d", [SH, D], F32)
    cx.xb_d = scratch("xb_d", [SH, D], F32)
    cx.part_d = scratch("part_d", [S, D], F32)
    cx.sum_d = scratch("sum_d", [SH, D], F32)
    cx.xfull_d = scratch("xfull_d", [S, D], F32)
    cx.h2T_d = scratch("h2T_d", [16, 128, SH], BF16)
    cx.bxa, cx.bxb, cx.bpart, cx.bsum, cx.bxfull, cx.bh2T = [kb.buf() for _ in range(6)]
    cx.out = nc.dram_tensor("out", [SH, D], F32, kind="ExternalOutput").ap()
    cx.bout = kb.buf()
    cx.btm = {nm: kb.buf() for nm, _, _ in TM}
    cx.dbg_out = scratch("dbg_out", [128, 512], F32) if "dbg_out" in dbg else None
    finals = []
    with ExitStack() as es:
        ident = es.enter_context(nc.sbuf_tensor(U("ident"), [128, 128], F32))
        cact = es.enter_context(nc.sbuf_tensor(U("cact"), [128, 16], F32))
        ones = es.enter_context(nc.sbuf_tensor(U("ones"), [128, 128], F32))
        gcols = es.enter_context(nc.sbuf_tensor(U("gcols"), [128, 64], F32))
        cx.cols = es.enter_context(nc.sbuf_tensor(U("cols"), [128, NCOL], F32))
        cx.bcols = kb.buf()
        cx.epsc = es.enter_context(nc.sbuf_tensor(U("epsc"), [128, 2], F32))
        cx.ccsem = nc.alloc_semaphore(U("ccsem"))
        cx.ccdummy = es.enter_context(nc.sbuf_tensor(U("ccd"), [128, 2], F32))
        cx.bccd = kb.buf()
        modT0 = es.enter_context(nc.sbuf_tensor(U("modT0"), [128, 96], F32))
        AB0 = es.enter_context(nc.sbuf_tensor(U("AB0"), [128, 64], F32))
        g1b0 = es.enter_context(nc.sbuf_tensor(U("g1b0"), [128, D], F32))
        g2b0 = es.enter_context(nc.sbuf_tensor(U("g2b0"), [128, D], F32))
        ps0 = es.enter_context(nc.psum_tensor(U("ps0"), [128, 512], F32))
        ps1 = es.enter_context(nc.psum_tensor(U("ps1"), [128, 512], F32))
        ps2 = es.enter_context(nc.psum_tensor(U("ps2"), [128, 512], F32))
        ps3 = es.enter_context(nc.psum_tensor(U("ps3"), [128, 512], F32))
        ps4 = es.enter_context(nc.psum_tensor(U("ps4"), [128, 512], F32))
        ps5 = es.enter_context(nc.psum_tensor(U("ps5"), [128, 512], F32))
        ps6 = es.enter_context(nc.psum_tensor(U("ps6"), [128, 512], F32))
        ps7 = es.enter_context(nc.psum_tensor(U("ps7"), [128, 512], F32))
        cx.ps = [ps0, ps1, ps2, ps3, ps4, ps5, ps6, ps7]
        cx.pb = [kb.buf() for _ in range(8)]
        cx.ident, cx.cact, cx.ones = ident, cact, ones
        cx.bconst, cx.bcact, cx.bcrep = kb.buf(), kb.buf(), kb.buf()
        _b = kb.buf()
        cx.modT, cx.bmod = [modT0, modT0], [_b, _b]
        _b = kb.buf()
        cx.AB, cx.bAB = [AB0, AB0], [_b, _b]
        cx.gb = [[g1b0, g2b0], [g1b0, g2b0]]
        _b = [kb.buf(), kb.buf()]
        cx.bgb = [_b, _b]
        cx.gmix = [gcols[:, 0:16], gcols[:, 16:32]]
        cx.gffn = [gcols[:, 32:48], gcols[:, 48:64]]
        kb.dma("sp", ident[:], cx.I["ident"], [], [cx.bconst])
        for l in range(2):
            kb.dma("sp", gcols[:, 16 * l:16 * l + 16], cx.I["gmix_col"][l], [], [cx.bconst])
            kb.dma("sp", gcols[:, 32 + 16 * l:48 + 16 * l], cx.I["gffn_col"][l], [], [cx.bconst])
        kb.dma("sp", cact[:], cx.I["c_col"], [], [cx.bcact])
        MEMSET(kb, "dve", ones[:], 1.0, [cx.bconst])
        MEMSET(kb, "dve", cx.epsc[:], EPS, [cx.bconst])
        kb.dma("sp", cx.cols[:], cx.I["cols"][0], [], [cx.bcols])
        ACT(kb, cact[:], cact[:], AF.Silu, [cx.bcact], [cx.bcact])
        if "full" in stages:
            bx0, bxh0 = kb.buf(), kb.buf()
            for l in range(2):
                kb.barrier()
                if l > 0:
                    kb.dma("sp", cx.cols[:], cx.I["cols"][l], [], [cx.bcols])
                phase_ada(cx, l)
                kb.barrier()
                xsrc, bxs = (cx.I["x"], bx0) if l == 0 else (cx.xfull_d, cx.bxfull)
                phase_norm(cx, xsrc, bxs, cx.AB[l][:, 0:16], cx.AB[l][:, 16:32], cx.bAB[l], cx.hT_d, cx.bhT, rowmap=(None if l == 0 else pair_row))
                kb.barrier()
                phase_inproj(cx, l)
                kb.barrier()
                phase_lru(cx, l)
                kb.barrier()
                phase_att(cx, l)
                kb.barrier()
                phase_hyena(cx, l)
                kb.barrier()
                phase_mlstm(cx, l)
                kb.barrier()
                phase_outproj(cx, l)
                collective(cx, "ReduceScatter", ALU.add, cx.part_d, cx.sum_d)
                xh, bxh = (cx.I["x_half"], bxh0) if l == 0 else (cx.xb_d, cx.bxb)
                phase_resid(cx, l, xh, bxh)
                kb.barrier()
                phase_norm(cx, cx.xa_d, cx.bxa, cx.AB[l][:, 32:48], cx.AB[l][:, 48:64], cx.bAB[l], cx.h2T_d, cx.bh2T, ntiles=NTH)
                kb.barrier()
                phase_ffn(cx, l, cx.xa_d, cx.bxa, cx.xb_d, cx.bxb)
                if l == 0:
                    collective(cx, "AllGather", ALU.bypass, cx.xb_d, cx.xfull_d)
            kb.barrier()
            phase_final(cx, cx.xb_d, cx.bxb, cx.out, cx.bout)
            finals.append(cx.bout)
        if "ada" in stages:
            phase_ada(cx, 0)
        kb.barrier()
        if "norm1" in stages:
            phase_norm(cx, cx.I["x"], kb.buf(), cx.AB[0][:, 0:16], cx.AB[0][:, 16:32], cx.bAB[0], cx.hT_d, cx.bhT)
            finals.append(cx.bhT)
        if "inproj" in stages:
            kb.barrier()
            phase_inproj(cx, 0)
            finals += [cx.bpfm] + list(cx.btm.values())
        if "lru" in stages:
            kb.barrier()
            phase_lru(cx, 0)
            finals.append(cx.byT)
        if "att" in stages:
            kb.barrier()
            phase_att(cx, 0)
            finals.append(cx.byT)
        if "hyena" in stages:
            kb.barrier()
            phase_hyena(cx, 0)
            finals.append(cx.byT)
        if "mlstm" in stages:
            kb.barrier()
            phase_mlstm(cx, 0)
            finals.append(cx.byT)
        if "dbgmod" in dbg:
            with nc.sbuf_tensor(U("dbgt"), [128, 512], F32) as dbgt:
                bd, bo = kb.buf(), kb.buf()
                MEMSET(kb, "dve", dbgt[:], 0.0, [bd])
                CP(kb, "dve", dbgt[:, 0:96], modT0[:], [cx.bmod[0]], [bd])
                CP(kb, "dve", dbgt[:, 96:160], AB0[:], [cx.bAB[0]], [bd])
                CP(kb, "dve", dbgt[:, 160:416], g1b0[:, 0:256], [cx.bgb[0][0]], [bd])
                kb.dma("sp", cx.dbg_out, dbgt[:], [bd], [bo])
                finals.append(bo)
        kb.finish(finals)
    return nc


def phase_outproj(cx, l):
    nc, kb = cx.nc, cx.kb
    wsrc = cx.I["w_out_h"][l].rearrange("(kc p) n -> p kc n", p=128)
    yview = cx.yT_d.rearrange("c p t -> p c t")
    KC = 8
    with ExitStack() as es:
        wt = es.enter_context(nc.sbuf_tensor(U("opw"), [128, KC, D], BF16))
        ybs = [es.enter_context(nc.sbuf_tensor(U("opy"), [128, KC, 512], BF16)) for _ in range(2)]
        xts = [es.enter_context(nc.sbuf_tensor(U("opx"), [128, D], F32)) for _ in range(2)]
        bw = kb.buf()
        byb, bxt = [kb.buf(), kb.buf()], [kb.buf(), kb.buf()]
        for q in range(4):
            kb.dma("pool", wt[:, :, q * 512:(q + 1) * 512], wsrc[:, :, q * 512:(q + 1) * 512], [], [bw])
        no = 0
        for tb in range(8):
            yb, by = ybs[tb % 2], byb[tb % 2]
            kb.dma("sp", yb[:], yview[:, :, tb * 512:(tb + 1) * 512], [cx.byT], [by])
            for tt in range(4):
                ti = tb * 4 + tt
                xt, bx = xts[ti % 2], bxt[ti % 2]
                for db in range(4):
                    p, bp = cx.ps[no % 4], cx.pb[no % 4]
                    no += 1
                    for kc in range(KC):
                        MM(kb, p[:, :], yb[:, kc, tt * 128:(tt + 1) * 128], wt[:, kc, db * 512:(db + 1) * 512], kc == 0, kc == KC - 1, [by, bw], [bp])
                    if db % 2 == 0:
                        CP(kb, "dve", xt[:, db * 512:(db + 1) * 512], p[:, :], [bp], [bx])
                    else:
                        ACT(kb, xt[:, db * 512:(db + 1) * 512], p[:, :], AF.Identity, [bp], [bx])
                r0 = pair_row(ti)
                kb.dma("pool", cx.part_d[r0:r0 + 128, :], xt[:], [bx], [cx.bpart])


def pair_row(ti):
    return (ti % NTH) * 256 + (ti // NTH) * 128


def collective(cx, kind, op, src, dst):
    nc, kb = cx.nc, cx.kb
    kb.barrier()
    for j in range(NTH):
        big = slice(j * 256, (j + 1) * 256)
        small = slice(j * 128, (j + 1) * 128)
        i_ap, o_ap = (src[big, :], dst[small, :]) if kind == "ReduceScatter" else (src[small, :], dst[big, :])
        nc.gpsimd.sem_clear(cx.ccsem)
        nc.gpsimd.collective_compute(kind, op, replica_groups=[[0, 1], [2, 3], [4, 5], [6, 7]],
                                     ins=[i_ap], outs=[o_ap]).then_inc(cx.ccsem)
        nc.gpsimd.wait_ge(cx.ccsem, 1)
    MEMSET(kb, "pool", cx.ccdummy[:], 0.0, [cx.bccd])
    kb.barrier()


def phase_resid(cx, l, xsrc, bxsrc):
    nc, kb = cx.nc, cx.kb
    with ExitStack() as es:
        xs = [es.enter_context(nc.sbuf_tensor(U("rsx"), [128, D], F32)) for _ in range(2)]
        ss = [es.enter_context(nc.sbuf_tensor(U("rss"), [128, D], F32)) for _ in range(2)]
        bxs, bss = [kb.buf(), kb.buf()], [kb.buf(), kb.buf()]
        for ti in range(NTH):
            u = ti % 2
            kb.dma("sp", xs[u][:], xsrc[ti * 128:(ti + 1) * 128, :], [bxsrc], [bxs[u]])
            kb.dma("sp", ss[u][:], cx.sum_d[ti * 128:(ti + 1) * 128, :], [cx.bsum], [bss[u]])
            TT(kb, "dve", ss[u][:], ss[u][:], cx.gb[l][0][:], ALU.mult, [bss[u], cx.bgb[l][0]], [bss[u]])
            TT(kb, "pool", xs[u][:], xs[u][:], ss[u][:], ALU.add, [bxs[u], bss[u]], [bxs[u]])
            kb.dma("act", cx.xa_d[ti * 128:(ti + 1) * 128, :], xs[u][:], [bxs[u]], [cx.bxa])


def phase_ffn(cx, l, xsrc, bxsrc, xdst, bxdst):
    nc, kb = cx.nc, cx.kb
    w1s = cx.I["ffn_w1"][l].rearrange("(kc p) n -> p kc n", p=128)
    w3s = cx.I["ffn_w3"][l].rearrange("(kc p) n -> p kc n", p=128)
    w2s = cx.I["ffn_w2"][l].rearrange("(fc p) n -> p fc n", p=128)
    hview = cx.h2T_d.rearrange("c p t -> p c t")
    xs4 = xsrc.rearrange("(n p) d -> p n d", p=128)
    xd4 = xdst.rearrange("(n p) d -> p n d", p=128)
    NG = DFF // 256
    with ExitStack() as es:
        h0 = es.enter_context(nc.sbuf_tensor(U("fh0"), [128, 16, 512], BF16))
        h1 = es.enter_context(nc.sbuf_tensor(U("fh1"), [128, 16, 512], BF16))
        uT = es.enter_context(nc.sbuf_tensor(U("fu"), [128, 44, 512], BF16))
        wa = [es.enter_context(nc.sbuf_tensor(U("fw1"), [128, 16, 256], BF16)) for _ in range(2)]
        wb = [es.enter_context(nc.sbuf_tensor(U("fw3"), [128, 16, 256], BF16)) for _ in range(2)]
        w2t = [es.enter_context(nc.sbuf_tensor(U("fw2"), [128, 44, 256], BF16)) for _ in range(2)]
        xr = es.enter_context(nc.sbuf_tensor(U("fx"), [128, 4, D], F32))
        tmp = [es.enter_context(nc.sbuf_tensor(U("ft"), [128, 512], F32)) for _ in range(2)]
        hs, bhs = [h0, h1], [kb.buf(), kb.buf()]
        bu = kb.buf()
        bwa, bwb, bw2 = [kb.buf(), kb.buf()], [kb.buf(), kb.buf()], [kb.buf(), kb.buf()]
        bxr = kb.buf()
        btmp = [kb.buf(), kb.buf()]
        no = 0
        nw = 0
        nw2 = 0
        for tb in range(SH // 512):
            h, bh = hs[tb % 2], bhs[tb % 2]
            kb.dma("sp", h[:], hview[:, :, tb * 512:(tb + 1) * 512], [cx.bh2T], [bh])
            kb.dma("sp", xr[:], xs4[:, tb * 4:(tb + 1) * 4, :], [bxsrc], [bxr])
            for g in range(NG):
                a, ba = wa[nw % 2], bwa[nw % 2]
                b3, bb = wb[nw % 2], bwb[nw % 2]
                nw += 1
                kb.dma("pool", a[:], w1s[:, :, g * 256:(g + 1) * 256], [], [ba])
                kb.dma("pool", b3[:], w3s[:, :, g * 256:(g + 1) * 256], [], [bb])
                for cc in range(2):
                    fc = 2 * g + cc
                    pa, bpa = cx.ps[(no * 2) % 6], cx.pb[(no * 2) % 6]
                    pb_, bpb = cx.ps[(no * 2 + 1) % 6], cx.pb[(no * 2 + 1) % 6]
                    t, bt = tmp[no % 2], btmp[no % 2]
                    no += 1
                    for kc in range(16):
                        MM(kb, pa[:, :], a[:, kc, cc * 128:(cc + 1) * 128], h[:, kc, :], kc == 0, kc == 15, [ba, bh], [bpa])
                    for kc in range(16):
                        MM(kb, pb_[:, :], b3[:, kc, cc * 128:(cc + 1) * 128], h[:, kc, :], kc == 0, kc == 15, [bb, bh], [bpb])
                    ACT(kb, t[:], pa[:, :], AF.Silu, [bpa], [bt])
                    TT(kb, "dve", uT[:, fc, :], t[:], pb_[:, :], ALU.mult, [bt, bpb], [bu])
            for dq in range(8):
                w2, b2 = w2t[nw2 % 2], bw2[nw2 % 2]
                nw2 += 1
                kb.dma("pool", w2[:], w2s[:, :, dq * 256:(dq + 1) * 256], [], [b2])
                for tt in range(4):
                    p, bp = cx.ps[6 + (no % 2)], cx.pb[6 + (no % 2)]
                    t, bt = tmp[no % 2], btmp[no % 2]
                    no += 1
                    for fc in range(44):
                        MM(kb, p[:, 0:256], uT[:, fc, tt * 128:(tt + 1) * 128], w2[:, fc, :], fc == 0, fc == 43, [bu, b2], [bp])
                    TT(kb, "dve", t[:, 0:256], p[:, 0:256], cx.gb[l][1][:, dq * 256:(dq + 1) * 256], ALU.mult, [bp, cx.bgb[l][1]], [bt])
                    TT(kb, "dve", xr[:, tt, dq * 256:(dq + 1) * 256], xr[:, tt, dq * 256:(dq + 1) * 256], t[:, 0:256], ALU.add, [bt, bxr], [bxr])
            kb.dma("act", xd4[:, tb * 4:(tb + 1) * 4, :], xr[:], [bxr], [bxdst])


def phase_final(cx, xsrc, bxsrc, out, bout):
    nc, kb = cx.nc, cx.kb
    with ExitStack() as es:
        xs = [es.enter_context(nc.sbuf_tensor(U("fnx"), [128, D], F32)) for _ in range(2)]
        sq = [es.enter_context(nc.sbuf_tensor(U("fnq"), [128, D], F32)) for _ in range(2)]
        fg = es.enter_context(nc.sbuf_tensor(U("fng"), [128, D], F32))
        ss = es.enter_context(nc.sbuf_tensor(U("fns"), [128, 8], F32))
        bxs, bsq, bss = [kb.buf(), kb.buf()], [kb.buf(), kb.buf()], [kb.buf(), kb.buf()]
        bfg = kb.buf()
        kb.dma("sp", fg[:], cx.I["final_g_rep"], [], [bfg])
        for ti in range(NTH):
            u = ti % 2
            c0 = 4 * u
            kb.dma("sp", xs[u][:], xsrc[ti * 128:(ti + 1) * 128, :], [bxsrc], [bxs[u]])
            ACT(kb, sq[u][:], xs[u][:], AF.Square, [bxs[u]], [bsq[u]])
            kb.op("dve", lambda e: e.tensor_reduce(out=ss[:, c0:c0 + 1], in_=sq[u][:], axis=AX.X, op=ALU.add), [bsq[u]], [bss[u]])
            TS(kb, "dve", ss[:, c0 + 1:c0 + 2], ss[:, c0:c0 + 1], 1.0 / D, EPS, ALU.mult, ALU.add, [bss[u]], [bss[u]])
            ACT(kb, ss[:, c0 + 2:c0 + 3], ss[:, c0 + 1:c0 + 2], AF.Sqrt, [bss[u]], [bss[u]])
            RECIP(kb, ss[:, c0 + 3:c0 + 4], ss[:, c0 + 2:c0 + 3], [bss[u]], [bss[u]])
            STT(kb, sq[u][:], xs[u][:], ss[:, c0 + 3:c0 + 4], fg[:], ALU.mult, ALU.mult, [bxs[u], bss[u], bfg], [bsq[u]])
            kb.dma("pool", out[ti * 128:(ti + 1) * 128, :], sq[u][:], [bsq[u]], [bout])


def phase_lru(cx, l):
    nc, kb = cx.nc, cx.kb
    C = cx.cols
    bC = cx.bcols
    with ExitStack() as es:
        T = {nm: es.enter_context(nc.sbuf_tensor(U("lr" + nm), [128, S], F32)) for nm in ("ax", "ag", "xc", "r", "i", "t", "h", "hs")}
        B = {nm: kb.buf() for nm in T}
        ys = es.enter_context(nc.sbuf_tensor(U("lry"), [128, S], BF16))
        bys = kb.buf()
        wa = es.enter_context(nc.sbuf_tensor(U("lrwa"), [128, 128], F32))
        wx = es.enter_context(nc.sbuf_tensor(U("lrwx"), [128, 128], F32))
        cl = es.enter_context(nc.sbuf_tensor(U("lrcl"), [128, 4], F32))
        bwa, bwx, bcl = kb.buf(), kb.buf(), kb.buf()
        lam = C[:, CO["lru_lam"]:CO["lru_lam"] + 4]
        ACT(kb, cl[:], lam, AF.Exp, [bC], [bcl], scale=-1.0)
        ACT(kb, cl[:], cl[:], AF.Ln, [bcl], [bcl], bias=1.0)
        TS(kb, "dve", cl[:], cl[:], -8.0, None, ALU.mult, None, [bcl], [bcl])
        no = 0
        for c in range(NCK):
            ax, ag, xc, r, ii, t, h, hs = (T[k] for k in ("ax", "ag", "xc", "r", "i", "t", "h", "hs"))
            kb.dma("sp", ax[:], cx.pfm_d[c], [cx.bpfm], [B["ax"]])
            kb.dma("sp", ag[:], cx.pfm_d[NCK + c], [cx.bpfm], [B["ag"]])
            cw = lambda j: C[:, CO["lru_cw"] + 4 * c + j:CO["lru_cw"] + 4 * c + j + 1]
            ACT(kb, xc[:], ax[:], AF.Identity, [B["ax"], bC], [B["xc"]], scale=cw(2), bias=C[:, CO["lru_cb"] + c:CO["lru_cb"] + c + 1])
            STT(kb, xc[:, 2:S], ax[:, 0:S - 2], cw(0), xc[:, 2:S], ALU.mult, ALU.add, [B["ax"], B["xc"], bC], [B["xc"]])
            STT(kb, xc[:, 1:S], ax[:, 0:S - 1], cw(1), xc[:, 1:S], ALU.mult, ALU.add, [B["ax"], B["xc"], bC], [B["xc"]])
            STT(kb, xc[:, 0:S - 1], ax[:, 1:S], cw(3), xc[:, 0:S - 1], ALU.mult, ALU.add, [B["ax"], B["xc"], bC], [B["xc"]])
            for dr in range(2):
                kb.dma("sp", wa[:], cx.I["lru_bd"][l, dr, 0, c], [], [bwa])
                kb.dma("sp", wx[:], cx.I["lru_bd"][l, dr, 1, c], [], [bwx])
                ba = C[:, CO["lru_ba"] + 2 * dr + c:CO["lru_ba"] + 2 * dr + c + 1]
                bx = C[:, CO["lru_bx"] + 2 * dr + c:CO["lru_bx"] + 2 * dr + c + 1]
                for tb in range(8):
                    sl = slice(tb * 512, (tb + 1) * 512)
                    p1, bp1 = cx.ps[no % 4], cx.pb[no % 4]
                    p2, bp2 = cx.ps[(no + 1) % 4], cx.pb[(no + 1) % 4]
                    no += 2
                    MM(kb, p1[:, :], wa[:], xc[:, sl], True, True, [bwa, B["xc"]], [bp1])
                    MM(kb, p2[:, :], wx[:], xc[:, sl], True, True, [bwx, B["xc"]], [bp2])
                    ACT(kb, r[:, sl], p1[:, :], AF.Sigmoid, [bp1, bC], [B["r"]], bias=ba)
                    ACT(kb, ii[:, sl], p2[:, :], AF.Sigmoid, [bp2, bC], [B["i"]], bias=bx)
                ACT(kb, r[:], r[:], AF.Exp, [B["r"], bcl], [B["r"]], scale=cl[:, 2 * dr + c:2 * dr + c + 1])
                TT(kb, "dve", t[:], r[:], r[:], ALU.mult, [B["r"]], [B["t"]])
                TS(kb, "dve", t[:], t[:], -1.0, 1.0, ALU.mult, ALU.add, [B["t"]], [B["t"]])
                TS(kb, "dve", t[:], t[:], 0.0, None, ALU.max, None, [B["t"]], [B["t"]])
                ACT(kb, t[:], t[:], AF.Sqrt, [B["t"]], [B["t"]])
                TT(kb, "dve", t[:], t[:], ii[:], ALU.mult, [B["t"], B["i"]], [B["t"]])
                TT(kb, "dve", t[:], t[:], xc[:], ALU.mult, [B["t"], B["xc"]], [B["t"]])
                dst = hs if dr == 0 else h
                bdst = B["hs"] if dr == 0 else B["h"]
                if dr == 0:
                    kb.op("dve", lambda e: e.tensor_tensor_scan(out=dst[:], data0=r[:], data1=t[:], initial=0.0, op0=ALU.mult, op1=ALU.add), [B["r"], B["t"]], [bdst])
                else:
                    kb.op("dve", lambda e: e.tensor_tensor_scan(out=dst[:, ::-1], data0=r[:, ::-1], data1=t[:, ::-1], initial=0.0, op0=ALU.mult, op1=ALU.add), [B["r"], B["t"]], [bdst])
                    TT(kb, "dve", hs[:], hs[:], h[:], ALU.add, [B["hs"], B["h"]], [B["hs"]])
            ACT(kb, t[:], ag[:], AF.Square, [B["ag"]], [B["t"]])
            TS(kb, "dve", t[:], t[:], 0.044715, 1.0, ALU.mult, ALU.add, [B["t"]], [B["t"]])
            TT(kb, "dve", t[:], t[:], ag[:], ALU.mult, [B["t"], B["ag"]], [B["t"]])
            ACT(kb, t[:], t[:], AF.Sigmoid, [B["t"]], [B["t"]], scale=1.5957691216057308)
            TT(kb, "dve", t[:], t[:], ag[:], ALU.mult, [B["t"], B["ag"]], [B["t"]])
            TT(kb, "dve", ys[:], t[:], hs[:], ALU.mult, [B["t"], B["hs"]], [bys])
            kb.dma("pool", cx.yT_d[c], ys[:], [bys], [cx.byT])


def phase_att(cx, l):
    nc, kb = cx.nc, cx.kb
    C, bC = cx.cols, cx.bcols
    with ExitStack() as es:
        cos = es.enter_context(nc.sbuf_tensor(U("atc"), [128, S], F32))
        sin = es.enter_context(nc.sbuf_tensor(U("ats"), [128, S], F32))
        PT = es.enter_context(nc.sbuf_tensor(U("atP"), [128, 128], F32))
        BD = es.enter_context(nc.sbuf_tensor(U("atB"), [128, 128], F32))
        qk = [es.enter_context(nc.sbuf_tensor(U("atq"), [128, S], BF16)) for _ in range(3)]
        bqk = [kb.buf() for _ in range(3)]
        bk_ = kb.buf()
        kb.dma("sp", cos[:], cx.I["rope_cos"], [], [bk_])
        kb.dma("sp", sin[:], cx.I["rope_sin"], [], [bk_])
        kb.dma("sp", PT[:], cx.I["rope_PT"], [], [bk_])
        kb.dma("sp", BD[:], cx.I["bd64"], [], [bk_])
        no = 0
        with ExitStack() as es2:
            raw = [es2.enter_context(nc.sbuf_tensor(U("atr"), [128, S], F32)) for _ in range(2)]
            braw = [kb.buf(), kb.buf()]
            tq = [es2.enter_context(nc.sbuf_tensor(U("att"), [128, 512], F32)) for _ in range(4)]
            btq = [kb.buf() for _ in range(4)]
            for ci in range(3):
                ch = 4 + ci
                rw, brw = raw[ci % 2], braw[ci % 2]
                kb.dma("sp", rw[:], cx.pfm_d[ch], [cx.bpfm], [brw])
                gcol = C[:, CO["att_qg"]:CO["att_qg"] + 1] if ci < 2 else C[:, CO["att_kg"]:CO["att_kg"] + 1]
                qs = 0.125 if ci < 2 else 1.0
                for tb in range(8):
                    sl = slice(tb * 512, (tb + 1) * 512)
                    p1, bp1 = cx.ps[no % 4], cx.pb[no % 4]
                    p2, bp2 = cx.ps[(no + 1) % 4], cx.pb[(no + 1) % 4]
                    no += 2
                    t0, t1, t2, t3 = tq
                    ACT(kb, t0[:], rw[:, sl], AF.Square, [brw], [btq[0]])
                    MM(kb, p1[:, :], BD[:], t0[:], True, True, [bk_, btq[0]], [bp1])
                    ACT(kb, t1[:], p1[:, :], AF.Sqrt, [bp1], [btq[1]], scale=1.0 / 64.0, bias=cx.epsc[:, 0:1])
                    RECIP(kb, t1[:], t1[:], [btq[1]], [btq[1]])
                    TT(kb, "dve", t2[:], rw[:, sl], t1[:], ALU.mult, [brw, btq[1]], [btq[2]])
                    TS(kb, "dve", t2[:], t2[:], gcol, qs, ALU.mult, ALU.mult, [btq[2], bC], [btq[2]])
                    MM(kb, p2[:, :], PT[:], t2[:], True, True, [bk_, btq[2]], [bp2])
                    TT(kb, "dve", t3[:], p2[:, :], sin[:, sl], ALU.mult, [bp2, bk_], [btq[3]])
                    TT(kb, "pool", t2[:], t2[:], cos[:, sl], ALU.mult, [btq[2], bk_], [btq[2]])
                    TT(kb, "pool", qk[ci][:, sl], t2[:], t3[:], ALU.add, [btq[2], btq[3]], [bqk[ci]])
        kb.barrier()
        with ExitStack() as es3:
            vraw = es3.enter_context(nc.sbuf_tensor(U("atv"), [128, 32, 64], F32))
            va = [es3.enter_context(nc.sbuf_tensor(U("atva"), [128, 32, 128], BF16)) for _ in range(2)]
            eb = [es3.enter_context(nc.sbuf_tensor(U("ate"), [128, 512], BF16)) for _ in range(4)]
            rt = [es3.enter_context(nc.sbuf_tensor(U("atrt"), [128, 512], F32)) for _ in range(2)]
            ys = [es3.enter_context(nc.sbuf_tensor(U("aty"), [128, S], BF16)) for _ in range(2)]
            bv, bva, beb, brt, bys = kb.buf(), [kb.buf(), kb.buf()], [kb.buf() for _ in range(4)], [kb.buf(), kb.buf()], [kb.buf(), kb.buf()]
            kb.dma("sp", vraw[:], cx.tm_d["vtm"].rearrange("(n p) d -> p n d", p=128), [cx.btm["vtm"]], [bv])
            for kv in range(1):
                MEMSET(kb, "pool", va[kv][:], 1.0, [bva[kv]])
                CP(kb, "dve", va[kv][:, :, 0:64], vraw[:, :, kv * 64:(kv + 1) * 64], [bv], [bva[kv]])
            ne = 0
            pend = []

            def flush(keep):
                while len(pend) > keep:
                    (e_, be_, kv_, kc_, po_, bpo_, last, fin) = pend.pop(0)
                    MM(kb, po_[:, :], va[kv_][:, kc_, :], e_[:], kc_ == 0, last, [bva[kv_], be_], [bpo_])
                    if fin is not None:
                        fin()
            for hd in range(4):
                c, ph, kv = hd // 2, (hd % 2) * 64, 0
                q_, bq_ = qk[c], bqk[c]
                k_, bk2 = qk[2 + kv], bqk[2 + kv]
                y_, by_ = ys[c % 2], bys[c % 2]
                for qb in range(8):
                    po, bpo = cx.ps[4 + (qb % 2)], cx.pb[4 + (qb % 2)]
                    for kc in range(32):
                        p, bp = cx.ps[ne % 4], cx.pb[ne % 4]
                        e_, be_ = eb[ne % 4], beb[ne % 4]
                        ne += 1
                        MM(kb, p[:, :], k_[ph:ph + 64, kc * 128:(kc + 1) * 128], q_[ph:ph + 64, qb * 512:(qb + 1) * 512], True, True, [bk2, bq_], [bp])
                        ACT(kb, e_[:], p[:, :], AF.Exp, [bp], [be_])
                        fin = None
                        if kc == 31:
                            def fin(po=po, bpo=bpo, y_=y_, by_=by_, ph=ph, qb=qb, hd=hd, c=c):
                                r_, br_ = rt[qb % 2], brt[qb % 2]
                                RECIP(kb, r_[0:64, :], po[64:128, :], [bpo], [br_])
                                TT(kb, "dve", y_[ph:ph + 64, qb * 512:(qb + 1) * 512], po[0:64, :], r_[0:64, :], ALU.mult, [bpo, br_], [by_])
                                if hd % 2 == 1 and qb == 7:
                                    kb.dma("sp", cx.yT_d[NCK + c], y_[:], [by_], [cx.byT])
                        pend.append((e_, be_, kv, kc, po, bpo, kc == 31, fin))
                        flush(2)
            flush(0)


def phase_mlstm(cx, l):
    nc, kb = cx.nc, cx.kb
    with ExitStack() as es:
        def sb(name, shape, dt=F32):
            return es.enter_context(nc.sbuf_tensor(U(name), shape, dt))
        mlc = sb("mlc", [128, 4, 128])
        G = sb("mlG", [128, 32, 8])
        LF = sb("mlLF", [128, 32, 8])
        T1 = sb("mlT1", [128, 32, 8])
        BF_, BB_, BT_ = sb("mlBF", [128, 32, 8]), sb("mlBB", [128, 32, 8]), sb("mlBT", [128, 32, 8])
        BIAS = [sb("mlBI", [128, 32, 2]) for _ in range(2)]
        W = [sb("mlW", [128, 32, 2]) for _ in range(2)]
        EB = [sb("mlEB", [128, 32, 2]) for _ in range(2)]
        EBT = [sb("mlEBT", [128, 32, 2]) for _ in range(2)]
        gml = sb("mlg", [128, HW])
        bc, bG, bg = kb.buf(), kb.buf(), kb.buf()
        kb.dma("sp", mlc[:], cx.I["mlc"].rearrange("m p j -> p m j"), [], [bc])
        kb.dma("sp", G[:], cx.tm_d["g"].rearrange("(n p) c -> p n c", p=128), [cx.btm["g"]], [bG])
        kb.dma("sp", gml[:], cx.I["ml_g_rep"][l], [], [bg])
        STT(kb, T1[:], G[:], -1.0, G[:], ALU.mult, ALU.max, [bG], [bG])
        ACT(kb, T1[:], T1[:], AF.Exp, [bG], [bG], scale=-1.0)
        ACT(kb, T1[:], T1[:], AF.Ln, [bG], [bG], bias=1.0)
        TS(kb, "dve", LF[:], G[:], 0.0, None, ALU.min, None, [bG], [bG])
        TT(kb, "dve", LF[:], LF[:], T1[:], ALU.subtract, [bG], [bG])
        LF2 = LF[:].rearrange("p n c -> p (n c)")
        for mi, dst in ((0, BF_), (1, BB_), (None, BT_)):
            p, bp = cx.ps[0], cx.pb[0]
            lhs = mlc[:, mi, :] if mi is not None else cx.ones[:]
            MM(kb, p[:, 0:256], lhs, LF2, True, True, [bc, bG, cx.bconst], [bp])
            CP(kb, "dve", dst[:].rearrange("p n c -> p (n c)"), p[:, 0:256], [bp], [bG])
        for dr in range(2):
            Bx = BF_ if dr == 0 else BB_
            li = G[:, :, 4 * dr:4 * dr + 2]
            b4 = Bx[:, :, 4 * dr + 2:4 * dr + 4]
            bt4 = BT_[:, :, 4 * dr + 2:4 * dr + 4]
            TT(kb, "dve", BIAS[dr][:], li, b4, ALU.subtract, [bG], [bG])
            TT(kb, "dve", W[dr][:], bt4, BIAS[dr][:], ALU.add, [bG], [bG])
            ACT(kb, W[dr][:], W[dr][:], AF.Exp, [bG], [bG])
            ACT(kb, EB[dr][:], b4, AF.Exp, [bG], [bG])
            ACT(kb, EBT[dr][:], bt4, AF.Exp, [bG], [bG])
        raw = sb("mlraw", [128, S])
        qb = sb("mlq", [128, S], BF16)
        kbf = sb("mlk", [128, S], BF16)
        ktok = sb("mlkt", [128, 32, 128])
        vtok = sb("mlvt", [128, 32, 128])
        vaug = sb("mlva", [128, 32, 129], BF16)
        hF, hB = sb("mlhF", [128, 32, 128]), sb("mlhB", [128, 32, 128])
        ys = sb("mlys", [128, S], BF16)
        dg = [sb("mldg", [128, 128]) for _ in range(2)]
        DT = [sb("mlDT", [128, 128]) for _ in range(2)]
        PTt = [sb("mlPT", [128, 128], BF16) for _ in range(2)]
        ins = [sb("mlin", [128, 129]) for _ in range(2)]
        tot = [sb("mltot", [128, 129]) for _ in range(2)]
        den = [sb("mlden", [128, 2]) for _ in range(2)]
        kp = [sb("mlkp", [128, 128], BF16) for _ in range(2)]
        Cst = [sb("mlC", [128, 129]) for _ in range(2)]
        Cbf = [sb("mlCb", [128, 129], BF16) for _ in range(2)]
        rst = sb("mlrs", [128, 64])
        braw, bq, bk, bkt, bvt, bva, bys, brs = [kb.buf() for _ in range(8)]
        bh = [kb.buf(), kb.buf()]
        bdg, bDT, bPT, bin_, btot, bden, bkp = [[kb.buf(), kb.buf()] for _ in range(7)]
        bC, bCb = [kb.buf(), kb.buf()], [kb.buf(), kb.buf()]
        nps = [0]

        def nxt():
            i = nps[0] % 8
            nps[0] += 1
            return cx.ps[i], cx.pb[i]
        for hd in range(2):
            kb.dma("sp", raw[:], cx.pfm_d[13 + hd], [cx.bpfm], [braw])
            ACT(kb, qb[:], raw[:], AF.Identity, [braw], [bq], scale=128.0 ** -0.5)
            kb.dma("sp", raw[:], cx.pfm_d[15 + hd], [cx.bpfm], [braw])
            ACT(kb, kbf[:], raw[:], AF.Identity, [braw], [bk])
            kb.dma("sp", ktok[:], cx.tm_d["dk"].rearrange("(n p) d -> p n d", p=128)[:, :, hd * 128:(hd + 1) * 128], [cx.btm["dk"]], [bkt])
            kb.dma("sp", vtok[:], cx.tm_d["dv"].rearrange("(n p) d -> p n d", p=128)[:, :, hd * 128:(hd + 1) * 128], [cx.btm["dv"]], [bvt])
            MEMSET(kb, "pool", vaug[:], 1.0, [bva])
            CP(kb, "dve", vaug[:, :, 0:128], vtok[:], [bvt], [bva])
            for dr in range(2):
                MEMSET(kb, "dve", Cst[dr][:], 0.0, [bC[dr]])
                MEMSET(kb, "dve", Cbf[dr][:], 0.0, [bCb[dr]])
            for i in range(32):
                for dr in range(2):
                    n = i if dr == 0 else 31 - i
                    u = dr
                    cs = slice(n * 128, (n + 1) * 128)
                    Bx = BF_ if dr == 0 else BB_
                    lfc = 4 * dr + 2 + hd
                    hacc, bhh = (hF, bh[0]) if dr == 0 else (hB, bh[1])
                    TS(kb, "pool", dg[u][:], cx.ident[:], Bx[:, n, lfc:lfc + 1], None, ALU.mult, None, [cx.bconst, bG], [bdg[u]])
                    pB, bpB = nxt()
                    MM(kb, pB[:, 0:128], cx.ones[:], dg[u][:], True, False, [cx.bconst, bdg[u]], [bpB])
                    MM(kb, pB[:, 0:128], cx.ident[:], mlc[:, 2 + dr, :], False, True, [cx.bconst, bc], [bpB])
                    ACT(kb, DT[u][:], pB[:, 0:128], AF.Exp, [bpB, bG], [bDT[u]], bias=BIAS[dr][:, n, hd:hd + 1])
                    pS, bpS = nxt()
                    MM(kb, pS[:, 0:128], kbf[:, cs], qb[:, cs], True, True, [bk, bq], [bpS])
                    TT(kb, "dve", PTt[u][:], pS[:, 0:128], DT[u][:], ALU.mult, [bpS, bDT[u]], [bPT[u]])
                    pI, bpI = nxt()
                    MM(kb, pI[:, 0:129], PTt[u][:], vaug[:, n, :], True, True, [bPT[u], bva], [bpI])
                    pN, bpN = nxt()
                    MM(kb, pN[:, 0:129], qb[:, cs], Cbf[dr][:], True, True, [bq, bCb[dr]], [bpN])
                    ACT(kb, ins[u][:], pN[:, 0:129], AF.Identity, [bpN, bG], [bin_[u]], scale=EB[dr][:, n, hd:hd + 1])
                    TT(kb, "dve", tot[u][:], pI[:, 0:129], ins[u][:], ALU.add, [bpI, bin_[u]], [btot[u]])
                    STT(kb, den[u][:, 0:1], tot[u][:, 128:129], -1.0, tot[u][:, 128:129], ALU.mult, ALU.max, [btot[u]], [bden[u]])
                    TS(kb, "dve", den[u][:, 0:1], den[u][:, 0:1], 1.0, None, ALU.max, None, [bden[u]], [bden[u]])
                    RECIP(kb, den[u][:, 1:2], den[u][:, 0:1], [bden[u]], [bden[u]])
                    TS(kb, "dve", hacc[:, n, :], tot[u][:, 0:128], den[u][:, 1:2], None, ALU.mult, None, [btot[u], bden[u]], [bhh])
                    ACT(kb, kp[u][:], ktok[:, n, :], AF.Identity, [bkt, bG], [bkp[u]], scale=W[dr][:, n, hd:hd + 1])
                    pC, bpC = nxt()
                    MM(kb, pC[:, 0:129], kp[u][:], vaug[:, n, :], True, True, [bkp[u], bva], [bpC])
                    STT(kb, Cst[dr][:], Cst[dr][:], EBT[dr][:, n, hd:hd + 1], pC[:, 0:129], ALU.mult, ALU.add, [bC[dr], bG, bpC], [bC[dr]])
                    CP(kb, "pool", Cbf[dr][:], Cst[dr][:], [bC[dr]], [bCb[dr]])
            TT(kb, "dve", hF[:], hF[:], hB[:], ALU.add, [bh[0], bh[1]], [bh[0]])
            ACT(kb, hB[:], hF[:], AF.Square, [bh[0]], [bh[1]])
            kb.op("dve", lambda e: e.tensor_reduce(out=rst[:, 0:32], in_=hB[:], axis=AX.X, op=ALU.add), [bh[1]], [brs])
            TS(kb, "dve", rst[:, 0:32], rst[:, 0:32], 1.0 / 128.0, EPS, ALU.mult, ALU.add, [brs], [brs])
            ACT(kb, rst[:, 0:32], rst[:, 0:32], AF.Sqrt, [brs], [brs])
            RECIP(kb, rst[:, 32:64], rst[:, 0:32], [brs], [brs])
            kb.dma("sp", vtok[:], cx.tm_d["do"].rearrange("(n p) d -> p n d", p=128)[:, :, hd * 128:(hd + 1) * 128], [cx.btm["do"]], [bvt])
            ACT(kb, vtok[:], vtok[:], AF.Sigmoid, [bvt], [bvt])
            for n in range(32):
                STT(kb, hF[:, n, :], hF[:, n, :], rst[:, 32 + n:33 + n], gml[:, hd * 128:(hd + 1) * 128], ALU.mult, ALU.mult, [bh[0], brs, bg], [bh[0]])
            TT(kb, "dve", hF[:], hF[:], vtok[:], ALU.mult, [bh[0], bvt], [bh[0]])
            for n4 in range(8):
                p, bp = nxt()
                for j in range(4):
                    n = 4 * n4 + j
                    TR(kb, p[:, j * 128:(j + 1) * 128], hF[:, n, :], cx.ident[:], [bh[0], cx.bconst], [bp])
                CP(kb, "dve", ys[:, n4 * 512:(n4 + 1) * 512], p[:, :], [bp], [bys])
            kb.dma("sp", cx.yT_d[6 + hd], ys[:], [bys], [cx.byT])


def phase_hyena(cx, l):
    nc, kb = cx.nc, cx.kb
    C_, bC_ = cx.cols, cx.bcols
    zview = cx.ztok_d.rearrange("(n p) c -> p n c", p=128)
    nps = [0]

    def nxt():
        i = nps[0] % 6
        nps[0] += 1
        return cx.ps[i], cx.pb[i]
    with ExitStack() as es0:
        utok = es0.enter_context(nc.sbuf_tensor(U("hyu"), [128, 32, HW], BF16))
        RS = es0.enter_context(nc.sbuf_tensor(U("hyRS"), [128, 2 * HW], F32))
        wfc = es0.enter_context(nc.sbuf_tensor(U("hywf"), [128, NF], F32))
        skr = es0.enter_context(nc.sbuf_tensor(U("hysk"), [128, 2 * HW], F32))
        bu, bRS, bwf = kb.buf(), kb.buf(), kb.buf()
        kb.dma("sp", wfc[:], cx.I["dft_wf"], [], [bwf])
        kb.dma("sp", skr[:], cx.I["hy_skip_rep"][l], [], [bwf])
        with ExitStack() as es:
            raw = [es.enter_context(nc.sbuf_tensor(U("hyraw"), [128, S], F32)) for _ in range(2)]
            zc = [es.enter_context(nc.sbuf_tensor(U("hyz"), [128, S], F32)) for _ in range(2)]
            st = [es.enter_context(nc.sbuf_tensor(U("hyst"), [128, 32, 128], F32)) for _ in range(2)]
            braw, bz, bst = [kb.buf(), kb.buf()], [kb.buf(), kb.buf()], [kb.buf(), kb.buf()]
            for ch in range(6):
                u = ch % 2
                kb.dma("sp", raw[u][:], cx.pfm_d[7 + ch], [cx.bpfm], [braw[u]])
                cw = lambda j: C_[:, CO["hy_cw"] + 3 * ch + j:CO["hy_cw"] + 3 * ch + j + 1]
                ACT(kb, zc[u][:], raw[u][:], AF.Identity, [braw[u], bC_], [bz[u]], scale=cw(1), bias=C_[:, CO["hy_cb"] + ch:CO["hy_cb"] + ch + 1])
                STT(kb, zc[u][:, 1:S], raw[u][:, 0:S - 1], cw(0), zc[u][:, 1:S], ALU.mult, ALU.add, [braw[u], bz[u], bC_], [bz[u]])
                STT(kb, zc[u][:, 0:S - 1], raw[u][:, 1:S], cw(2), zc[u][:, 0:S - 1], ALU.mult, ALU.add, [braw[u], bz[u], bC_], [bz[u]])
                for n4 in range(8):
                    p, bp = nxt()
                    for j in range(4):
                        n = 4 * n4 + j
                        TR(kb, p[:, j * 128:(j + 1) * 128], zc[u][:, n * 128:(n + 1) * 128], cx.ident[:], [bz[u], cx.bconst], [bp])
                    CP(kb, "dve", st[u][:, 4 * n4:4 * n4 + 4, :], p[:, :].rearrange("p (j c) -> p j c", j=4), [bp], [bst[u]])
                    if ch < 2:
                        CP(kb, "pool", utok[:, 4 * n4:4 * n4 + 4, ch * 128:(ch + 1) * 128], st[u][:, 4 * n4:4 * n4 + 4, :], [bst[u]], [bu])
                kb.dma("act", zview[:, :, ch * 128:(ch + 1) * 128], st[u][:], [bst[u]], [cx.bztok])
        kb.barrier()
        with ExitStack() as es:
            def sb(name, shape, dt=F32):
                return es.enter_context(nc.sbuf_tensor(U(name), shape, dt))
            feat = sb("hyfe", [128, S])
            h1 = sb("hyh1", [128, S])
            h2 = sb("hyh2", [128, S])
            w1s, w2s, w3s = sb("hyw1", [128, 64]), sb("hyw2", [128, 64]), sb("hyw3", [128, 4 * HW])
            adec = sb("hyad", [128, 4 * HW])
            tcol = sb("hytc", [128, 32])
            sfb = sb("hysfb", [128, 2])
            arg = [sb("hyarg", [128, 512]) for _ in range(2)]
            E = [sb("hyE", [128, 512]) for _ in range(2)]
            hq = [sb("hyhq", [128, 512]) for _ in range(4)]
            sq = [sb("hysq", [128, 512]) for _ in range(2)]
            gst = [sb("hygs", [128, 2, 2 * HW], BF16) for _ in range(2)]
            bk, bh1, bh2, bsfb = kb.buf(), kb.buf(), kb.buf(), kb.buf()
            barg, bE, bsq, bgst = [kb.buf(), kb.buf()], [kb.buf(), kb.buf()], [kb.buf(), kb.buf()], [kb.buf(), kb.buf()]
            bhq = [kb.buf() for _ in range(4)]
            kb.dma("sp", feat[0:17, :], cx.I["hy_feat"], [], [bk])
            kb.dma("sp", w1s[0:17, :], cx.I["hy_w1"][l], [], [bk])
            kb.dma("sp", w2s[0:64, :], cx.I["hy_w2"][l], [], [bk])
            kb.dma("sp", w3s[0:64, :], cx.I["hy_w3"][l], [], [bk])
            kb.dma("sp", adec[:], cx.I["hy_decay_rep"][l], [], [bk])
            kb.dma("sp", tcol[:], cx.I["hy_negt"], [], [bk])
            ACT(kb, adec[:], adec[:], AF.Abs, [bk], [bk])
            sf = C_[0:64, CO["hy_sf"]:CO["hy_sf"] + 1]
            TT(kb, "dve", sfb[0:64, 0:1], C_[0:64, CO["hy_b1"]:CO["hy_b1"] + 1], sf, ALU.mult, [bC_], [bsfb])
            TT(kb, "dve", sfb[0:64, 1:2], C_[0:64, CO["hy_b2"]:CO["hy_b2"] + 1], sf, ALU.mult, [bC_], [bsfb])
            for li, (wsrc, K, src, bsrc, dst, bdst) in enumerate(((w1s, 17, feat, bk, h1, bh1), (w2s, 64, h1, bh1, h2, bh2))):
                for tb in range(8):
                    sl = slice(tb * 512, (tb + 1) * 512)
                    p, bp = nxt()
                    a, ba = arg[tb % 2], barg[tb % 2]
                    MM(kb, p[0:64, :], wsrc[0:K, 0:64], src[0:K, sl], True, True, [bk, bsrc], [bp])
                    ACT(kb, a[0:64, :], p[0:64, :], AF.Identity, [bp, bC_, bsfb], [ba], scale=sf, bias=sfb[0:64, li:li + 1])
                    m_ = E[tb % 2]
                    bm_ = bE[tb % 2]
                    for _rep in range(2):
                        TS(kb, "dve", m_[0:64, :], a[0:64, :], PI, 2 * PI, ALU.is_gt, ALU.mult, [ba], [bm_])
                        TT(kb, "dve", a[0:64, :], a[0:64, :], m_[0:64, :], ALU.subtract, [ba, bm_], [ba])
                        TS(kb, "dve", m_[0:64, :], a[0:64, :], -PI, 2 * PI, ALU.is_lt, ALU.mult, [ba], [bm_])
                        TT(kb, "dve", a[0:64, :], a[0:64, :], m_[0:64, :], ALU.add, [ba, bm_], [ba])
                    ACT(kb, dst[0:64, sl], a[0:64, :], AF.Sin, [ba], [bdst])
            pss = [(cx.ps[6], cx.pb[6]), (cx.ps[7], cx.pb[7])]
            for n in range(32):
                g, bg = gst[n % 2], bgst[n % 2]
                for q in range(4):
                    o, dr = q // 2, q % 2
                    p, bp = nxt()
                    e_, be_ = E[q % 2], bE[q % 2]
                    s_, bs_ = sq[q % 2], bsq[q % 2]
                    MM(kb, p[:, 0:HW], h2[0:64, n * 128:(n + 1) * 128], w3s[0:64, q * HW:(q + 1) * HW], True, True, [bh2, bk], [bp])
                    ACT(kb, e_[:, 0:HW], adec[:, q * HW:(q + 1) * HW], AF.Exp, [bk], [be_], scale=tcol[:, n:n + 1])
                    TT(kb, "dve", hq[q][:, 0:HW], p[:, 0:HW], e_[:, 0:HW], ALU.mult, [bp, be_], [bhq[q]])
                    ACT(kb, s_[:, 0:HW], hq[q][:, 0:HW], AF.Square, [bhq[q]], [bs_])
                    MM(kb, pss[o][0][:, 0:HW], cx.ones[:], s_[:, 0:HW], (n == 0 and dr == 0), (n == 31 and dr == 1), [cx.bconst, bs_], [pss[o][1]])
                    if n == 0 and dr == 1:
                        MEMSET(kb, "dve", hq[q][0:1, 0:HW], 0.0, [bhq[q]])
                for o in range(2):
                    TT(kb, "pool", g[:, 0, o * HW:(o + 1) * HW], hq[2 * o][:, 0:HW], hq[2 * o + 1][:, 0:HW], ALU.add, [bhq[2 * o], bhq[2 * o + 1]], [bg])
                    TT(kb, "pool", g[:, 1, o * HW:(o + 1) * HW], hq[2 * o + 1][:, 0:HW], hq[2 * o][:, 0:HW], ALU.subtract, [bhq[2 * o], bhq[2 * o + 1]], [bg])
                kb.dma("sp", cx.G_d.rearrange("r t c -> t r c")[n * 128:(n + 1) * 128], g[:], [bg], [cx.bG_d])
            for o in range(2):
                ACT(kb, RS[:, o * HW:(o + 1) * HW], pss[o][0][:, 0:HW], AF.Sqrt, [pss[o][1]], [bRS], bias=cx.epsc[:, 0:1])
            RECIP(kb, RS[:], RS[:], [bRS], [bRS])
        kb.barrier()
        with ExitStack() as es:
            Gs = es.enter_context(nc.sbuf_tensor(U("hyG"), [128, 32, 2 * HW], BF16))
            ct = [es.enter_context(nc.sbuf_tensor(U("hyct"), [128, NF, 128], BF16)) for _ in range(2)]
            ho = [es.enter_context(nc.sbuf_tensor(U("hyho"), [128, 2 * HW], F32)) for _ in range(2)]
            bGs, bct, bho = kb.buf(), [kb.buf(), kb.buf()], [kb.buf(), kb.buf()]
            k = 0
            for ri, blk in enumerate(("dft_c", "dft_s")):
                kb.dma("sp", Gs[:], cx.G_d[ri].rearrange("(n p) c -> p n c", p=128), [cx.bG_d], [bGs])
                for fc in range(NF):
                    c_, bc_ = ct[k % 2], bct[k % 2]
                    h_, bh_ = ho[k % 2], bho[k % 2]
                    k += 1
                    kb.dma("sp", c_[:], cx.I[blk][fc], [], [bc_])
                    for o in range(2):
                        p, bp = nxt()
                        for dc in range(32):
                            MM(kb, p[:, 0:HW], c_[:, dc, :], Gs[:, dc, o * HW:(o + 1) * HW], dc == 0, dc == 31, [bc_, bGs], [bp])
                        STT(kb, h_[:, o * HW:(o + 1) * HW], p[:, 0:HW], wfc[:, fc:fc + 1], RS[:, o * HW:(o + 1) * HW], ALU.mult, ALU.mult, [bp, bwf, bRS], [bh_])
                    kb.dma("act", cx.H_d[ri, fc], h_[:], [bh_], [cx.bH_d])
        kb.barrier()
        with ExitStack() as es:
            def sb(name, shape, dt=F32):
                return es.enter_context(nc.sbuf_tensor(U(name), shape, dt))
            YA = sb("hyYA", [128, NF, HW], BF16)
            YB = sb("hyYB", [128, NF, HW], BF16)
            ct = [sb("hyc2", [128, NF, 128], BF16) for _ in range(2)]
            stt = [sb("hys2", [128, NF, 128], BF16) for _ in range(2)]
            Hr = [sb("hyHr", [128, HW]) for _ in range(2)]
            Hi = [sb("hyHi", [128, HW]) for _ in range(2)]
            tt = [sb("hyt", [128, HW]) for _ in range(4)]
            uu = [sb("hyuu", [128, HW]) for _ in range(2)]
            xx = [sb("hyxx", [128, HW]) for _ in range(2)]
            zz = [sb("hyzz", [128, HW]) for _ in range(2)]
            yst = [sb("hyys", [128, 2, 128], BF16) for _ in range(2)]
            bYA, bYB = kb.buf(), kb.buf()
            bct, bstt, bHr, bHi, buu, bxx, bzz, byst = [[kb.buf(), kb.buf()] for _ in range(8)]
            btt = [kb.buf() for _ in range(4)]
            k = 0
            for o in range(2):
                for fc in range(NF):
                    c_, bc_ = ct[k % 2], bct[k % 2]
                    s_, bs_ = stt[k % 2], bstt[k % 2]
                    hr, bhr = Hr[k % 2], bHr[k % 2]
                    hi, bhi = Hi[k % 2], bHi[k % 2]
                    k += 1
                    kb.dma("sp", c_[:], cx.I["dft_c"][fc], [], [bc_])
                    kb.dma("sp", s_[:], cx.I["dft_s"][fc], [], [bs_])
                    kb.dma("sp", hr[:], cx.H_d[0, fc, :, o * HW:(o + 1) * HW], [cx.bH_d], [bhr])
                    kb.dma("sp", hi[:], cx.H_d[1, fc, :, o * HW:(o + 1) * HW], [cx.bH_d], [bhi])
                    pA, bpA = nxt()
                    pB, bpB = nxt()
                    for dc in range(32):
                        MM(kb, pA[:, 0:HW], c_[:, dc, :], utok[:, dc, :], dc == 0, dc == 31, [bc_, bu], [bpA])
                    for dc in range(32):
                        MM(kb, pB[:, 0:HW], s_[:, dc, :], utok[:, dc, :], dc == 0, dc == 31, [bs_, bu], [bpB])
                    TT(kb, "dve", tt[0][:], pA[:, 0:HW], hr[:], ALU.mult, [bpA, bhr], [btt[0]])
                    TT(kb, "dve", tt[1][:], pB[:, 0:HW], hi[:], ALU.mult, [bpB, bhi], [btt[1]])
                    TT(kb, "dve", tt[2][:], pB[:, 0:HW], hr[:], ALU.mult, [bpB, bhr], [btt[2]])
                    TT(kb, "dve", tt[3][:], pA[:, 0:HW], hi[:], ALU.mult, [bpA, bhi], [btt[3]])
                    TT(kb, "pool", YA[:, fc, :], tt[0][:], tt[1][:], ALU.add, [btt[0], btt[1]], [bYA])
                    TT(kb, "pool", YB[:, fc, :], tt[2][:], tt[3][:], ALU.subtract, [btt[2], btt[3]], [bYB])
                for n in range(32):
                    c_, bc_ = ct[k % 2], bct[k % 2]
                    s_, bs_ = stt[k % 2], bstt[k % 2]
                    u_, bu_ = uu[k % 2], buu[k % 2]
                    x_, bx_ = xx[k % 2], bxx[k % 2]
                    z_, bz_ = zz[k % 2], bzz[k % 2]
                    k += 1
                    kb.dma("sp", c_[:], cx.I["dft_c"][n], [], [bc_])
                    kb.dma("sp", s_[:], cx.I["dft_s"][n], [], [bs_])
                    if o == 0:
                        kb.dma("sp", u_[:], cx.ztok_d[n * 128:(n + 1) * 128, 0:HW], [cx.bztok], [bu_])
                    else:
                        kb.dma("sp", u_[:], cx.z1_d[n * 128:(n + 1) * 128, :], [cx.bz1], [bu_])
                    kb.dma("sp", x_[:], cx.ztok_d[n * 128:(n + 1) * 128, HW * (o + 1):HW * (o + 2)], [cx.bztok], [bx_])
                    p, bp = nxt()
                    for fc in range(NF):
                        MM(kb, p[:, 0:HW], c_[:, fc, :], YA[:, fc, :], fc == 0, False, [bc_, bYA], [bp])
                    for fc in range(NF):
                        MM(kb, p[:, 0:HW], s_[:, fc, :], YB[:, fc, :], False, fc == NF - 1, [bs_, bYB], [bp])
                    TT(kb, "pool", u_[:], u_[:], skr[:, o * HW:(o + 1) * HW], ALU.mult, [bu_, bwf], [bu_])
                    TT(kb, "dve", z_[:], p[:, 0:HW], u_[:], ALU.add, [bp, bu_], [bz_])
                    TT(kb, "dve", z_[:], z_[:], x_[:], ALU.mult, [bz_, bx_], [bz_])
                    if o == 0:
                        kb.dma("act", cx.z1_d[n * 128:(n + 1) * 128, :], z_[:], [bz_], [cx.bz1])
                        CP(kb, "pool", utok[:, n, :], z_[:], [bz_], [bu])
                    else:
                        ys_, bys_ = yst[n % 2], byst[n % 2]
                        pt, bpt = nxt()
                        for j in range(2):
                            TR(kb, pt[:, j * 128:(j + 1) * 128], z_[:, j * 128:(j + 1) * 128], cx.ident[:], [bz_, cx.bconst], [bpt])
                        CP(kb, "dve", ys_[:], pt[:, 0:256].rearrange("p (j c) -> p j c", j=2), [bpt], [bys_])
                        kb.dma("act", cx.yT_d[4:6].rearrange("c p t -> p c t")[:, :, n * 128:(n + 1) * 128], ys_[:], [bys_], [cx.byT])
                if o == 0:
                    kb.barrier()


def kernel(**inputs):
    inp = {k: np.asarray(v) for k, v in inputs.items()}
    nc = build({"full"})
    shared = host_shared(inp)
    halves = [host_half(inp, 0), host_half(inp, 1)]
    in_maps = [host_inputs(inp, c, shared, halves) for c in range(8)]
    res = run_bass_kernel_spmd(nc, in_maps, core_ids=list(range(8)))
    out = np.zeros((4, S, D), np.float32)
    for c in range(8):
        out[c // 2, (c % 2) * SH:(c % 2 + 1) * SH] = np.asarray(res.results[c]["out"])
    return out
```

```python
import math
import ml_dtypes
from contextlib import ExitStack
from concourse.bass_utils import run_bass_kernel_spmd
import numpy as np
import concourse.bass as bass
import concourse.mybir as mybir

F32 = mybir.dt.float32
BF16 = mybir.dt.bfloat16
AF = mybir.ActivationFunctionType
ALU = mybir.AluOpType
AX = mybir.AxisListType


NPOOL = 84


class Buf:
    __slots__ = ("name", "lw", "rd", "sem", "cnt")

    def __init__(self, name=""):
        self.name = name
        self.lw = None
        self.rd = {}
        self.sem = None
        self.cnt = 0


class KB:
    def __init__(self, nc):
        self.nc = nc
        self.eng = {"pe": nc.tensor, "dve": nc.vector, "act": nc.scalar,
                    "pool": nc.gpsimd, "sp": nc.sync}
        self.sems = {}
        self.cnt = {}
        for k in self.eng:
            self.sems[k] = nc.alloc_semaphore("c_" + k)
            self.cnt[k] = 0
        self.waited = {k: {} for k in self.eng}
        self.ndma = 0
        self.nbuf = 0
        self.dmacnt = {}
        self.bufs = []
        self.pool = [nc.alloc_semaphore("d%d" % i) for i in range(NPOOL)]
        for h in list(self.sems.values()) + self.pool:
            nc.gpsimd.sem_clear(h)
        nc.all_engine_barrier()

    def buf(self, name=""):
        self.nbuf += 1
        b = Buf(name or ("b%d" % self.nbuf))
        self.bufs.append(b)
        return b

    def _wait(self, e, reads, writes):
        need = {}
        for b in reads:
            if b.lw is not None:
                k, v = b.lw
                if need.get(k, 0) < v:
                    need[k] = v
        for b in writes:
            if b.lw is not None:
                k, v = b.lw
                if need.get(k, 0) < v:
                    need[k] = v
            for k, v in b.rd.items():
                if need.get(k, 0) < v:
                    need[k] = v
        w = self.waited[e]
        for k, v in need.items():
            if k == e and (e == "pe" or v > self.cnt[e]):
                continue
            if w.get(k, 0) < v:
                self.eng[e].wait_ge(self.sems[k], v)
                w[k] = v

    def op(self, e, fn, reads=(), writes=(), inc=True):
        self._wait(e, reads, writes)
        ins = fn(self.eng[e])
        if inc:
            self.cnt[e] += 1
            ins.then_inc(self.sems[e], 1)
            tok = (e, self.cnt[e])
        else:
            tok = (e, self.cnt[e] + 1)
        for b in writes:
            b.lw = tok
            b.rd = {}
        for b in reads:
            if b.rd.get(tok[0], 0) < tok[1]:
                b.rd[tok[0]] = tok[1]
        return ins

    def dma(self, q, out, in_, reads, writes, **kw):
        self._wait(q, reads, writes)
        wb = writes[0]
        if wb.sem is None:
            key = "d%d" % self.ndma
            self.sems[key] = self.pool[self.ndma]
            self.ndma += 1
            wb.sem = key
        self.dmacnt[wb.sem] = self.dmacnt.get(wb.sem, 0) + 16
        self.eng[q].dma_start(out=out, in_=in_, **kw).then_inc(self.sems[wb.sem], 16)
        tok = (wb.sem, self.dmacnt[wb.sem])
        for b in writes:
            b.lw = tok
            b.rd = {}
        for b in reads:
            if b.rd.get(tok[0], 0) < tok[1]:
                b.rd[tok[0]] = tok[1]

    def barrier(self):
        cur = dict(self.cnt)
        cur.update(self.dmacnt)
        for e in self.eng:
            w = self.waited[e]
            for k, v in cur.items():
                if v <= 0 or (k == e and e == "pe"):
                    continue
                if w.get(k, 0) < v:
                    self.eng[e].wait_ge(self.sems[k], v)
                    w[k] = v
        for b in self.bufs:
            b.lw = None
            b.rd = {}
            b.sem = None
        self.ndma = 0

    def finish(self, bufs):
        self._wait("sp", bufs, [])


S = 4096
D = 2048
NT = 32
DIN = 5392
DFF = 5632
EPS = 1e-6
NF = 33
NDFT = 8192
PI = math.pi

SH = S // 2
NTH = SH // 128
NCK = 2
HW = 256


def half_cols(r):
    c = []
    c += list(range(0 + HW * r, 0 + HW * r + HW))
    c += list(range(512 + HW * r, 512 + HW * r + HW))
    c += list(range(1024 + HW * r, 1024 + HW * r + HW))
    k = list(range(1536 + 64 * r, 1536 + 64 * r + 64))
    c += k + k
    for seg in range(3):
        c += list(range(1792 + 512 * seg + HW * r, 1792 + 512 * seg + HW * r + HW))
    c += list(range(3328 + HW * r, 3328 + HW * r + HW))
    c += list(range(3840 + HW * r, 3840 + HW * r + HW))
    nfm = len(c)
    c += list(range(1664 + 64 * r, 1664 + 64 * r + 64))
    c += list(range(3840 + HW * r, 3840 + HW * r + HW))
    c += list(range(4352 + HW * r, 4352 + HW * r + HW))
    c += list(range(4864 + HW * r, 4864 + HW * r + HW))
    for g in range(4):
        c += [5376 + 4 * g + 2 * r, 5376 + 4 * g + 2 * r + 1]
    return np.array(c), nfm


NFM = 17
TM = [("vtm", 2176, 64), ("dk", 2240, 256), ("dv", 2496, 256), ("do", 2752, 256), ("g", 3008, 8)]
NCH = 3016
TMOFF = {}
_o = 0
for nm, lo, n in TM:
    TMOFF[nm] = _o
    _o += n
NTM = _o


CO = {}
_o = 0
for nm, n in (("lru_cw", 8), ("lru_cb", 2), ("lru_ba", 4), ("lru_bx", 4), ("lru_lam", 4), ("att_qg", 1), ("att_kg", 1),
              ("hy_cw", 18), ("hy_cb", 6), ("hy_b1", 1), ("hy_b2", 1), ("hy_sf", 1)):
    CO[nm] = _o
    _o += n
NCOL = _o


class Ctx:
    pass


_UC = [0]


def U(name):
    _UC[0] += 1
    return "%s_%d" % (name, _UC[0])


def ACT(kb, out, in_, func, rd, wr, **kw):
    return kb.op("act", lambda e: e.activation(out=out, in_=in_, func=func, **kw), rd, wr)


def MM(kb, out, lhsT, rhs, start, stop, rd, wr):
    return kb.op("pe", lambda e: e.matmul(out, lhsT=lhsT, rhs=rhs, start=start, stop=stop), rd, wr, inc=stop)


def TR(kb, out, in_, ident, rd, wr):
    return kb.op("pe", lambda e: e.transpose(out=out, in_=in_, identity=ident), rd, wr)


def TT(kb, eng, out, in0, in1, op, rd, wr):
    return kb.op(eng, lambda e: e.tensor_tensor(out=out, in0=in0, in1=in1, op=op), rd, wr)


def TS(kb, eng, out, in0, s1, s2, op0, op1, rd, wr):
    if op1 is None:
        return kb.op(eng, lambda e: e.tensor_scalar(out=out, in0=in0, scalar1=s1, scalar2=None, op0=op0), rd, wr)
    return kb.op(eng, lambda e: e.tensor_scalar(out=out, in0=in0, scalar1=s1, scalar2=s2, op0=op0, op1=op1), rd, wr)


def STT(kb, out, in0, scalar, in1, op0, op1, rd, wr):
    return kb.op("dve", lambda e: e.scalar_tensor_tensor(out=out, in0=in0, scalar=scalar, in1=in1, op0=op0, op1=op1), rd, wr)


def CP(kb, eng, out, in_, rd, wr):
    return kb.op(eng, lambda e: e.tensor_copy(out=out, in_=in_), rd, wr)


def RECIP(kb, out, in_, rd, wr):
    return kb.op("dve", lambda e: e.reciprocal(out=out, in_=in_), rd, wr)


def MEMSET(kb, eng, ap, val, wr):
    return kb.op(eng, lambda e: e.memset(ap, val), [], wr)


def phase_ada(cx, l):
    nc, kb = cx.nc, cx.kb
    adaw = cx.I["ada_w"][l].rearrange("(kc p) n -> p kc n", p=128)
    with ExitStack() as es:
        aw0 = es.enter_context(nc.sbuf_tensor(U("adaw0"), [128, 16, 512], F32))
        aw1 = es.enter_context(nc.sbuf_tensor(U("adaw1"), [128, 16, 512], F32))
        adab = es.enter_context(nc.sbuf_tensor(U("adab"), [128, 96], F32))
        gbias = es.enter_context(nc.sbuf_tensor(U("gbias"), [128, 512], F32))
        crep = es.enter_context(nc.sbuf_tensor(U("crep"), [128, 16, 128], F32))
        cx.crep = crep
        cx.bcrep = kb.buf()
        for kc in range(16):
            TS(kb, "dve", crep[:, kc, :], cx.ones[:], cx.cact[:, kc:kc + 1], None, ALU.mult, None, [cx.bconst, cx.bcact], [cx.bcrep])
        aws = [aw0, aw1]
        baw = [kb.buf(), kb.buf()]
        bab, bgb = kb.buf(), kb.buf()
        kb.dma("sp", adab[:], cx.I["ada_b_col"][l], [], [bab])
        psm, bpsm = cx.ps[7], cx.pb[7]
        for j in range(24):
            a, ba = aws[j % 2], baw[j % 2]
            kb.dma("sp", a[:], adaw[:, :, j * 512:(j + 1) * 512], [], [ba])
            which = {2: 0, 5: 1}.get(j // 4)
            if which is not None:
                p, bp = cx.ps[j % 2], cx.pb[j % 2]
                for kc in range(16):
                    MM(kb, p[:, :], cx.crep[:, kc, :], a[:, kc, :], kc == 0, kc == 15, [cx.bcrep, ba], [bp])
                kb.dma("sp", gbias[:], cx.I["ada_b_rep"][l][:, j * 512:(j + 1) * 512], [], [bgb])
                TT(kb, "dve", cx.gb[l][which][:, (j % 4) * 512:(j % 4 + 1) * 512], p[:, :], gbias[:], ALU.add, [bp, bgb], [cx.bgb[l][which]])
            else:
                for cc in range(4):
                    jj = 4 * j + cc
                    for kc in range(16):
                        MM(kb, psm[:, jj:jj + 1], a[:, kc, cc * 128:(cc + 1) * 128], cx.cact[:, kc:kc + 1], kc == 0, kc == 15, [ba, cx.bcact], [bpsm])
        for lo, hi in ((0, 32), (48, 80)):
            TT(kb, "dve", cx.modT[l][:, lo:hi], psm[:, lo:hi], adab[:, lo:hi], ALU.add, [bpsm, bab], [cx.bmod[l]])
        for s, (sc_lo, sh_lo, gcol) in enumerate(((16, 0, cx.gmix[l]), (64, 48, cx.gffn[l]))):
            TS(kb, "dve", cx.AB[l][:, 32 * s:32 * s + 16], cx.modT[l][:, sc_lo:sc_lo + 16], 1.0, None, ALU.add, None, [cx.bmod[l]], [cx.bAB[l]])
            TT(kb, "dve", cx.AB[l][:, 32 * s:32 * s + 16], cx.AB[l][:, 32 * s:32 * s + 16], gcol, ALU.mult, [cx.bAB[l], cx.bconst], [cx.bAB[l]])
            CP(kb, "dve", cx.AB[l][:, 32 * s + 16:32 * s + 32], cx.modT[l][:, sh_lo:sh_lo + 16], [cx.bmod[l]], [cx.bAB[l]])


def phase_norm(cx, xsrc, bxsrc, Acol, Bcol, bAB, hT_d, bhT, ntiles=NT, rowmap=None, hT_sb=None):
    nc, kb = cx.nc, cx.kb
    hview = hT_d.rearrange("c p t -> p c t")
    with ExitStack() as es:
        x0 = es.enter_context(nc.sbuf_tensor(U("nx0"), [128, D], F32))
        x1 = es.enter_context(nc.sbuf_tensor(U("nx1"), [128, D], F32))
        xh0 = es.enter_context(nc.sbuf_tensor(U("nxh0"), [128, D], F32))
        xh1 = es.enter_context(nc.sbuf_tensor(U("nxh1"), [128, D], F32))
        if hT_sb is None:
            st0 = es.enter_context(nc.sbuf_tensor(U("nst0"), [128, 16, 512], BF16))
            st1 = es.enter_context(nc.sbuf_tensor(U("nst1"), [128, 16, 512], BF16))
        else:
            st0 = st1 = None
        ss = es.enter_context(nc.sbuf_tensor(U("nss"), [128, 8], F32))
        xs, bxs = [x0, x1], [kb.buf(), kb.buf()]
        xhs, bxh = [xh0, xh1], [kb.buf(), kb.buf()]
        sts, bst = [st0, st1], [kb.buf(), kb.buf()]
        bss = [kb.buf(), kb.buf()]
        for tb in range(ntiles // 4):
            st, bs = sts[tb % 2], bst[tb % 2]
            for tt in range(4):
                ti = tb * 4 + tt
                u = ti % 2
                r0 = ti * 128 if rowmap is None else rowmap(ti)
                kb.dma("sp", xs[u][:], xsrc[r0:r0 + 128, :], [bxsrc], [bxs[u]])
                c0 = 4 * u
                ACT(kb, xhs[u][:], xs[u][:], AF.Square, [bxs[u]], [bxh[u]])
                kb.op("dve", lambda e: e.tensor_reduce(out=ss[:, c0:c0 + 1], in_=xhs[u][:], axis=AX.X, op=ALU.add), [bxh[u]], [bss[u]])
                TS(kb, "dve", ss[:, c0 + 1:c0 + 2], ss[:, c0:c0 + 1], 1.0 / D, EPS, ALU.mult, ALU.add, [bss[u]], [bss[u]])
                ACT(kb, ss[:, c0 + 2:c0 + 3], ss[:, c0 + 1:c0 + 2], AF.Sqrt, [bss[u]], [bss[u]])
                RECIP(kb, ss[:, c0 + 3:c0 + 4], ss[:, c0 + 2:c0 + 3], [bss[u]], [bss[u]])
                ACT(kb, xhs[u][:], xs[u][:], AF.Identity, [bxs[u], bss[u]], [bxh[u]], scale=ss[:, c0 + 3:c0 + 4])
                for q in range(4):
                    p, bp = cx.ps[(ti * 4 + q) % 4], cx.pb[(ti * 4 + q) % 4]
                    for r in range(4):
                        kc = 4 * q + r
                        TR(kb, p[:, r * 128:(r + 1) * 128], xhs[u][:, kc * 128:(kc + 1) * 128], cx.ident[:], [bxh[u], cx.bconst], [bp])
                    for r in range(4):
                        kc = 4 * q + r
                        if hT_sb is None:
                            ACT(kb, st[:, kc, tt * 128:(tt + 1) * 128], p[:, r * 128:(r + 1) * 128], AF.Identity, [bp, bAB], [bs],
                                scale=Acol[:, kc:kc + 1], bias=Bcol[:, kc:kc + 1])
                        else:
                            ACT(kb, hT_sb[:, kc, ti * 128:(ti + 1) * 128], p[:, r * 128:(r + 1) * 128], AF.Identity, [bp, bAB], [bhT],
                                scale=Acol[:, kc:kc + 1], bias=Bcol[:, kc:kc + 1])
            if hT_sb is None:
                kb.dma("pool", hview[:, :, tb * 512:(tb + 1) * 512], st[:], [bs], [bhT])


def phase_inproj(cx, l, hTs_in=None):
    nc, kb = cx.nc, cx.kb
    win = cx.I["w_in_h"][l].rearrange("(kc p) n -> p kc n", p=128)
    with ExitStack() as es:
        hTs = hTs_in if hTs_in is not None else es.enter_context(nc.sbuf_tensor(U("hTs"), [128, 16, S], BF16))
        w0 = es.enter_context(nc.sbuf_tensor(U("ipw0"), [128, 16, 512], BF16))
        w1 = es.enter_context(nc.sbuf_tensor(U("ipw1"), [128, 16, 512], BF16))
        o0 = es.enter_context(nc.sbuf_tensor(U("ipo0"), [128, 512], F32))
        o1 = es.enter_context(nc.sbuf_tensor(U("ipo1"), [128, 512], F32))
        bfm = es.enter_context(nc.sbuf_tensor(U("ipbf"), [128, NFM], F32))
        btm = es.enter_context(nc.sbuf_tensor(U("ipbt"), [128, NTM], F32))
        bh = kb.buf()
        ws, bws = [w0, w1], [kb.buf(), kb.buf()]
        os_, bos = [o0, o1], [kb.buf() for _ in range(2)]
        bb = kb.buf()
        if hTs_in is None:
            for kc in range(16):
                kb.dma("sp", hTs[:, kc, :], cx.hT_d[kc], [cx.bhT], [bh])
        kb.dma("sp", bfm[:], cx.I["b_fm"][l], [], [bb])
        kb.dma("sp", btm[:], cx.I["b_tm"][l], [], [bb])
        no = 0
        groups = [list(range(g, min(g + 4, NFM))) for g in range(0, NFM, 4)]
        for gi, chunks in enumerate(groups):
            w, bw = ws[gi % 2], bws[gi % 2]
            nw = 128 * len(chunks)
            kb.dma("pool", w[:, :, 0:nw], win[:, :, chunks[0] * 128:chunks[0] * 128 + nw], [], [bw])
            for tb in range(8):
                for ci, ch in enumerate(chunks):
                    p, bp = cx.ps[no % 6], cx.pb[no % 6]
                    o, bo = os_[no % 2], bos[no % 2]
                    no += 1
                    for kc in range(16):
                        MM(kb, p[:, :], w[:, kc, ci * 128:(ci + 1) * 128], hTs[:, kc, tb * 512:(tb + 1) * 512], kc == 0, kc == 15, [bw, bh], [bp])
                    ACT(kb, o[:], p[:, :], AF.Identity, [bp, bb], [bo], bias=bfm[:, ch:ch + 1])
                    kb.dma("sp", cx.pfm_d[ch, :, tb * 512:(tb + 1) * 512], o[:], [bo], [cx.bpfm])
        for gi, (nm, lo, n) in enumerate(TM):
            w, bw = ws[(gi + 1) % 2], bws[(gi + 1) % 2]
            kb.dma("pool", w[:, :, 0:n], win[:, :, lo:lo + n], [], [bw])
            dst = cx.tm_d[nm]
            for tt in range(NT):
                p, bp = cx.ps[no % 6], cx.pb[no % 6]
                o, bo = os_[no % 2], bos[no % 2]
                no += 1
                for kc in range(16):
                    MM(kb, p[:, 0:n], hTs[:, kc, tt * 128:(tt + 1) * 128], w[:, kc, 0:n], kc == 0, kc == 15, [bh, bw], [bp])
                TT(kb, "dve", o[:, 0:n], p[:, 0:n], btm[:, TMOFF[nm]:TMOFF[nm] + n], ALU.add, [bp, bb], [bo])
                kb.dma("sp", dst[tt * 128:(tt + 1) * 128, :], o[:, 0:n], [bo], [cx.btm[nm]])


def host_shared(inp):
    f = np.float32
    d = {}
    d["ada_w"] = inp["ada_w"]
    d["ada_b_col"] = np.ascontiguousarray(inp["ada_b"].reshape(2, 96, 128).transpose(0, 2, 1))
    d["ada_b_rep"] = np.ascontiguousarray(np.broadcast_to(inp["ada_b"][:, None, :], (2, 128, 12288)))
    d["gmix_col"] = np.ascontiguousarray(inp["norm_mix_g"].reshape(2, 16, 128).transpose(0, 2, 1))
    d["gffn_col"] = np.ascontiguousarray(inp["norm_ffn_g"].reshape(2, 16, 128).transpose(0, 2, 1))
    d["ident"] = np.eye(128, dtype=f)
    t = np.arange(S)
    inv = (10000.0 ** (-np.arange(0, 32, 2, dtype=np.float64) / 32)).astype(np.float64)
    ang = np.zeros((64, S))
    for j in range(64):
        pos = (t // 64) if j < 32 else (t % 64)
        ang[j] = pos * inv[j % 16]
    ang = np.concatenate([ang, ang], 0)
    d["rope_cos"] = np.cos(ang).astype(f)
    d["rope_sin"] = np.sin(ang).astype(f)
    P = np.zeros((128, 128), f)
    for m in range(128):
        if m % 32 < 16:
            P[m, m + 16] = -1.0
        else:
            P[m, m - 16] = 1.0
    d["rope_PT"] = np.ascontiguousarray(P.T)
    b64 = np.zeros((128, 128), f)
    b64[:64, :64] = 1.0
    b64[64:, 64:] = 1.0
    d["bd64"] = b64
    s_, t_ = np.meshgrid(np.arange(128), np.arange(128), indexing="ij")
    mlc = np.zeros((4, 128, 128), f)
    mlc[0] = (s_ <= t_)
    mlc[1] = (s_ >= t_)
    mlc[2] = np.where(s_ <= t_, 0.0, -30000.0)
    mlc[3] = np.where(s_ >= t_, 0.0, -30000.0)
    d["mlc"] = mlc
    a = np.arange(NF * 128, dtype=np.int64)
    m = (a[:, None] * a[None, :]) % NDFT
    ang = 2.0 * np.pi * m.astype(np.float64) / NDFT
    valid = (a[:, None] <= 4096) & (a[None, :] <= 4096)
    for nm, fn in (("dft_c", np.cos), ("dft_s", np.sin)):
        full = np.where(valid, fn(ang), 0.0).astype(f)
        blk = full.reshape(NF, 128, NF, 128).transpose(2, 1, 0, 3)
        d[nm] = np.ascontiguousarray(blk).astype(ml_dtypes.bfloat16)
    wf = np.zeros(NF * 128, f)
    wf[0:4097] = 2.0 / NDFT
    wf[0] = 1.0 / NDFT
    wf[4096] = 1.0 / NDFT
    d["dft_wf"] = np.ascontiguousarray(wf.reshape(NF, 128).T)
    pos = np.arange(S, dtype=f)
    tt_ = pos / (S - 1)
    bands = np.linspace(1e-4, 7, 8, dtype=f)
    angf = (2.0 * np.pi * pos / S)[:, None] * bands
    d["hy_feat"] = np.ascontiguousarray(np.concatenate([tt_[:, None], np.cos(angf), -np.sin(angf)], axis=-1).T.astype(f))
    d["hy_negt"] = np.ascontiguousarray((-tt_).reshape(32, 128).T.astype(f))
    d["hy_w1"] = inp["hy_w1"]
    d["hy_w2"] = inp["hy_w2"]
    d["ffn_w1"] = inp["ffn_w1"]
    d["ffn_w3"] = inp["ffn_w3"]
    d["ffn_w2"] = inp["ffn_w2"]
    d["final_g_rep"] = np.ascontiguousarray(np.broadcast_to(inp["final_g"][None, :], (128, D)))
    return d


def host_half(inp, r):
    f = np.float32
    d = {}
    ci, nfm = half_cols(r)
    assert nfm == NFM * 128 and len(ci) == NCH
    d["w_in_h"] = np.ascontiguousarray(inp["w_in"][:, :, ci])
    bh = inp["b_in"][:, ci]
    d["b_fm"] = np.ascontiguousarray(bh[:, :nfm].reshape(2, NFM, 128).transpose(0, 2, 1))
    d["b_tm"] = np.ascontiguousarray(np.broadcast_to(bh[:, None, nfm:], (2, 128, NTM)))
    cols = np.zeros((2, 128, NCOL), f)
    for l in range(2):
        cw = inp["lru_conv_w"][l].reshape(4, 4, 128)[:, 2 * r:2 * r + 2]
        cols[l, :, CO["lru_cw"]:CO["lru_cw"] + 8] = cw.transpose(2, 1, 0).reshape(128, 8)
        cols[l, :, CO["lru_cb"]:CO["lru_cb"] + 2] = inp["lru_conv_b"][l].reshape(4, 128)[2 * r:2 * r + 2].T
        for nm in ("lru_ba", "lru_bx"):
            v = inp[nm][l].reshape(2, 4, 128)[:, 2 * r:2 * r + 2]
            cols[l, :, CO[nm]:CO[nm] + 4] = v.transpose(2, 0, 1).reshape(128, 4)
        v = inp["lru_lambda"][l].reshape(2, 4, 128)[:, 2 * r:2 * r + 2]
        cols[l, :, CO["lru_lam"]:CO["lru_lam"] + 4] = v.transpose(2, 0, 1).reshape(128, 4)
        cols[l, :, CO["att_qg"]] = np.tile(inp["att_q_norm_g"][l], 2)
        cols[l, :, CO["att_kg"]] = np.tile(inp["att_k_norm_g"][l], 2)
        hw = inp["hy_conv_w"][l].reshape(3, 3, 4, 128)[:, :, 2 * r:2 * r + 2]
        cols[l, :, CO["hy_cw"]:CO["hy_cw"] + 18] = hw.transpose(3, 1, 2, 0).reshape(128, 18)
        hb = inp["hy_conv_b"][l].reshape(3, 4, 128)[:, 2 * r:2 * r + 2]
        cols[l, :, CO["hy_cb"]:CO["hy_cb"] + 6] = hb.transpose(2, 0, 1).reshape(128, 6)
        cols[l, :64, CO["hy_b1"]] = inp["hy_b1"][l]
        cols[l, :64, CO["hy_b2"]] = inp["hy_b2"][l]
        cols[l, :64, CO["hy_sf"]] = inp["hy_sin_freq"][l]
    d["cols"] = cols
    bd = np.zeros((2, 2, 2, 2, 128, 128), f)
    for gi, nm in enumerate(("lru_wa", "lru_wx")):
        for c in range(2):
            for i in range(2):
                bd[:, :, gi, c, 64 * i:64 * i + 64, 64 * i:64 * i + 64] = inp[nm][:, :, 2 * (2 * r + c) + i]
    d["lru_bd"] = bd
    d["ml_g_rep"] = np.ascontiguousarray(np.broadcast_to(inp["ml_norm_g"][:, None, HW * r:HW * r + HW], (2, 128, HW)))
    w3 = inp["hy_w3"].reshape(2, 64, 4, 512)[:, :, :, HW * r:HW * r + HW]
    d["hy_w3"] = np.ascontiguousarray(w3.reshape(2, 64, 4 * HW))
    dec = inp["hy_decay"].reshape(2, 4, 512)[:, :, HW * r:HW * r + HW].reshape(2, 1, 4 * HW)
    d["hy_decay_rep"] = np.ascontiguousarray(np.broadcast_to(dec, (2, 128, 4 * HW)))
    sk = inp["hy_skip"][:, :, HW * r:HW * r + HW].reshape(2, 1, 2 * HW)
    d["hy_skip_rep"] = np.ascontiguousarray(np.broadcast_to(sk, (2, 128, 2 * HW)))
    rows = np.concatenate([np.arange(512 * g + HW * r, 512 * g + HW * r + HW) for g in range(4)])
    d["w_out_h"] = np.ascontiguousarray(inp["w_out"][:, rows, :])
    return d


def host_inputs(inp, core, shared, halves):
    b, r = core // 2, core % 2
    d = dict(shared)
    d.update(halves[r])
    d["x"] = np.ascontiguousarray(inp["x"][b])
    d["x_half"] = np.ascontiguousarray(inp["x"][b, r * SH:(r + 1) * SH])
    d["c_col"] = np.ascontiguousarray(inp["c"][b].reshape(16, 128).T)
    return d


IN_SPECS = {
    "x": ([S, D], F32), "x_half": ([SH, D], F32), "c_col": ([128, 16], F32), "ada_w": ([2, D, 6 * D], F32),
    "ada_b_col": ([2, 128, 96], F32), "ada_b_rep": ([2, 128, 6 * D], F32),
    "gmix_col": ([2, 128, 16], F32), "gffn_col": ([2, 128, 16], F32),
    "w_in_h": ([2, D, NCH], F32), "b_fm": ([2, 128, NFM], F32), "b_tm": ([2, 128, NTM], F32),
    "ident": ([128, 128], F32),
    "cols": ([2, 128, NCOL], F32), "lru_bd": ([2, 2, 2, 2, 128, 128], F32),
    "rope_cos": ([128, S], F32), "rope_sin": ([128, S], F32), "rope_PT": ([128, 128], F32), "bd64": ([128, 128], F32),
    "w_out_h": ([2, D // 2, D], F32), "ffn_w1": ([2, D, DFF], F32), "ffn_w3": ([2, D, DFF], F32), "ffn_w2": ([2, DFF, D], F32),
    "final_g_rep": ([128, D], F32),
    "dft_c": ([NF, 128, NF, 128], BF16), "dft_s": ([NF, 128, NF, 128], BF16), "dft_wf": ([128, NF], F32),
    "hy_feat": ([17, S], F32), "hy_negt": ([128, 32], F32), "hy_w1": ([2, 17, 64], F32), "hy_w2": ([2, 64, 64], F32),
    "hy_w3": ([2, 64, 4 * HW], F32), "hy_decay_rep": ([2, 128, 4 * HW], F32), "hy_skip_rep": ([2, 128, 2 * HW], F32),
    "mlc": ([4, 128, 128], F32), "ml_g_rep": ([2, 128, HW], F32),
}


def build(stages, dbg=()):
    nc = bass.Bass("TRN2", target_bir_lowering=False)
    kb = KB(nc)
    cx = Ctx()
    cx.nc, cx.kb = nc, kb
    cx.I = {k: nc.dram_tensor(k, sh, dt, kind="ExternalInput").ap() for k, (sh, dt) in IN_SPECS.items()}

    def scratch(name, shape, dt):
        kind = "ExternalOutput" if name in dbg else "Internal"
        return nc.dram_tensor(name, shape, dt, kind=kind).ap()

    cx.hT_d = scratch("hT_d", [16, 128, S], BF16)
    cx.pfm_d = scratch("pfm_d", [NFM, 128, S], F32)
    cx.tm_d = {nm: scratch("tm_" + nm, [S, n], F32) for nm, lo, n in TM}
    cx.bhT, cx.bpfm = kb.buf(), kb.buf()
    cx.yT_d = scratch("yT_d", [8, 128, S], BF16)
    cx.byT = kb.buf()
    cx.ztok_d = scratch("ztok_d", [S, 3 * HW], F32)
    cx.z1_d = scratch("z1_d", [S, HW], F32)
    cx.G_d = scratch("G_d", [2, S, 2 * HW], BF16)
    cx.H_d = scratch("H_d", [2, NF, 128, 2 * HW], F32)
    cx.bztok, cx.bz1, cx.bG_d, cx.bH_d = kb.buf(), kb.buf(), kb.buf(), kb.buf()
    cx.xa_d = scratch("xa_d", [SH, D], F32)
    cx.xb_d = scratch("xb_d", [SH, D], F32)
    cx.part_d = scratch("part_d", [S, D], F32)
    cx.sum_d = scratch("sum_d", [SH, D], F32)
    cx.xfull_d = scratch("xfull_d", [S, D], F32)
    cx.h2T_d = scratch("h2T_d", [16, 128, SH], BF16)
    cx.bxa, cx.bxb, cx.bpart, cx.bsum, cx.bxfull, cx.bh2T = [kb.buf() for _ in range(6)]
    cx.out = nc.dram_tensor("out", [SH, D], F32, kind="ExternalOutput").ap()
    cx.bout = kb.buf()
    cx.btm = {nm: kb.buf() for nm, _, _ in TM}
    cx.dbg_out = scratch("dbg_out", [128, 512], F32) if "dbg_out" in dbg else None
    finals = []
    with ExitStack() as es:
        ident = es.enter_context(nc.sbuf_tensor(U("ident"), [128, 128], F32))
        cact = es.enter_context(nc.sbuf_tensor(U("cact"), [128, 16], F32))
        ones = es.enter_context(nc.sbuf_tensor(U("ones"), [128, 128], F32))
        gcols = es.enter_context(nc.sbuf_tensor(U("gcols"), [128, 64], F32))
        cx.cols = es.enter_context(nc.sbuf_tensor(U("cols"), [128, NCOL], F32))
        cx.bcols = kb.buf()
        cx.epsc = es.enter_context(nc.sbuf_tensor(U("epsc"), [128, 2], F32))
        cx.ccsem = nc.alloc_semaphore(U("ccsem"))
        cx.ccdummy = es.enter_context(nc.sbuf_tensor(U("ccd"), [128, 2], F32))
        cx.bccd = kb.buf()
        modT0 = es.enter_context(nc.sbuf_tensor(U("modT0"), [128, 96], F32))
        AB0 = es.enter_context(nc.sbuf_tensor(U("AB0"), [128, 64], F32))
        g1b0 = es.enter_context(nc.sbuf_tensor(U("g1b0"), [128, D], F32))
        g2b0 = es.enter_context(nc.sbuf_tensor(U("g2b0"), [128, D], F32))
        ps0 = es.enter_context(nc.psum_tensor(U("ps0"), [128, 512], F32))
        ps1 = es.enter_context(nc.psum_tensor(U("ps1"), [128, 512], F32))
        ps2 = es.enter_context(nc.psum_tensor(U("ps2"), [128, 512], F32))
        ps3 = es.enter_context(nc.psum_tensor(U("ps3"), [128, 512], F32))
        ps4 = es.enter_context(nc.psum_tensor(U("ps4"), [128, 512], F32))
        ps5 = es.enter_context(nc.psum_tensor(U("ps5"), [128, 512], F32))
        ps6 = es.enter_context(nc.psum_tensor(U("ps6"), [128, 512], F32))
        ps7 = es.enter_context(nc.psum_tensor(U("ps7"), [128, 512], F32))
        cx.ps = [ps0, ps1, ps2, ps3, ps4, ps5, ps6, ps7]
        cx.pb = [kb.buf() for _ in range(8)]
        cx.ident, cx.cact, cx.ones = ident, cact, ones
        cx.bconst, cx.bcact, cx.bcrep = kb.buf(), kb.buf(), kb.buf()
        _b = kb.buf()
        cx.modT, cx.bmod = [modT0, modT0], [_b, _b]
        _b = kb.buf()
        cx.AB, cx.bAB = [AB0, AB0], [_b, _b]
        cx.gb = [[g1b0, g2b0], [g1b0, g2b0]]
        _b = [kb.buf(), kb.buf()]
        cx.bgb = [_b, _b]
        cx.gmix = [gcols[:, 0:16], gcols[:, 16:32]]
        cx.gffn = [gcols[:, 32:48], gcols[:, 48:64]]
        kb.dma("sp", ident[:], cx.I["ident"], [], [cx.bconst])
        for l in range(2):
            kb.dma("sp", gcols[:, 16 * l:16 * l + 16], cx.I["gmix_col"][l], [], [cx.bconst])
            kb.dma("sp", gcols[:, 32 + 16 * l:48 + 16 * l], cx.I["gffn_col"][l], [], [cx.bconst])
        kb.dma("sp", cact[:], cx.I["c_col"], [], [cx.bcact])
        MEMSET(kb, "dve", ones[:], 1.0, [cx.bconst])
        MEMSET(kb, "dve", cx.epsc[:], EPS, [cx.bconst])
        kb.dma("sp", cx.cols[:], cx.I["cols"][0], [], [cx.bcols])
        ACT(kb, cact[:], cact[:], AF.Silu, [cx.bcact], [cx.bcact])
        if "full" in stages:
            bx0, bxh0 = kb.buf(), kb.buf()
            for l in range(2):
                kb.barrier()
                if l > 0:
                    kb.dma("sp", cx.cols[:], cx.I["cols"][l], [], [cx.bcols])
                phase_ada(cx, l)
                kb.barrier()
                xsrc, bxs = (cx.I["x"], bx0) if l == 0 else (cx.xfull_d, cx.bxfull)
                with nc.sbuf_tensor(U("hTres"), [128, 16, S], BF16) as hTres:
                    phase_norm(cx, xsrc, bxs, cx.AB[l][:, 0:16], cx.AB[l][:, 16:32], cx.bAB[l], cx.hT_d, kb.buf(),
                               rowmap=(None if l == 0 else pair_row), hT_sb=hTres)
                    kb.barrier()
                    phase_inproj(cx, l, hTs_in=hTres)
                kb.barrier()
                phase_lru(cx, l)
                kb.barrier()
                phase_att(cx, l)
                kb.barrier()
                phase_hyena(cx, l)
                kb.barrier()
                phase_mlstm(cx, l)
                kb.barrier()
                phase_outproj(cx, l)
                collective(cx, "ReduceScatter", ALU.add, cx.part_d, cx.sum_d)
                xh, bxh = (cx.I["x_half"], bxh0) if l == 0 else (cx.xb_d, cx.bxb)
                phase_resid(cx, l, xh, bxh)
                kb.barrier()
                phase_norm(cx, cx.xa_d, cx.bxa, cx.AB[l][:, 32:48], cx.AB[l][:, 48:64], cx.bAB[l], cx.h2T_d, cx.bh2T, ntiles=NTH)
                kb.barrier()
                phase_ffn(cx, l, cx.xa_d, cx.bxa, cx.xb_d, cx.bxb)
                if l == 0:
                    collective(cx, "AllGather", ALU.bypass, cx.xb_d, cx.xfull_d)
            kb.barrier()
            phase_final(cx, cx.xb_d, cx.bxb, cx.out, cx.bout)
            finals.append(cx.bout)
        if "ada" in stages:
            phase_ada(cx, 0)
        kb.barrier()
        if "norm1" in stages:
            phase_norm(cx, cx.I["x"], kb.buf(), cx.AB[0][:, 0:16], cx.AB[0][:, 16:32], cx.bAB[0], cx.hT_d, cx.bhT)
            finals.append(cx.bhT)
        if "inproj" in stages:
            kb.barrier()
            phase_inproj(cx, 0)
            finals += [cx.bpfm] + list(cx.btm.values())
        if "lru" in stages:
            kb.barrier()
            phase_lru(cx, 0)
            finals.append(cx.byT)
        if "att" in stages:
            kb.barrier()
            phase_att(cx, 0)
            finals.append(cx.byT)
        if "hyena" in stages:
            kb.barrier()
            phase_hyena(cx, 0)
            finals.append(cx.byT)
        if "mlstm" in stages:
            kb.barrier()
            phase_mlstm(cx, 0)
            finals.append(cx.byT)
        if "dbgmod" in dbg:
            with nc.sbuf_tensor(U("dbgt"), [128, 512], F32) as dbgt:
                bd, bo = kb.buf(), kb.buf()
                MEMSET(kb, "dve", dbgt[:], 0.0, [bd])
                CP(kb, "dve", dbgt[:, 0:96], modT0[:], [cx.bmod[0]], [bd])
                CP(kb, "dve", dbgt[:, 96:160], AB0[:], [cx.bAB[0]], [bd])
                CP(kb, "dve", dbgt[:, 160:416], g1b0[:, 0:256], [cx.bgb[0][0]], [bd])
                kb.dma("sp", cx.dbg_out, dbgt[:], [bd], [bo])
                finals.append(bo)
        kb.finish(finals)
    return nc


def phase_outproj(cx, l):
    nc, kb = cx.nc, cx.kb
    wsrc = cx.I["w_out_h"][l].rearrange("(kc p) n -> p kc n", p=128)
    yview = cx.yT_d.rearrange("c p t -> p c t")
    KC = 8
    with ExitStack() as es:
        wt = es.enter_context(nc.sbuf_tensor(U("opw"), [128, KC, D], BF16))
        ybs = [es.enter_context(nc.sbuf_tensor(U("opy"), [128, KC, 512], BF16)) for _ in range(2)]
        xts = [es.enter_context(nc.sbuf_tensor(U("opx"), [128, D], F32)) for _ in range(2)]
        bw = kb.buf()
        byb, bxt = [kb.buf(), kb.buf()], [kb.buf(), kb.buf()]
        for q in range(4):
            kb.dma("pool", wt[:, :, q * 512:(q + 1) * 512], wsrc[:, :, q * 512:(q + 1) * 512], [], [bw])
        no = 0
        for tb in range(8):
            yb, by = ybs[tb % 2], byb[tb % 2]
            kb.dma("sp", yb[:], yview[:, :, tb * 512:(tb + 1) * 512], [cx.byT], [by])
            for tt in range(4):
                ti = tb * 4 + tt
                xt, bx = xts[ti % 2], bxt[ti % 2]
                for db in range(4):
                    p, bp = cx.ps[no % 4], cx.pb[no % 4]
                    no += 1
                    for kc in range(KC):
                        MM(kb, p[:, :], yb[:, kc, tt * 128:(tt + 1) * 128], wt[:, kc, db * 512:(db + 1) * 512], kc == 0, kc == KC - 1, [by, bw], [bp])
                    if db % 2 == 0:
                        CP(kb, "dve", xt[:, db * 512:(db + 1) * 512], p[:, :], [bp], [bx])
                    else:
                        ACT(kb, xt[:, db * 512:(db + 1) * 512], p[:, :], AF.Identity, [bp], [bx])
                r0 = pair_row(ti)
                kb.dma("pool", cx.part_d[r0:r0 + 128, :], xt[:], [bx], [cx.bpart])


def pair_row(ti):
    return (ti % NTH) * 256 + (ti // NTH) * 128


def collective(cx, kind, op, src, dst):
    nc, kb = cx.nc, cx.kb
    kb.barrier()
    for j in range(NTH):
        big = slice(j * 256, (j + 1) * 256)
        small = slice(j * 128, (j + 1) * 128)
        i_ap, o_ap = (src[big, :], dst[small, :]) if kind == "ReduceScatter" else (src[small, :], dst[big, :])
        nc.gpsimd.sem_clear(cx.ccsem)
        nc.gpsimd.collective_compute(kind, op, replica_groups=[[0, 1], [2, 3], [4, 5], [6, 7]],
                                     ins=[i_ap], outs=[o_ap]).then_inc(cx.ccsem)
        nc.gpsimd.wait_ge(cx.ccsem, 1)
    MEMSET(kb, "pool", cx.ccdummy[:], 0.0, [cx.bccd])
    kb.barrier()


def phase_resid(cx, l, xsrc, bxsrc):
    nc, kb = cx.nc, cx.kb
    with ExitStack() as es:
        xs = [es.enter_context(nc.sbuf_tensor(U("rsx"), [128, D], F32)) for _ in range(2)]
        ss = [es.enter_context(nc.sbuf_tensor(U("rss"), [128, D], F32)) for _ in range(2)]
        bxs, bss = [kb.buf(), kb.buf()], [kb.buf(), kb.buf()]
        for ti in range(NTH):
            u = ti % 2
            kb.dma("sp", xs[u][:], xsrc[ti * 128:(ti + 1) * 128, :], [bxsrc], [bxs[u]])
            kb.dma("sp", ss[u][:], cx.sum_d[ti * 128:(ti + 1) * 128, :], [cx.bsum], [bss[u]])
            TT(kb, "dve", ss[u][:], ss[u][:], cx.gb[l][0][:], ALU.mult, [bss[u], cx.bgb[l][0]], [bss[u]])
            TT(kb, "dve", xs[u][:], xs[u][:], ss[u][:], ALU.add, [bxs[u], bss[u]], [bxs[u]])
            kb.dma("act", cx.xa_d[ti * 128:(ti + 1) * 128, :], xs[u][:], [bxs[u]], [cx.bxa])


def phase_ffn(cx, l, xsrc, bxsrc, xdst, bxdst):
    nc, kb = cx.nc, cx.kb
    w1s = cx.I["ffn_w1"][l].rearrange("(kc p) n -> p kc n", p=128)
    w3s = cx.I["ffn_w3"][l].rearrange("(kc p) n -> p kc n", p=128)
    w2s = cx.I["ffn_w2"][l].rearrange("(fc p) n -> p fc n", p=128)
    hview = cx.h2T_d.rearrange("c p t -> p c t")
    xs4 = xsrc.rearrange("(n p) d -> p n d", p=128)
    xd4 = xdst.rearrange("(n p) d -> p n d", p=128)
    NG = DFF // 256
    with ExitStack() as es:
        h0 = es.enter_context(nc.sbuf_tensor(U("fh0"), [128, 16, 512], BF16))
        h1 = es.enter_context(nc.sbuf_tensor(U("fh1"), [128, 16, 512], BF16))
        uT = es.enter_context(nc.sbuf_tensor(U("fu"), [128, 44, 512], BF16))
        wa = [es.enter_context(nc.sbuf_tensor(U("fw1"), [128, 16, 256], BF16)) for _ in range(2)]
        wb = [es.enter_context(nc.sbuf_tensor(U("fw3"), [128, 16, 256], BF16)) for _ in range(2)]
        w2t = [es.enter_context(nc.sbuf_tensor(U("fw2"), [128, 44, 256], BF16)) for _ in range(2)]
        xr = es.enter_context(nc.sbuf_tensor(U("fx"), [128, 4, D], F32))
        tmp = [es.enter_context(nc.sbuf_tensor(U("ft"), [128, 512], F32)) for _ in range(2)]
        hs, bhs = [h0, h1], [kb.buf(), kb.buf()]
        bu = kb.buf()
        bwa, bwb, bw2 = [kb.buf(), kb.buf()], [kb.buf(), kb.buf()], [kb.buf(), kb.buf()]
        bxr = kb.buf()
        btmp = [kb.buf(), kb.buf()]
        no = 0
        nw = 0
        nw2 = 0
        for tb in range(SH // 512):
            h, bh = hs[tb % 2], bhs[tb % 2]
            kb.dma("sp", h[:], hview[:, :, tb * 512:(tb + 1) * 512], [cx.bh2T], [bh])
            kb.dma("sp", xr[:], xs4[:, tb * 4:(tb + 1) * 4, :], [bxsrc], [bxr])
            for g in range(NG):
                a, ba = wa[nw % 2], bwa[nw % 2]
                b3, bb = wb[nw % 2], bwb[nw % 2]
                nw += 1
                kb.dma("pool", a[:], w1s[:, :, g * 256:(g + 1) * 256], [], [ba])
                kb.dma("pool", b3[:], w3s[:, :, g * 256:(g + 1) * 256], [], [bb])
                for cc in range(2):
                    fc = 2 * g + cc
                    pa, bpa = cx.ps[(no * 2) % 6], cx.pb[(no * 2) % 6]
                    pb_, bpb = cx.ps[(no * 2 + 1) % 6], cx.pb[(no * 2 + 1) % 6]
                    t, bt = tmp[no % 2], btmp[no % 2]
                    no += 1
                    for kc in range(16):
                        MM(kb, pa[:, :], a[:, kc, cc * 128:(cc + 1) * 128], h[:, kc, :], kc == 0, kc == 15, [ba, bh], [bpa])
                    for kc in range(16):
                        MM(kb, pb_[:, :], b3[:, kc, cc * 128:(cc + 1) * 128], h[:, kc, :], kc == 0, kc == 15, [bb, bh], [bpb])
                    ACT(kb, t[:], pa[:, :], AF.Silu, [bpa], [bt])
                    TT(kb, "dve", uT[:, fc, :], t[:], pb_[:, :], ALU.mult, [bt, bpb], [bu])
            for dq in range(8):
                w2, b2 = w2t[nw2 % 2], bw2[nw2 % 2]
                nw2 += 1
                kb.dma("pool", w2[:], w2s[:, :, dq * 256:(dq + 1) * 256], [], [b2])
                for tt in range(4):
                    p, bp = cx.ps[6 + (no % 2)], cx.pb[6 + (no % 2)]
                    t, bt = tmp[no % 2], btmp[no % 2]
                    no += 1
                    for fc in range(44):
                        MM(kb, p[:, 0:256], uT[:, fc, tt * 128:(tt + 1) * 128], w2[:, fc, :], fc == 0, fc == 43, [bu, b2], [bp])
                    TT(kb, "dve", t[:, 0:256], p[:, 0:256], cx.gb[l][1][:, dq * 256:(dq + 1) * 256], ALU.mult, [bp, cx.bgb[l][1]], [bt])
                    TT(kb, "dve", xr[:, tt, dq * 256:(dq + 1) * 256], xr[:, tt, dq * 256:(dq + 1) * 256], t[:, 0:256], ALU.add, [bt, bxr], [bxr])
            kb.dma("act", xd4[:, tb * 4:(tb + 1) * 4, :], xr[:], [bxr], [bxdst])


def phase_final(cx, xsrc, bxsrc, out, bout):
    nc, kb = cx.nc, cx.kb
    with ExitStack() as es:
        xs = [es.enter_context(nc.sbuf_tensor(U("fnx"), [128, D], F32)) for _ in range(2)]
        sq = [es.enter_context(nc.sbuf_tensor(U("fnq"), [128, D], F32)) for _ in range(2)]
        fg = es.enter_context(nc.sbuf_tensor(U("fng"), [128, D], F32))
        ss = es.enter_context(nc.sbuf_tensor(U("fns"), [128, 8], F32))
        bxs, bsq, bss = [kb.buf(), kb.buf()], [kb.buf(), kb.buf()], [kb.buf(), kb.buf()]
        bfg = kb.buf()
        kb.dma("sp", fg[:], cx.I["final_g_rep"], [], [bfg])
        for ti in range(NTH):
            u = ti % 2
            c0 = 4 * u
            kb.dma("sp", xs[u][:], xsrc[ti * 128:(ti + 1) * 128, :], [bxsrc], [bxs[u]])
            ACT(kb, sq[u][:], xs[u][:], AF.Square, [bxs[u]], [bsq[u]])
            kb.op("dve", lambda e: e.tensor_reduce(out=ss[:, c0:c0 + 1], in_=sq[u][:], axis=AX.X, op=ALU.add), [bsq[u]], [bss[u]])
            TS(kb, "dve", ss[:, c0 + 1:c0 + 2], ss[:, c0:c0 + 1], 1.0 / D, EPS, ALU.mult, ALU.add, [bss[u]], [bss[u]])
            ACT(kb, ss[:, c0 + 2:c0 + 3], ss[:, c0 + 1:c0 + 2], AF.Sqrt, [bss[u]], [bss[u]])
            RECIP(kb, ss[:, c0 + 3:c0 + 4], ss[:, c0 + 2:c0 + 3], [bss[u]], [bss[u]])
            STT(kb, sq[u][:], xs[u][:], ss[:, c0 + 3:c0 + 4], fg[:], ALU.mult, ALU.mult, [bxs[u], bss[u], bfg], [bsq[u]])
            kb.dma("pool", out[ti * 128:(ti + 1) * 128, :], sq[u][:], [bsq[u]], [bout])


def phase_lru(cx, l):
    nc, kb = cx.nc, cx.kb
    C = cx.cols
    bC = cx.bcols
    with ExitStack() as es:
        T = {nm: es.enter_context(nc.sbuf_tensor(U("lr" + nm), [128, S], F32)) for nm in ("ax", "ag", "xc", "r", "i", "t", "h", "hs")}
        B = {nm: kb.buf() for nm in T}
        ys = es.enter_context(nc.sbuf_tensor(U("lry"), [128, S], BF16))
        bys = kb.buf()
        wa = es.enter_context(nc.sbuf_tensor(U("lrwa"), [128, 128], F32))
        wx = es.enter_context(nc.sbuf_tensor(U("lrwx"), [128, 128], F32))
        cl = es.enter_context(nc.sbuf_tensor(U("lrcl"), [128, 4], F32))
        bwa, bwx, bcl = kb.buf(), kb.buf(), kb.buf()
        lam = C[:, CO["lru_lam"]:CO["lru_lam"] + 4]
        ACT(kb, cl[:], lam, AF.Exp, [bC], [bcl], scale=-1.0)
        ACT(kb, cl[:], cl[:], AF.Ln, [bcl], [bcl], bias=1.0)
        TS(kb, "dve", cl[:], cl[:], -8.0, None, ALU.mult, None, [bcl], [bcl])
        no = 0
        for c in range(NCK):
            ax, ag, xc, r, ii, t, h, hs = (T[k] for k in ("ax", "ag", "xc", "r", "i", "t", "h", "hs"))
            kb.dma("sp", ax[:], cx.pfm_d[c], [cx.bpfm], [B["ax"]])
            kb.dma("sp", ag[:], cx.pfm_d[NCK + c], [cx.bpfm], [B["ag"]])
            cw = lambda j: C[:, CO["lru_cw"] + 4 * c + j:CO["lru_cw"] + 4 * c + j + 1]
            ACT(kb, xc[:], ax[:], AF.Identity, [B["ax"], bC], [B["xc"]], scale=cw(2), bias=C[:, CO["lru_cb"] + c:CO["lru_cb"] + c + 1])
            STT(kb, xc[:, 2:S], ax[:, 0:S - 2], cw(0), xc[:, 2:S], ALU.mult, ALU.add, [B["ax"], B["xc"], bC], [B["xc"]])
            STT(kb, xc[:, 1:S], ax[:, 0:S - 1], cw(1), xc[:, 1:S], ALU.mult, ALU.add, [B["ax"], B["xc"], bC], [B["xc"]])
            STT(kb, xc[:, 0:S - 1], ax[:, 1:S], cw(3), xc[:, 0:S - 1], ALU.mult, ALU.add, [B["ax"], B["xc"], bC], [B["xc"]])
            for dr in range(2):
                kb.dma("sp", wa[:], cx.I["lru_bd"][l, dr, 0, c], [], [bwa])
                kb.dma("sp", wx[:], cx.I["lru_bd"][l, dr, 1, c], [], [bwx])
                ba = C[:, CO["lru_ba"] + 2 * dr + c:CO["lru_ba"] + 2 * dr + c + 1]
                bx = C[:, CO["lru_bx"] + 2 * dr + c:CO["lru_bx"] + 2 * dr + c + 1]
                for tb in range(8):
                    sl = slice(tb * 512, (tb + 1) * 512)
                    p1, bp1 = cx.ps[no % 4], cx.pb[no % 4]
                    p2, bp2 = cx.ps[(no + 1) % 4], cx.pb[(no + 1) % 4]
                    no += 2
                    MM(kb, p1[:, :], wa[:], xc[:, sl], True, True, [bwa, B["xc"]], [bp1])
                    MM(kb, p2[:, :], wx[:], xc[:, sl], True, True, [bwx, B["xc"]], [bp2])
                    ACT(kb, r[:, sl], p1[:, :], AF.Sigmoid, [bp1, bC], [B["r"]], bias=ba)
                    ACT(kb, ii[:, sl], p2[:, :], AF.Sigmoid, [bp2, bC], [B["i"]], bias=bx)
                ACT(kb, r[:], r[:], AF.Exp, [B["r"], bcl], [B["r"]], scale=cl[:, 2 * dr + c:2 * dr + c + 1])
                TT(kb, "dve", t[:], r[:], r[:], ALU.mult, [B["r"]], [B["t"]])
                TS(kb, "dve", t[:], t[:], -1.0, 1.0, ALU.mult, ALU.add, [B["t"]], [B["t"]])
                TS(kb, "dve", t[:], t[:], 0.0, None, ALU.max, None, [B["t"]], [B["t"]])
                ACT(kb, t[:], t[:], AF.Sqrt, [B["t"]], [B["t"]])
                TT(kb, "dve", t[:], t[:], ii[:], ALU.mult, [B["t"], B["i"]], [B["t"]])
                TT(kb, "dve", t[:], t[:], xc[:], ALU.mult, [B["t"], B["xc"]], [B["t"]])
                dst = hs if dr == 0 else h
                bdst = B["hs"] if dr == 0 else B["h"]
                if dr == 0:
                    kb.op("dve", lambda e: e.tensor_tensor_scan(out=dst[:], data0=r[:], data1=t[:], initial=0.0, op0=ALU.mult, op1=ALU.add), [B["r"], B["t"]], [bdst])
                else:
                    kb.op("dve", lambda e: e.tensor_tensor_scan(out=dst[:, ::-1], data0=r[:, ::-1], data1=t[:, ::-1], initial=0.0, op0=ALU.mult, op1=ALU.add), [B["r"], B["t"]], [bdst])
                    TT(kb, "dve", hs[:], hs[:], h[:], ALU.add, [B["hs"], B["h"]], [B["hs"]])
            ACT(kb, t[:], ag[:], AF.Square, [B["ag"]], [B["t"]])
            TS(kb, "dve", t[:], t[:], 0.044715, 1.0, ALU.mult, ALU.add, [B["t"]], [B["t"]])
            TT(kb, "dve", t[:], t[:], ag[:], ALU.mult, [B["t"], B["ag"]], [B["t"]])
            ACT(kb, t[:], t[:], AF.Sigmoid, [B["t"]], [B["t"]], scale=1.5957691216057308)
            TT(kb, "dve", t[:], t[:], ag[:], ALU.mult, [B["t"], B["ag"]], [B["t"]])
            TT(kb, "dve", ys[:], t[:], hs[:], ALU.mult, [B["t"], B["hs"]], [bys])
            kb.dma("pool", cx.yT_d[c], ys[:], [bys], [cx.byT])


def phase_att(cx, l):
    nc, kb = cx.nc, cx.kb
    C, bC = cx.cols, cx.bcols
    with ExitStack() as es:
        cos = es.enter_context(nc.sbuf_tensor(U("atc"), [128, S], F32))
        sin = es.enter_context(nc.sbuf_tensor(U("ats"), [128, S], F32))
        PT = es.enter_context(nc.sbuf_tensor(U("atP"), [128, 128], F32))
        BD = es.enter_context(nc.sbuf_tensor(U("atB"), [128, 128], F32))
        qk = [es.enter_context(nc.sbuf_tensor(U("atq"), [128, S], BF16)) for _ in range(3)]
        bqk = [kb.buf() for _ in range(3)]
        bk_ = kb.buf()
        kb.dma("sp", cos[:], cx.I["rope_cos"], [], [bk_])
        kb.dma("sp", sin[:], cx.I["rope_sin"], [], [bk_])
        kb.dma("sp", PT[:], cx.I["rope_PT"], [], [bk_])
        kb.dma("sp", BD[:], cx.I["bd64"], [], [bk_])
        no = 0
        with ExitStack() as es2:
            raw = [es2.enter_context(nc.sbuf_tensor(U("atr"), [128, S], F32)) for _ in range(2)]
            braw = [kb.buf(), kb.buf()]
            tq = [es2.enter_context(nc.sbuf_tensor(U("att"), [128, 512], F32)) for _ in range(4)]
            btq = [kb.buf() for _ in range(4)]
            for ci in range(3):
                ch = 4 + ci
                rw, brw = raw[ci % 2], braw[ci % 2]
                kb.dma("sp", rw[:], cx.pfm_d[ch], [cx.bpfm], [brw])
                gcol = C[:, CO["att_qg"]:CO["att_qg"] + 1] if ci < 2 else C[:, CO["att_kg"]:CO["att_kg"] + 1]
                qs = 0.125 if ci < 2 else 1.0
                for tb in range(8):
                    sl = slice(tb * 512, (tb + 1) * 512)
                    p1, bp1 = cx.ps[no % 4], cx.pb[no % 4]
                    p2, bp2 = cx.ps[(no + 1) % 4], cx.pb[(no + 1) % 4]
                    no += 2
                    t0, t1, t2, t3 = tq
                    ACT(kb, t0[:], rw[:, sl], AF.Square, [brw], [btq[0]])
                    MM(kb, p1[:, :], BD[:], t0[:], True, True, [bk_, btq[0]], [bp1])
                    ACT(kb, t1[:], p1[:, :], AF.Sqrt, [bp1], [btq[1]], scale=1.0 / 64.0, bias=cx.epsc[:, 0:1])
                    RECIP(kb, t1[:], t1[:], [btq[1]], [btq[1]])
                    TT(kb, "dve", t2[:], rw[:, sl], t1[:], ALU.mult, [brw, btq[1]], [btq[2]])
                    TS(kb, "dve", t2[:], t2[:], gcol, qs, ALU.mult, ALU.mult, [btq[2], bC], [btq[2]])
                    MM(kb, p2[:, :], PT[:], t2[:], True, True, [bk_, btq[2]], [bp2])
                    TT(kb, "dve", t3[:], p2[:, :], sin[:, sl], ALU.mult, [bp2, bk_], [btq[3]])
                    TT(kb, "dve", t2[:], t2[:], cos[:, sl], ALU.mult, [btq[2], bk_], [btq[2]])
                    TT(kb, "dve", qk[ci][:, sl], t2[:], t3[:], ALU.add, [btq[2], btq[3]], [bqk[ci]])
        kb.barrier()
        with ExitStack() as es3:
            vraw = es3.enter_context(nc.sbuf_tensor(U("atv"), [128, 32, 64], F32))
            va = [es3.enter_context(nc.sbuf_tensor(U("atva"), [128, 32, 128], BF16)) for _ in range(2)]
            eb = [es3.enter_context(nc.sbuf_tensor(U("ate"), [128, 512], BF16)) for _ in range(4)]
            rt = [es3.enter_context(nc.sbuf_tensor(U("atrt"), [128, 512], F32)) for _ in range(2)]
            ys = [es3.enter_context(nc.sbuf_tensor(U("aty"), [128, S], BF16)) for _ in range(2)]
            bv, bva, beb, brt, bys = kb.buf(), [kb.buf(), kb.buf()], [kb.buf() for _ in range(4)], [kb.buf(), kb.buf()], [kb.buf(), kb.buf()]
            kb.dma("sp", vraw[:], cx.tm_d["vtm"].rearrange("(n p) d -> p n d", p=128), [cx.btm["vtm"]], [bv])
            for kv in range(1):
                MEMSET(kb, "dve", va[kv][:], 1.0, [bva[kv]])
                CP(kb, "dve", va[kv][:, :, 0:64], vraw[:, :, kv * 64:(kv + 1) * 64], [bv], [bva[kv]])
            ne = 0
            pend = []

            def flush(keep):
                while len(pend) > keep:
                    (e_, be_, kv_, kc_, po_, bpo_, last, fin) = pend.pop(0)
                    MM(kb, po_[:, :], va[kv_][:, kc_, :], e_[:], kc_ == 0, last, [bva[kv_], be_], [bpo_])
                    if fin is not None:
                        fin()
            for hd in range(4):
                c, ph, kv = hd // 2, (hd % 2) * 64, 0
                q_, bq_ = qk[c], bqk[c]
                k_, bk2 = qk[2 + kv], bqk[2 + kv]
                y_, by_ = ys[c % 2], bys[c % 2]
                for qb in range(8):
                    po, bpo = cx.ps[4 + (qb % 2)], cx.pb[4 + (qb % 2)]
                    for kc in range(32):
                        p, bp = cx.ps[ne % 4], cx.pb[ne % 4]
                        e_, be_ = eb[ne % 4], beb[ne % 4]
                        ne += 1
                        MM(kb, p[:, :], k_[ph:ph + 64, kc * 128:(kc + 1) * 128], q_[ph:ph + 64, qb * 512:(qb + 1) * 512], True, True, [bk2, bq_], [bp])
                        ACT(kb, e_[:], p[:, :], AF.Exp, [bp], [be_])
                        fin = None
                        if kc == 31:
                            def fin(po=po, bpo=bpo, y_=y_, by_=by_, ph=ph, qb=qb, hd=hd, c=c):
                                r_, br_ = rt[qb % 2], brt[qb % 2]
                                RECIP(kb, r_[0:64, :], po[64:128, :], [bpo], [br_])
                                TT(kb, "dve", y_[ph:ph + 64, qb * 512:(qb + 1) * 512], po[0:64, :], r_[0:64, :], ALU.mult, [bpo, br_], [by_])
                                if hd % 2 == 1 and qb == 7:
                                    kb.dma("sp", cx.yT_d[NCK + c], y_[:], [by_], [cx.byT])
                        pend.append((e_, be_, kv, kc, po, bpo, kc == 31, fin))
                        flush(2)
            flush(0)


def phase_mlstm(cx, l):
    nc, kb = cx.nc, cx.kb
    with ExitStack() as es:
        def sb(name, shape, dt=F32):
            return es.enter_context(nc.sbuf_tensor(U(name), shape, dt))
        mlc = sb("mlc", [128, 4, 128])
        G = sb("mlG", [128, 32, 8])
        LF = sb("mlLF", [128, 32, 8])
        T1 = sb("mlT1", [128, 32, 8])
        BF_, BB_, BT_ = sb("mlBF", [128, 32, 8]), sb("mlBB", [128, 32, 8]), sb("mlBT", [128, 32, 8])
        BIAS = [sb("mlBI", [128, 32, 2]) for _ in range(2)]
        W = [sb("mlW", [128, 32, 2]) for _ in range(2)]
        EB = [sb("mlEB", [128, 32, 2]) for _ in range(2)]
        EBT = [sb("mlEBT", [128, 32, 2]) for _ in range(2)]
        gml = sb("mlg", [128, HW])
        bc, bG, bg = kb.buf(), kb.buf(), kb.buf()
        kb.dma("sp", mlc[:], cx.I["mlc"].rearrange("m p j -> p m j"), [], [bc])
        kb.dma("sp", G[:], cx.tm_d["g"].rearrange("(n p) c -> p n c", p=128), [cx.btm["g"]], [bG])
        kb.dma("sp", gml[:], cx.I["ml_g_rep"][l], [], [bg])
        STT(kb, T1[:], G[:], -1.0, G[:], ALU.mult, ALU.max, [bG], [bG])
        ACT(kb, T1[:], T1[:], AF.Exp, [bG], [bG], scale=-1.0)
        ACT(kb, T1[:], T1[:], AF.Ln, [bG], [bG], bias=1.0)
        TS(kb, "dve", LF[:], G[:], 0.0, None, ALU.min, None, [bG], [bG])
        TT(kb, "dve", LF[:], LF[:], T1[:], ALU.subtract, [bG], [bG])
        LF2 = LF[:].rearrange("p n c -> p (n c)")
        for mi, dst in ((0, BF_), (1, BB_), (None, BT_)):
            p, bp = cx.ps[0], cx.pb[0]
            lhs = mlc[:, mi, :] if mi is not None else cx.ones[:]
            MM(kb, p[:, 0:256], lhs, LF2, True, True, [bc, bG, cx.bconst], [bp])
            CP(kb, "dve", dst[:].rearrange("p n c -> p (n c)"), p[:, 0:256], [bp], [bG])
        for dr in range(2):
            Bx = BF_ if dr == 0 else BB_
            li = G[:, :, 4 * dr:4 * dr + 2]
            b4 = Bx[:, :, 4 * dr + 2:4 * dr + 4]
            bt4 = BT_[:, :, 4 * dr + 2:4 * dr + 4]
            TT(kb, "dve", BIAS[dr][:], li, b4, ALU.subtract, [bG], [bG])
            TT(kb, "dve", W[dr][:], bt4, BIAS[dr][:], ALU.add, [bG], [bG])
            ACT(kb, W[dr][:], W[dr][:], AF.Exp, [bG], [bG])
            ACT(kb, EB[dr][:], b4, AF.Exp, [bG], [bG])
            ACT(kb, EBT[dr][:], bt4, AF.Exp, [bG], [bG])
        raw = sb("mlraw", [128, S])
        qb = sb("mlq", [128, S], BF16)
        kbf = sb("mlk", [128, S], BF16)
        ktok = sb("mlkt", [128, 32, 128])
        vtok = sb("mlvt", [128, 32, 128])
        vaug = sb("mlva", [128, 32, 129], BF16)
        hF, hB = sb("mlhF", [128, 32, 128]), sb("mlhB", [128, 32, 128])
        ys = sb("mlys", [128, S], BF16)
        dg = [sb("mldg", [128, 128]) for _ in range(2)]
        DT = [sb("mlDT", [128, 128]) for _ in range(2)]
        PTt = [sb("mlPT", [128, 128], BF16) for _ in range(2)]
        ins = [sb("mlin", [128, 129]) for _ in range(2)]
        tot = [sb("mltot", [128, 129]) for _ in range(2)]
        den = [sb("mlden", [128, 2]) for _ in range(2)]
        kp = [sb("mlkp", [128, 128], BF16) for _ in range(2)]
        Cst = [sb("mlC", [128, 129]) for _ in range(2)]
        Cbf = [sb("mlCb", [128, 129], BF16) for _ in range(2)]
        rst = sb("mlrs", [128, 64])
        braw, bq, bk, bkt, bvt, bva, bys, brs = [kb.buf() for _ in range(8)]
        bh = [kb.buf(), kb.buf()]
        bdg, bDT, bPT, bin_, btot, bden, bkp = [[kb.buf(), kb.buf()] for _ in range(7)]
        bC, bCb = [kb.buf(), kb.buf()], [kb.buf(), kb.buf()]
        nps = [0]

        def nxt():
            i = nps[0] % 8
            nps[0] += 1
            return cx.ps[i], cx.pb[i]
        for hd in range(2):
            kb.dma("sp", raw[:], cx.pfm_d[13 + hd], [cx.bpfm], [braw])
            ACT(kb, qb[:], raw[:], AF.Identity, [braw], [bq], scale=128.0 ** -0.5)
            kb.dma("sp", raw[:], cx.pfm_d[15 + hd], [cx.bpfm], [braw])
            ACT(kb, kbf[:], raw[:], AF.Identity, [braw], [bk])
            kb.dma("sp", ktok[:], cx.tm_d["dk"].rearrange("(n p) d -> p n d", p=128)[:, :, hd * 128:(hd + 1) * 128], [cx.btm["dk"]], [bkt])
            kb.dma("sp", vtok[:], cx.tm_d["dv"].rearrange("(n p) d -> p n d", p=128)[:, :, hd * 128:(hd + 1) * 128], [cx.btm["dv"]], [bvt])
            MEMSET(kb, "dve", vaug[:], 1.0, [bva])
            CP(kb, "dve", vaug[:, :, 0:128], vtok[:], [bvt], [bva])
            for dr in range(2):
                MEMSET(kb, "dve", Cst[dr][:], 0.0, [bC[dr]])
                MEMSET(kb, "dve", Cbf[dr][:], 0.0, [bCb[dr]])
            units = []
            for i in range(32):
                for dr in range(2):
                    units.append((i if dr == 0 else 31 - i, dr))

            def partA(n, dr):
                u = dr
                cs = slice(n * 128, (n + 1) * 128)
                Bx = BF_ if dr == 0 else BB_
                lfc = 4 * dr + 2 + hd
                TS(kb, "dve", dg[u][:], cx.ident[:], Bx[:, n, lfc:lfc + 1], None, ALU.mult, None, [cx.bconst, bG], [bdg[u]])
                pB, bpB = nxt()
                MM(kb, pB[:, 0:128], cx.ones[:], dg[u][:], True, False, [cx.bconst, bdg[u]], [bpB])
                MM(kb, pB[:, 0:128], cx.ident[:], mlc[:, 2 + dr, :], False, True, [cx.bconst, bc], [bpB])
                ACT(kb, DT[u][:], pB[:, 0:128], AF.Exp, [bpB, bG], [bDT[u]], bias=BIAS[dr][:, n, hd:hd + 1])
                pS, bpS = nxt()
                MM(kb, pS[:, 0:128], kbf[:, cs], qb[:, cs], True, True, [bk, bq], [bpS])
                ACT(kb, kp[u][:], ktok[:, n, :], AF.Identity, [bkt, bG], [bkp[u]], scale=W[dr][:, n, hd:hd + 1])
                pC, bpC = nxt()
                MM(kb, pC[:, 0:129], kp[u][:], vaug[:, n, :], True, True, [bkp[u], bva], [bpC])
                return (pS, bpS, pC, bpC)

            def partB(n, dr, st):
                u = dr
                cs = slice(n * 128, (n + 1) * 128)
                pS, bpS, pC, bpC = st
                hacc, bhh = (hF, bh[0]) if dr == 0 else (hB, bh[1])
                TT(kb, "dve", PTt[u][:], pS[:, 0:128], DT[u][:], ALU.mult, [bpS, bDT[u]], [bPT[u]])
                pI, bpI = nxt()
                MM(kb, pI[:, 0:129], PTt[u][:], vaug[:, n, :], True, True, [bPT[u], bva], [bpI])
                pN, bpN = nxt()
                MM(kb, pN[:, 0:129], qb[:, cs], Cbf[dr][:], True, True, [bq, bCb[dr]], [bpN])
                ACT(kb, ins[u][:], pN[:, 0:129], AF.Identity, [bpN, bG], [bin_[u]], scale=EB[dr][:, n, hd:hd + 1])
                TT(kb, "dve", tot[u][:], pI[:, 0:129], ins[u][:], ALU.add, [bpI, bin_[u]], [btot[u]])
                STT(kb, den[u][:, 0:1], tot[u][:, 128:129], -1.0, tot[u][:, 128:129], ALU.mult, ALU.max, [btot[u]], [bden[u]])
                TS(kb, "dve", den[u][:, 0:1], den[u][:, 0:1], 1.0, None, ALU.max, None, [bden[u]], [bden[u]])
                RECIP(kb, den[u][:, 1:2], den[u][:, 0:1], [bden[u]], [bden[u]])
                TS(kb, "dve", hacc[:, n, :], tot[u][:, 0:128], den[u][:, 1:2], None, ALU.mult, None, [btot[u], bden[u]], [bhh])
                STT(kb, Cst[dr][:], Cst[dr][:], EBT[dr][:, n, hd:hd + 1], pC[:, 0:129], ALU.mult, ALU.add, [bC[dr], bG, bpC], [bC[dr]])
                ACT(kb, Cbf[dr][:], Cst[dr][:], AF.Identity, [bC[dr]], [bCb[dr]])

            stA = partA(*units[0])
            for k in range(1, len(units)):
                stNext = partA(*units[k])
                partB(units[k - 1][0], units[k - 1][1], stA)
                stA = stNext
            partB(units[-1][0], units[-1][1], stA)
            TT(kb, "dve", hF[:], hF[:], hB[:], ALU.add, [bh[0], bh[1]], [bh[0]])
            ACT(kb, hB[:], hF[:], AF.Square, [bh[0]], [bh[1]])
            kb.op("dve", lambda e: e.tensor_reduce(out=rst[:, 0:32], in_=hB[:], axis=AX.X, op=ALU.add), [bh[1]], [brs])
            TS(kb, "dve", rst[:, 0:32], rst[:, 0:32], 1.0 / 128.0, EPS, ALU.mult, ALU.add, [brs], [brs])
            ACT(kb, rst[:, 0:32], rst[:, 0:32], AF.Sqrt, [brs], [brs])
            RECIP(kb, rst[:, 32:64], rst[:, 0:32], [brs], [brs])
            kb.dma("sp", vtok[:], cx.tm_d["do"].rearrange("(n p) d -> p n d", p=128)[:, :, hd * 128:(hd + 1) * 128], [cx.btm["do"]], [bvt])
            ACT(kb, vtok[:], vtok[:], AF.Sigmoid, [bvt], [bvt])
            for n in range(32):
                STT(kb, hF[:, n, :], hF[:, n, :], rst[:, 32 + n:33 + n], gml[:, hd * 128:(hd + 1) * 128], ALU.mult, ALU.mult, [bh[0], brs, bg], [bh[0]])
            TT(kb, "dve", hF[:], hF[:], vtok[:], ALU.mult, [bh[0], bvt], [bh[0]])
            for n4 in range(8):
                p, bp = nxt()
                for j in range(4):
                    n = 4 * n4 + j
                    TR(kb, p[:, j * 128:(j + 1) * 128], hF[:, n, :], cx.ident[:], [bh[0], cx.bconst], [bp])
                CP(kb, "dve", ys[:, n4 * 512:(n4 + 1) * 512], p[:, :], [bp], [bys])
            kb.dma("sp", cx.yT_d[6 + hd], ys[:], [bys], [cx.byT])


def phase_hyena(cx, l):
    nc, kb = cx.nc, cx.kb
    C_, bC_ = cx.cols, cx.bcols
    zview = cx.ztok_d.rearrange("(n p) c -> p n c", p=128)
    nps = [0]

    def nxt():
        i = nps[0] % 6
        nps[0] += 1
        return cx.ps[i], cx.pb[i]
    with ExitStack() as es0:
        utok = es0.enter_context(nc.sbuf_tensor(U("hyu"), [128, 32, HW], BF16))
        RS = es0.enter_context(nc.sbuf_tensor(U("hyRS"), [128, 2 * HW], F32))
        wfc = es0.enter_context(nc.sbuf_tensor(U("hywf"), [128, NF], F32))
        skr = es0.enter_context(nc.sbuf_tensor(U("hysk"), [128, 2 * HW], F32))
        bu, bRS, bwf = kb.buf(), kb.buf(), kb.buf()
        kb.dma("sp", wfc[:], cx.I["dft_wf"], [], [bwf])
        kb.dma("sp", skr[:], cx.I["hy_skip_rep"][l], [], [bwf])
        with ExitStack() as es:
            raw = [es.enter_context(nc.sbuf_tensor(U("hyraw"), [128, S], F32)) for _ in range(2)]
            zc = [es.enter_context(nc.sbuf_tensor(U("hyz"), [128, S], F32)) for _ in range(2)]
            st = [es.enter_context(nc.sbuf_tensor(U("hyst"), [128, 32, 128], F32)) for _ in range(2)]
            braw, bz, bst = [kb.buf(), kb.buf()], [kb.buf(), kb.buf()], [kb.buf(), kb.buf()]
            for ch in range(6):
                u = ch % 2
                kb.dma("sp", raw[u][:], cx.pfm_d[7 + ch], [cx.bpfm], [braw[u]])
                cw = lambda j: C_[:, CO["hy_cw"] + 3 * ch + j:CO["hy_cw"] + 3 * ch + j + 1]
                ACT(kb, zc[u][:], raw[u][:], AF.Identity, [braw[u], bC_], [bz[u]], scale=cw(1), bias=C_[:, CO["hy_cb"] + ch:CO["hy_cb"] + ch + 1])
                STT(kb, zc[u][:, 1:S], raw[u][:, 0:S - 1], cw(0), zc[u][:, 1:S], ALU.mult, ALU.add, [braw[u], bz[u], bC_], [bz[u]])
                STT(kb, zc[u][:, 0:S - 1], raw[u][:, 1:S], cw(2), zc[u][:, 0:S - 1], ALU.mult, ALU.add, [braw[u], bz[u], bC_], [bz[u]])
                for n4 in range(8):
                    p, bp = nxt()
                    for j in range(4):
                        n = 4 * n4 + j
                        TR(kb, p[:, j * 128:(j + 1) * 128], zc[u][:, n * 128:(n + 1) * 128], cx.ident[:], [bz[u], cx.bconst], [bp])
                    CP(kb, "dve", st[u][:, 4 * n4:4 * n4 + 4, :], p[:, :].rearrange("p (j c) -> p j c", j=4), [bp], [bst[u]])
                    if ch < 2:
                        CP(kb, "dve", utok[:, 4 * n4:4 * n4 + 4, ch * 128:(ch + 1) * 128], st[u][:, 4 * n4:4 * n4 + 4, :], [bst[u]], [bu])
                kb.dma("act", zview[:, :, ch * 128:(ch + 1) * 128], st[u][:], [bst[u]], [cx.bztok])
        kb.barrier()
        with ExitStack() as es:
            def sb(name, shape, dt=F32):
                return es.enter_context(nc.sbuf_tensor(U(name), shape, dt))
            feat = sb("hyfe", [128, S])
            h1 = sb("hyh1", [128, S])
            h2 = sb("hyh2", [128, S])
            w1s, w2s, w3s = sb("hyw1", [128, 64]), sb("hyw2", [128, 64]), sb("hyw3", [128, 4 * HW])
            adec = sb("hyad", [128, 4 * HW])
            tcol = sb("hytc", [128, 32])
            sfb = sb("hysfb", [128, 2])
            arg = [sb("hyarg", [128, 512]) for _ in range(2)]
            E = [sb("hyE", [128, 512]) for _ in range(2)]
            hq = [sb("hyhq", [128, 512]) for _ in range(4)]
            sq = [sb("hysq", [128, 512]) for _ in range(2)]
            gst = [sb("hygs", [128, 2, 2 * HW], BF16) for _ in range(2)]
            bk, bh1, bh2, bsfb = kb.buf(), kb.buf(), kb.buf(), kb.buf()
            barg, bE, bsq, bgst = [kb.buf(), kb.buf()], [kb.buf(), kb.buf()], [kb.buf(), kb.buf()], [kb.buf(), kb.buf()]
            bhq = [kb.buf() for _ in range(4)]
            kb.dma("sp", feat[0:17, :], cx.I["hy_feat"], [], [bk])
            kb.dma("sp", w1s[0:17, :], cx.I["hy_w1"][l], [], [bk])
            kb.dma("sp", w2s[0:64, :], cx.I["hy_w2"][l], [], [bk])
            kb.dma("sp", w3s[0:64, :], cx.I["hy_w3"][l], [], [bk])
            kb.dma("sp", adec[:], cx.I["hy_decay_rep"][l], [], [bk])
            kb.dma("sp", tcol[:], cx.I["hy_negt"], [], [bk])
            ACT(kb, adec[:], adec[:], AF.Abs, [bk], [bk])
            sf = C_[0:64, CO["hy_sf"]:CO["hy_sf"] + 1]
            TT(kb, "dve", sfb[0:64, 0:1], C_[0:64, CO["hy_b1"]:CO["hy_b1"] + 1], sf, ALU.mult, [bC_], [bsfb])
            TT(kb, "dve", sfb[0:64, 1:2], C_[0:64, CO["hy_b2"]:CO["hy_b2"] + 1], sf, ALU.mult, [bC_], [bsfb])
            for li, (wsrc, K, src, bsrc, dst, bdst) in enumerate(((w1s, 17, feat, bk, h1, bh1), (w2s, 64, h1, bh1, h2, bh2))):
                for tb in range(8):
                    sl = slice(tb * 512, (tb + 1) * 512)
                    p, bp = nxt()
                    a, ba = arg[tb % 2], barg[tb % 2]
                    MM(kb, p[0:64, :], wsrc[0:K, 0:64], src[0:K, sl], True, True, [bk, bsrc], [bp])
                    ACT(kb, a[0:64, :], p[0:64, :], AF.Identity, [bp, bC_, bsfb], [ba], scale=sf, bias=sfb[0:64, li:li + 1])
                    m_ = E[tb % 2]
                    bm_ = bE[tb % 2]
                    for _rep in range(2):
                        TS(kb, "dve", m_[0:64, :], a[0:64, :], PI, 2 * PI, ALU.is_gt, ALU.mult, [ba], [bm_])
                        TT(kb, "dve", a[0:64, :], a[0:64, :], m_[0:64, :], ALU.subtract, [ba, bm_], [ba])
                        TS(kb, "dve", m_[0:64, :], a[0:64, :], -PI, 2 * PI, ALU.is_lt, ALU.mult, [ba], [bm_])
                        TT(kb, "dve", a[0:64, :], a[0:64, :], m_[0:64, :], ALU.add, [ba, bm_], [ba])
                    ACT(kb, dst[0:64, sl], a[0:64, :], AF.Sin, [ba], [bdst])
            pss = [(cx.ps[6], cx.pb[6]), (cx.ps[7], cx.pb[7])]
            for n in range(32):
                g, bg = gst[n % 2], bgst[n % 2]
                for q in range(4):
                    o, dr = q // 2, q % 2
                    p, bp = nxt()
                    e_, be_ = E[q % 2], bE[q % 2]
                    s_, bs_ = sq[q % 2], bsq[q % 2]
                    MM(kb, p[:, 0:HW], h2[0:64, n * 128:(n + 1) * 128], w3s[0:64, q * HW:(q + 1) * HW], True, True, [bh2, bk], [bp])
                    ACT(kb, e_[:, 0:HW], adec[:, q * HW:(q + 1) * HW], AF.Exp, [bk], [be_], scale=tcol[:, n:n + 1])
                    TT(kb, "dve", hq[q][:, 0:HW], p[:, 0:HW], e_[:, 0:HW], ALU.mult, [bp, be_], [bhq[q]])
                    ACT(kb, s_[:, 0:HW], hq[q][:, 0:HW], AF.Square, [bhq[q]], [bs_])
                    MM(kb, pss[o][0][:, 0:HW], cx.ones[:], s_[:, 0:HW], (n == 0 and dr == 0), (n == 31 and dr == 1), [cx.bconst, bs_], [pss[o][1]])
                    if n == 0 and dr == 1:
                        MEMSET(kb, "dve", hq[q][0:1, 0:HW], 0.0, [bhq[q]])
                for o in range(2):
                    TT(kb, "dve", g[:, 0, o * HW:(o + 1) * HW], hq[2 * o][:, 0:HW], hq[2 * o + 1][:, 0:HW], ALU.add, [bhq[2 * o], bhq[2 * o + 1]], [bg])
                    TT(kb, "dve", g[:, 1, o * HW:(o + 1) * HW], hq[2 * o + 1][:, 0:HW], hq[2 * o][:, 0:HW], ALU.subtract, [bhq[2 * o], bhq[2 * o + 1]], [bg])
                kb.dma("sp", cx.G_d.rearrange("r t c -> t r c")[n * 128:(n + 1) * 128], g[:], [bg], [cx.bG_d])
            for o in range(2):
                ACT(kb, RS[:, o * HW:(o + 1) * HW], pss[o][0][:, 0:HW], AF.Sqrt, [pss[o][1]], [bRS], bias=cx.epsc[:, 0:1])
            RECIP(kb, RS[:], RS[:], [bRS], [bRS])
        kb.barrier()
        with ExitStack() as es:
            Gs = es.enter_context(nc.sbuf_tensor(U("hyG"), [128, 32, 2 * HW], BF16))
            ct = [es.enter_context(nc.sbuf_tensor(U("hyct"), [128, NF, 128], BF16)) for _ in range(2)]
            ho = [es.enter_context(nc.sbuf_tensor(U("hyho"), [128, 2 * HW], F32)) for _ in range(2)]
            bGs, bct, bho = kb.buf(), [kb.buf(), kb.buf()], [kb.buf(), kb.buf()]
            k = 0
            for ri, blk in enumerate(("dft_c", "dft_s")):
                kb.dma("sp", Gs[:], cx.G_d[ri].rearrange("(n p) c -> p n c", p=128), [cx.bG_d], [bGs])
                for fc in range(NF):
                    c_, bc_ = ct[k % 2], bct[k % 2]
                    h_, bh_ = ho[k % 2], bho[k % 2]
                    k += 1
                    kb.dma("sp", c_[:], cx.I[blk][fc], [], [bc_])
                    for o in range(2):
                        p, bp = nxt()
                        for dc in range(32):
                            MM(kb, p[:, 0:HW], c_[:, dc, :], Gs[:, dc, o * HW:(o + 1) * HW], dc == 0, dc == 31, [bc_, bGs], [bp])
                        STT(kb, h_[:, o * HW:(o + 1) * HW], p[:, 0:HW], wfc[:, fc:fc + 1], RS[:, o * HW:(o + 1) * HW], ALU.mult, ALU.mult, [bp, bwf, bRS], [bh_])
                    kb.dma("act", cx.H_d[ri, fc], h_[:], [bh_], [cx.bH_d])
        kb.barrier()
        with ExitStack() as es:
            def sb(name, shape, dt=F32):
                return es.enter_context(nc.sbuf_tensor(U(name), shape, dt))
            YA = sb("hyYA", [128, NF, HW], BF16)
            YB = sb("hyYB", [128, NF, HW], BF16)
            ct = [sb("hyc2", [128, NF, 128], BF16) for _ in range(2)]
            stt = [sb("hys2", [128, NF, 128], BF16) for _ in range(2)]
            Hr = [sb("hyHr", [128, HW]) for _ in range(2)]
            Hi = [sb("hyHi", [128, HW]) for _ in range(2)]
            tt = [sb("hyt", [128, HW]) for _ in range(4)]
            uu = [sb("hyuu", [128, HW]) for _ in range(2)]
            xx = [sb("hyxx", [128, HW]) for _ in range(2)]
            zz = [sb("hyzz", [128, HW]) for _ in range(2)]
            yst = [sb("hyys", [128, 2, 128], BF16) for _ in range(2)]
            bYA, bYB = kb.buf(), kb.buf()
            bct, bstt, bHr, bHi, buu, bxx, bzz, byst = [[kb.buf(), kb.buf()] for _ in range(8)]
            btt = [kb.buf() for _ in range(4)]
            k = 0
            for o in range(2):
                for fc in range(NF):
                    c_, bc_ = ct[k % 2], bct[k % 2]
                    s_, bs_ = stt[k % 2], bstt[k % 2]
                    hr, bhr = Hr[k % 2], bHr[k % 2]
                    hi, bhi = Hi[k % 2], bHi[k % 2]
                    k += 1
                    kb.dma("sp", c_[:], cx.I["dft_c"][fc], [], [bc_])
                    kb.dma("sp", s_[:], cx.I["dft_s"][fc], [], [bs_])
                    kb.dma("sp", hr[:], cx.H_d[0, fc, :, o * HW:(o + 1) * HW], [cx.bH_d], [bhr])
                    kb.dma("sp", hi[:], cx.H_d[1, fc, :, o * HW:(o + 1) * HW], [cx.bH_d], [bhi])
                    pA, bpA = nxt()
                    pB, bpB = nxt()
                    for dc in range(32):
                        MM(kb, pA[:, 0:HW], c_[:, dc, :], utok[:, dc, :], dc == 0, dc == 31, [bc_, bu], [bpA])
                    for dc in range(32):
                        MM(kb, pB[:, 0:HW], s_[:, dc, :], utok[:, dc, :], dc == 0, dc == 31, [bs_, bu], [bpB])
                    TT(kb, "dve", tt[0][:], pA[:, 0:HW], hr[:], ALU.mult, [bpA, bhr], [btt[0]])
                    TT(kb, "dve", tt[1][:], pB[:, 0:HW], hi[:], ALU.mult, [bpB, bhi], [btt[1]])
                    TT(kb, "dve", tt[2][:], pB[:, 0:HW], hr[:], ALU.mult, [bpB, bhr], [btt[2]])
                    TT(kb, "dve", tt[3][:], pA[:, 0:HW], hi[:], ALU.mult, [bpA, bhi], [btt[3]])
                    TT(kb, "dve", YA[:, fc, :], tt[0][:], tt[1][:], ALU.add, [btt[0], btt[1]], [bYA])
                    TT(kb, "dve", YB[:, fc, :], tt[2][:], tt[3][:], ALU.subtract, [btt[2], btt[3]], [bYB])
                for n in range(32):
                    c_, bc_ = ct[k % 2], bct[k % 2]
                    s_, bs_ = stt[k % 2], bstt[k % 2]
                    u_, bu_ = uu[k % 2], buu[k % 2]
                    x_, bx_ = xx[k % 2], bxx[k % 2]
                    z_, bz_ = zz[k % 2], bzz[k % 2]
                    k += 1
                    kb.dma("sp", c_[:], cx.I["dft_c"][n], [], [bc_])
                    kb.dma("sp", s_[:], cx.I["dft_s"][n], [], [bs_])
                    if o == 0:
                        kb.dma("sp", u_[:], cx.ztok_d[n * 128:(n + 1) * 128, 0:HW], [cx.bztok], [bu_])
                    else:
                        kb.dma("sp", u_[:], cx.z1_d[n * 128:(n + 1) * 128, :], [cx.bz1], [bu_])
                    kb.dma("sp", x_[:], cx.ztok_d[n * 128:(n + 1) * 128, HW * (o + 1):HW * (o + 2)], [cx.bztok], [bx_])
                    p, bp = nxt()
                    for fc in range(NF):
                        MM(kb, p[:, 0:HW], c_[:, fc, :], YA[:, fc, :], fc == 0, False, [bc_, bYA], [bp])
                    for fc in range(NF):
                        MM(kb, p[:, 0:HW], s_[:, fc, :], YB[:, fc, :], False, fc == NF - 1, [bs_, bYB], [bp])
                    TT(kb, "dve", u_[:], u_[:], skr[:, o * HW:(o + 1) * HW], ALU.mult, [bu_, bwf], [bu_])
                    TT(kb, "dve", z_[:], p[:, 0:HW], u_[:], ALU.add, [bp, bu_], [bz_])
                    TT(kb, "dve", z_[:], z_[:], x_[:], ALU.mult, [bz_, bx_], [bz_])
                    if o == 0:
                        kb.dma("act", cx.z1_d[n * 128:(n + 1) * 128, :], z_[:], [bz_], [cx.bz1])
                        CP(kb, "dve", utok[:, n, :], z_[:], [bz_], [bu])
                    else:
                        ys_, bys_ = yst[n % 2], byst[n % 2]
                        pt, bpt = nxt()
                        for j in range(2):
                            TR(kb, pt[:, j * 128:(j + 1) * 128], z_[:, j * 128:(j + 1) * 128], cx.ident[:], [bz_, cx.bconst], [bpt])
                        CP(kb, "dve", ys_[:], pt[:, 0:256].rearrange("p (j c) -> p j c", j=2), [bpt], [bys_])
                        kb.dma("act", cx.yT_d[4:6].rearrange("c p t -> p c t")[:, :, n * 128:(n + 1) * 128], ys_[:], [bys_], [cx.byT])
                if o == 0:
                    kb.barrier()


def kernel(**inputs):
    inp = {k: np.asarray(v) for k, v in inputs.items()}
    nc = build({"full"})
    shared = host_shared(inp)
    halves = [host_half(inp, 0), host_half(inp, 1)]
    in_maps = [host_inputs(inp, c, shared, halves) for c in range(8)]
    res = run_bass_kernel_spmd(nc, in_maps, core_ids=list(range(8)))
    out = np.zeros((4, S, D), np.float32)
    for c in range(8):
        out[c // 2, (c % 2) * SH:(c % 2 + 1) * SH] = np.asarray(res.results[c]["out"])
    return out
```
